# Optimizing a Trainium2 kernel written in Bass

```python
import math
import jax
import jax.numpy as jnp
from jax import lax
import numpy as np

D_MODEL = 1024
BATCH = 4
SEQ = 4096
DEPTH = 2
DEC_BATCH = 32
DEC_SEQ = 8
PAST_LEN = 16384
PAGE_SIZE = 128

HEAD_DIM = 64
NSA_HEADS = 4
NSA_KV_HEADS = 2
NSA_GROUP = NSA_HEADS // NSA_KV_HEADS
NSA_BLOCK = 64
NSA_TOPK = 16
NSA_WINDOW = 512
Q_BLOCK = 128
NUM_BUCKETS = 32
MAX_DISTANCE = 128
CONV_CH = D_MODEL // 4
CONV_WIDTH = 31
RET_HEADS = 4
RET_W = RET_HEADS * HEAD_DIM
RET_CHUNK = 64
ROPE_BASE = 10000.0
GDN_HEADS = 4
GDN_W = GDN_HEADS * HEAD_DIM
GDN_CONV = 4
GDN_CHUNK = 64
D_FF = 2816
FFN_CONV = 3
N_BRANCH = 4
BRANCH_W = NSA_HEADS * HEAD_DIM
EPS = 1e-6
NEG_INF = -1e30
IN_SPLITS = (NSA_HEADS * HEAD_DIM, 6 * NSA_KV_HEADS * HEAD_DIM, 3 * NSA_HEADS, 2 * CONV_CH, 4 * RET_W, 4 * GDN_W + 2 * GDN_HEADS, N_BRANCH * D_MODEL)
IN_COLS = sum(IN_SPLITS)

kernel_name = "hybrid_nsa_conv_retention_gdn_step"


def split_cols(a, sizes):
    return jnp.split(a, [int(i) for i in np.cumsum(sizes)[:-1]], axis=-1)


def rms_norm(x, g):
    x32 = x.astype(jnp.float32)
    y = x32 * lax.rsqrt(jnp.mean(x32 * x32, axis=-1, keepdims=True) + EPS)
    return (y * g.astype(jnp.float32)).astype(x.dtype)


def layer_norm(x, g, b):
    x32 = x.astype(jnp.float32)
    mu = jnp.mean(x32, axis=-1, keepdims=True)
    var = jnp.mean(jnp.square(x32 - mu), axis=-1, keepdims=True)
    return ((x32 - mu) * lax.rsqrt(var + EPS) * g.astype(jnp.float32) + b.astype(jnp.float32)).astype(x.dtype)


def causal_dwconv(x, buf, w):
    xp = jnp.concatenate([buf.astype(x.dtype), x], axis=1)
    y = lax.conv_general_dilated(xp, w[:, None, :].astype(x.dtype), window_strides=(1,), padding='VALID',
                                 dimension_numbers=('NWC', 'WIO', 'NWC'), feature_group_count=x.shape[-1])
    return y, xp[:, xp.shape[1] - (w.shape[0] - 1):]


def t5_bucket(dist):
    n = jnp.maximum(dist, 0)
    exact = NUM_BUCKETS // 2
    large = exact + (jnp.log(jnp.maximum(n, 1).astype(jnp.float32) / exact) / math.log(MAX_DISTANCE / exact)
                     * (NUM_BUCKETS - exact)).astype(jnp.int32)
    return jnp.where(n < exact, n, jnp.minimum(large, NUM_BUCKETS - 1))


def masked_softmax(s, mask):
    return jax.nn.softmax(jnp.where(mask, s, NEG_INF), axis=-1) * mask


def over_query_blocks(fn, args, q_axes, T):
    if T <= Q_BLOCK or T % Q_BLOCK:
        return fn(args)
    nq = T // Q_BLOCK

    def split(a, ax):
        return jnp.moveaxis(a.reshape(a.shape[:ax] + (nq, Q_BLOCK) + a.shape[ax + 1:]), ax, 0)

    out = lax.map(fn, tuple(split(a, ax) for a, ax in zip(args, q_axes)))
    out = jnp.moveaxis(out, 0, 1)
    return out.reshape((out.shape[0], T) + out.shape[3:])


def rotary(x, pos):
    half = x.shape[-1] // 2
    inv = ROPE_BASE ** (-jnp.arange(half, dtype=jnp.float32) / half)
    ang = pos.astype(jnp.float32)[:, None] * inv[None, :]
    cos, sin = jnp.cos(ang)[None, :, None, :], jnp.sin(ang)[None, :, None, :]
    x1, x2 = x[..., :half], x[..., half:]
    return jnp.concatenate([x1 * cos - x2 * sin, x1 * sin + x2 * cos], axis=-1)


def nsa_attention(q, kv_new, gate_logits, past_rows, win_buf, pos0, cmp_pool, cmp_pe, rel_bias):
    B, T = q.shape[:2]
    dt = q.dtype
    qpos = pos0 + jnp.arange(T, dtype=jnp.int32)
    qg = (q * HEAD_DIM ** -0.5).reshape(B, T, NSA_KV_HEADS, NSA_GROUP, HEAD_DIM)
    rb = rel_bias.astype(jnp.float32).reshape(NUM_BUCKETS, NSA_KV_HEADS, NSA_GROUP)
    L = T if past_rows is None else past_rows.shape[1] + T
    nb = -(-L // NSA_BLOCK)
    new_rows = jnp.pad(kv_new[:, :, :4], ((0, 0), (0, nb * NSA_BLOCK - L), (0, 0), (0, 0), (0, 0)))
    full = new_rows if past_rows is None else jnp.concatenate([past_rows.astype(dt), new_rows], axis=1)
    blocks = full.reshape(B, nb, NSA_BLOCK, 4, NSA_KV_HEADS, HEAD_DIM)
    pe = jnp.transpose(cmp_pe, (1, 0, 2))[:, :, None, :].astype(dt)
    cmp = jnp.einsum('bnjckd,cj->bnckd', blocks[:, :, :, :2] + pe, cmp_pool.astype(dt))
    k_c, v_c = cmp[:, :, 0], cmp[:, :, 1]
    blk = jnp.arange(nb, dtype=jnp.int32)
    d_c = qpos[:, None] - (blk * NSA_BLOCK + NSA_BLOCK - 1)[None, :]
    s_c = jnp.einsum('btkgd,bnkd->bkgtn', qg, k_c).astype(jnp.float32) + jnp.transpose(rb[t5_bucket(d_c)], (2, 3, 0, 1))
    p_c = masked_softmax(s_c, d_c >= 0)
    o_c = jnp.einsum('bkgtn,bnkd->btkgd', p_c.astype(dt), v_c)
    cur = (qpos // NSA_BLOCK)[:, None]
    forced = (blk[None] == 0) | (blk[None] == cur) | (blk[None] == cur - 1)
    score = jnp.where(blk[None] <= cur, jnp.where(forced, 2.0, p_c.sum(axis=2)), -1.0)
    top_s, idx = lax.top_k(score, min(NSA_TOPK, nb))
    ok = top_s > -0.5
    ks = jnp.transpose(blocks[:, :, :, 2], (0, 3, 1, 2, 4))
    vs = jnp.transpose(blocks[:, :, :, 3], (0, 3, 1, 2, 4))
    take = jax.vmap(jax.vmap(lambda a, i: a[i]))
    kv_ix = jnp.arange(NSA_KV_HEADS)[None, :, None, None, None]
    rbk = jnp.transpose(rb, (1, 0, 2))

    def sel_block(args):
        qb, ib, okb, pb = args
        Bq, Q = qb.shape[:2]
        kg, vg = take(ks, ib), take(vs, ib)
        kpos = ib[..., None] * NSA_BLOCK + jnp.arange(NSA_BLOCK, dtype=jnp.int32)
        dist = pb[None, None, :, None, None] - kpos
        mask = (okb[..., None] & (dist >= 0)).reshape(Bq, NSA_KV_HEADS, 1, Q, -1)
        bias = jnp.moveaxis(rbk[kv_ix, t5_bucket(dist)], -1, 2)
        s = jnp.einsum('bqkgd,bkqsjd->bkgqsj', qb, kg).astype(jnp.float32) + bias
        p = masked_softmax(s.reshape(Bq, NSA_KV_HEADS, NSA_GROUP, Q, -1), mask)
        return jnp.einsum('bkgqn,bkqnd->bqkgd', p.astype(dt), vg.reshape(Bq, NSA_KV_HEADS, Q, -1, HEAD_DIM))

    o_s = over_query_blocks(sel_block, (qg, idx, ok, qpos), (1, 2, 2, 0), T)
    kw = kv_new[:, :, 4:]
    if win_buf is None:
        real = kw
        k_all = jnp.pad(kw, ((0, 0), (NSA_WINDOW, 0), (0, 0), (0, 0), (0, 0)))
        span = NSA_WINDOW
    else:
        real = jnp.concatenate([win_buf.astype(dt), kw], axis=1)
        k_all = real
        span = win_buf.shape[1]
    k0 = pos0 - span

    def win_block(args):
        qb, pb = args
        Q = qb.shape[1]
        start = pb[0] - pos0
        kb = lax.dynamic_slice_in_dim(k_all, start, span + Q, axis=1)
        kpos = k0 + start + jnp.arange(span + Q, dtype=jnp.int32)
        dist = pb[:, None] - kpos[None, :]
        mask = (dist >= 0) & (dist < NSA_WINDOW) & (kpos >= 0)[None, :]
        s = jnp.einsum('bqkgd,bnkd->bkgqn', qb, kb[:, :, 0]).astype(jnp.float32) + jnp.transpose(rb[t5_bucket(dist)], (2, 3, 0, 1))
        p = masked_softmax(s, mask)
        return jnp.einsum('bkgqn,bnkd->bqkgd', p.astype(dt), kb[:, :, 1])

    o_w = over_query_blocks(win_block, (qg, qpos), (1, 0), T)
    new_win = real[:, real.shape[1] - min(NSA_WINDOW, real.shape[1]):]
    g = jax.nn.sigmoid(gate_logits.astype(jnp.float32)).astype(dt).reshape(B, T, 3, NSA_KV_HEADS, NSA_GROUP, 1)
    o = g[:, :, 0] * o_c + g[:, :, 1] * o_s + g[:, :, 2] * o_w
    return o.reshape(B, T, NSA_HEADS * HEAD_DIM), new_win


def conformer_conv(u, buf, dw, dw_b, ln_g, ln_b):
    a, gte = jnp.split(u, 2, axis=-1)
    y, new_buf = causal_dwconv(a * jax.nn.sigmoid(gte), buf, dw)
    y = layer_norm(y + dw_b.astype(y.dtype), ln_g, ln_b)
    return jax.nn.silu(y), new_buf


def retention(q, k, v, gate, S0, pos0, gn):
    B, T, _ = q.shape
    dt = q.dtype
    H, d = RET_HEADS, HEAD_DIM
    pos = pos0 + jnp.arange(T, dtype=jnp.int32)
    f = lambda a: a.astype(jnp.float32).reshape(B, T, H, d)
    q = rotary(f(q), pos)
    k = rotary(f(k), pos) * d ** -0.5
    v = f(v)
    C = math.gcd(T, RET_CHUNK)
    n = T // C
    lg = jnp.log1p(-jnp.exp2(-5.0 - jnp.arange(H, dtype=jnp.float32)))
    ch = lambda a: a.reshape(B, n, C, H, d).transpose(0, 3, 1, 2, 4)
    qc, kc, vc = ch(q), ch(k), ch(v)
    i = jnp.arange(C, dtype=jnp.float32)
    diff = i[:, None] - i[None, :]
    Dm = jnp.where(diff >= 0, jnp.exp(jnp.maximum(diff, 0.0)[None] * lg[:, None, None]), 0.0)
    att = jnp.einsum('bhncd,bhnsd->bhncs', qc, kc) * Dm[None, :, None]
    o = jnp.einsum('bhncs,bhnse->bhnce', att, vc)
    xi = jnp.exp((i[None, :] + 1.0) * lg[:, None])
    zeta = jnp.exp((C - 1.0 - i[None, :]) * lg[:, None])
    kv = jnp.einsum('bhncd,bhnce->nbhde', kc * zeta[None, :, None, :, None], vc)
    decay = jnp.exp(C * lg)[None, :, None, None]

    def step(S, kvn):
        return S * decay + kvn, S

    S_fin, S_prev = lax.scan(step, S0.astype(jnp.float32), kv)
    o = o + jnp.einsum('bhncd,nbhde->bhnce', qc * xi[None, :, None, :, None], S_prev)
    o = o.transpose(0, 2, 3, 1, 4).reshape(B, T, H, d)
    mu = jnp.mean(o, axis=-1, keepdims=True)
    var = jnp.mean(jnp.square(o - mu), axis=-1, keepdims=True)
    o = ((o - mu) * lax.rsqrt(var + EPS)).reshape(B, T, H * d) * gn.astype(jnp.float32)
    return (o * jax.nn.silu(gate.astype(jnp.float32))).astype(dt), S_fin


def gated_deltanet(qkv, z, a, b, conv_buf, S0, conv_w, A_log, dt_bias, norm_g):
    B, T, _ = qkv.shape
    dt = qkv.dtype
    H, d = GDN_HEADS, HEAD_DIM
    y, new_buf = causal_dwconv(qkv, conv_buf, conv_w)
    y = jax.nn.silu(y.astype(jnp.float32))
    q, k, v = [t.reshape(B, T, H, d) for t in jnp.split(y, 3, axis=-1)]
    l2 = lambda t: t * lax.rsqrt(jnp.sum(t * t, axis=-1, keepdims=True) + EPS)
    q = l2(q) * d ** -0.5
    k = l2(k)
    g = -jnp.exp(A_log.astype(jnp.float32)) * jax.nn.softplus(a.astype(jnp.float32) + dt_bias.astype(jnp.float32))
    beta = jax.nn.sigmoid(b.astype(jnp.float32))
    C = math.gcd(T, GDN_CHUNK)
    n = T // C
    ch = lambda t: t.reshape(B, n, C, H, d).transpose(0, 3, 1, 2, 4)
    chs = lambda t: t.reshape(B, n, C, H).transpose(0, 3, 1, 2)
    qc, kc, vc = ch(q), ch(k), ch(v)
    gc = jnp.cumsum(chs(g), axis=-1)
    bc = chs(beta)
    ii = jnp.arange(C)
    tri = ii[:, None] >= ii[None, :]
    strict = ii[:, None] > ii[None, :]
    Lm = jnp.exp(jnp.where(tri, gc[..., :, None] - gc[..., None, :], -jnp.inf))
    kb = kc * bc[..., None]
    M = jnp.where(strict, jnp.einsum('bhncd,bhnsd->bhncs', kb, kc) * Lm, 0.0)
    A = M + jnp.eye(C, dtype=jnp.float32)
    rhs = jnp.concatenate([vc * bc[..., None], kb * jnp.exp(gc)[..., None]], axis=-1)
    sol = lax.linalg.triangular_solve(A, rhs, left_side=True, lower=True, unit_diagonal=True)
    u, w = sol[..., :d], sol[..., d:]
    att = jnp.einsum('bhncd,bhnsd->bhncs', qc, kc) * Lm
    qd = qc * jnp.exp(gc)[..., None]
    kd = kc * jnp.exp(gc[..., -1:] - gc)[..., None]
    glast = jnp.exp(gc[..., -1])
    xs = tuple(jnp.moveaxis(t, 2, 0) for t in (u, w, att, qd, kd, glast))

    def step(S, xs_n):
        u_, w_, at_, qd_, kd_, gl_ = xs_n
        vn = u_ - jnp.einsum('bhcd,bhde->bhce', w_, S)
        o_ = jnp.einsum('bhcd,bhde->bhce', qd_, S) + jnp.einsum('bhcs,bhse->bhce', at_, vn)
        S = S * gl_[..., None, None] + jnp.einsum('bhcd,bhce->bhde', kd_, vn)
        return S, o_

    S_fin, o = lax.scan(step, S0.astype(jnp.float32), xs)
    o = o.transpose(1, 0, 3, 2, 4).reshape(B, T, H, d)
    o = o * lax.rsqrt(jnp.mean(o * o, axis=-1, keepdims=True) + EPS) * norm_g.astype(jnp.float32)
    o = o * jax.nn.silu(z.astype(jnp.float32).reshape(B, T, H, d))
    return o.reshape(B, T, H * d).astype(dt), new_buf, S_fin


def conv_ffn(h, buf, w_up, dw, w_down):
    gpre, val = jnp.split(h @ w_up, 2, axis=-1)
    gconv, new_buf = causal_dwconv(gpre, buf, dw)
    return (jax.nn.gelu(gconv) * val) @ w_down, new_buf


def trunk_layer(x, c, pos0, nsa_past, win_buf, conv_buf, ret_s, gdn_buf, gdn_s, ffn_buf, p, rel_bias):
    B, T, _ = x.shape
    mod = jax.nn.silu(c) @ p["w_ada"] + p["b_ada"]
    sh1, sc1, gt1, sh2, sc2, gt2 = [m[:, None, :] for m in jnp.split(mod, 6, axis=-1)]
    h = rms_norm(x, p["norms"][0]) * (1.0 + sc1) + sh1
    nq, nkv, ngt, ucv, ret, gdn, mg = split_cols(h @ p["w_in"], IN_SPLITS)
    nkv = nkv.reshape(B, T, 6, NSA_KV_HEADS, HEAD_DIM)
    o_nsa, new_win = nsa_attention(nq.reshape(B, T, NSA_HEADS, HEAD_DIM), nkv, ngt.reshape(B, T, 3, NSA_HEADS),
                                   nsa_past, win_buf, pos0, p["cmp_pool"], p["cmp_pe"], rel_bias)
    o_conv, new_conv = conformer_conv(ucv, conv_buf, p["conv_dw"], p["conv_dw_b"], p["conv_ln_g"], p["conv_ln_b"])
    rq, rk, rv, rg = split_cols(ret, (RET_W, RET_W, RET_W, RET_W))
    o_ret, new_ret = retention(rq, rk, rv, rg, ret_s, pos0, p["ret_gn"])
    gqkv, gz, ga, gb = split_cols(gdn, (3 * GDN_W, GDN_W, GDN_HEADS, GDN_HEADS))
    o_gdn, new_gdn_buf, new_gdn = gated_deltanet(gqkv, gz, ga, gb, gdn_buf, gdn_s, p["gdn_conv_w"],
                                                 p["gdn_A_log"], p["gdn_dt_bias"], p["gdn_norm"])
    branches = jnp.stack([o_nsa, o_conv, o_ret, o_gdn], axis=2)
    merged = jnp.einsum('btnw,nwd->btnd', branches, p["w_branch"])
    gates = jax.nn.sigmoid(mg.reshape(B, T, N_BRANCH, D_MODEL))
    mixed = jnp.einsum('btnd,btnd->btd', gates, merged) @ p["w_out"]
    x = x + gt1 * rms_norm(mixed, p["norms"][1])
    h2 = rms_norm(x, p["norms"][2]) * (1.0 + sc2) + sh2
    f, new_ffn = conv_ffn(h2, ffn_buf, p["ffn_up"], p["ffn_dw"], p["ffn_down"])
    x = x + gt2 * rms_norm(f, p["norms"][3])
    return x, (nkv[:, :, :4], new_win, new_conv, new_ret, new_gdn_buf, new_gdn, new_ffn)


def setup_inputs(seed: int = 0) -> dict:
    key = jax.random.key(seed)
    keys = iter(jax.random.split(key, 48))

    def nrm(shape, scale):
        return scale * jax.random.normal(next(keys), shape, jnp.float32)

    n_pages = PAST_LEN // PAGE_SIZE
    n_used = DEC_BATCH * n_pages
    n_pool = n_used + max(1, n_used // 4)
    page_table = jax.random.permutation(next(keys), n_pool)[:n_used].reshape(DEC_BATCH, n_pages).astype(jnp.int32)
    dt_init = jnp.exp(jax.random.uniform(next(keys), (DEPTH, GDN_HEADS), jnp.float32, math.log(1e-3), math.log(1e-1)))
    return {
        "x_prompt": nrm((BATCH, SEQ, D_MODEL), 1.0),
        "x_sample": nrm((DEC_BATCH, DEC_SEQ, D_MODEL), 1.0),
        "cache_nsa_kv": nrm((n_pool, DEPTH, PAGE_SIZE, 4, NSA_KV_HEADS, HEAD_DIM), 1.0),
        "cache_nsa_win": nrm((DEPTH, DEC_BATCH, min(NSA_WINDOW, PAST_LEN), 2, NSA_KV_HEADS, HEAD_DIM), 1.0),
        "state_conv": nrm((DEPTH, DEC_BATCH, CONV_WIDTH - 1, CONV_CH), 0.5),
        "state_ret": nrm((DEPTH, DEC_BATCH, RET_HEADS, HEAD_DIM, HEAD_DIM), 0.1),
        "state_gdn_conv": nrm((DEPTH, DEC_BATCH, GDN_CONV - 1, 3 * GDN_W), 1.0),
        "state_gdn": nrm((DEPTH, DEC_BATCH, GDN_HEADS, HEAD_DIM, HEAD_DIM), 0.1),
        "state_ffn_conv": nrm((DEPTH, DEC_BATCH, FFN_CONV - 1, D_FF), 1.0),
        "page_table": page_table,
        "c_prompt": nrm((BATCH, D_MODEL), 1.0),
        "c_sample": nrm((DEC_BATCH, D_MODEL), 1.0),
        "w_ada": nrm((DEPTH, D_MODEL, 6 * D_MODEL), 0.5 * D_MODEL ** -0.5),
        "b_ada": nrm((DEPTH, 6 * D_MODEL), 0.01),
        "norms": 1.0 + nrm((DEPTH, 4, D_MODEL), 0.05),
        "w_in": nrm((DEPTH, D_MODEL, IN_COLS), D_MODEL ** -0.5),
        "cmp_pool": (1.0 + nrm((DEPTH, 2, NSA_BLOCK), 0.1)) * NSA_BLOCK ** -0.5,
        "cmp_pe": nrm((DEPTH, 2, NSA_BLOCK, HEAD_DIM), 0.1),
        "rel_bias": nrm((NUM_BUCKETS, NSA_HEADS), 0.5),
        "conv_dw": nrm((DEPTH, CONV_WIDTH, CONV_CH), CONV_WIDTH ** -0.5),
        "conv_dw_b": nrm((DEPTH, CONV_CH), 0.01),
        "conv_ln_g": 1.0 + nrm((DEPTH, CONV_CH), 0.05),
        "conv_ln_b": nrm((DEPTH, CONV_CH), 0.01),
        "ret_gn": 1.0 + nrm((DEPTH, RET_W), 0.05),
        "gdn_conv_w": nrm((DEPTH, GDN_CONV, 3 * GDN_W), GDN_CONV ** -0.5),
        "gdn_A_log": jnp.log(jax.random.uniform(next(keys), (DEPTH, GDN_HEADS), jnp.float32, 1.0, 16.0)),
        "gdn_dt_bias": dt_init + jnp.log(-jnp.expm1(-dt_init)),
        "gdn_norm": 1.0 + nrm((DEPTH, HEAD_DIM), 0.05),
        "w_branch": nrm((DEPTH, N_BRANCH, BRANCH_W, D_MODEL), BRANCH_W ** -0.5),
        "w_out": nrm((DEPTH, D_MODEL, D_MODEL), D_MODEL ** -0.5),
        "ffn_up": nrm((DEPTH, D_MODEL, 2 * D_FF), D_MODEL ** -0.5),
        "ffn_dw": nrm((DEPTH, FFN_CONV, D_FF), FFN_CONV ** -0.5),
        "ffn_down": nrm((DEPTH, D_FF, D_MODEL), D_FF ** -0.5),
    }


def reference(x_prompt, x_sample, cache_nsa_kv, cache_nsa_win, state_conv, state_ret, state_gdn_conv, state_gdn,
              state_ffn_conv, page_table, c_prompt, c_sample, w_ada, b_ada, norms, w_in, cmp_pool, cmp_pe, rel_bias,
              conv_dw, conv_dw_b, conv_ln_g, conv_ln_b, ret_gn, gdn_conv_w, gdn_A_log, gdn_dt_bias, gdn_norm,
              w_branch, w_out, ffn_up, ffn_dw, ffn_down):
    B = x_prompt.shape[0]
    Bd = x_sample.shape[0]
    past = page_table.shape[1] * PAGE_SIZE
    layer_w = {"w_ada": w_ada, "b_ada": b_ada, "norms": norms, "w_in": w_in, "cmp_pool": cmp_pool,
               "cmp_pe": cmp_pe, "conv_dw": conv_dw, "conv_dw_b": conv_dw_b, "conv_ln_g": conv_ln_g,
               "conv_ln_b": conv_ln_b, "ret_gn": ret_gn, "gdn_conv_w": gdn_conv_w, "gdn_A_log": gdn_A_log,
               "gdn_dt_bias": gdn_dt_bias, "gdn_norm": gdn_norm, "w_branch": w_branch, "w_out": w_out,
               "ffn_up": ffn_up, "ffn_dw": ffn_dw, "ffn_down": ffn_down}
    yp, ys = x_prompt, x_sample
    st_p, st_s = [], []
    for l in range(DEPTH):
        p = {name: w[l] for name, w in layer_w.items()}
        yp, sp = trunk_layer(
            yp, c_prompt, 0, None, None,
            jnp.zeros((B, CONV_WIDTH - 1, CONV_CH), x_prompt.dtype),
            jnp.zeros((B, RET_HEADS, HEAD_DIM, HEAD_DIM), jnp.float32),
            jnp.zeros((B, GDN_CONV - 1, 3 * GDN_W), x_prompt.dtype),
            jnp.zeros((B, GDN_HEADS, HEAD_DIM, HEAD_DIM), jnp.float32),
            jnp.zeros((B, FFN_CONV - 1, D_FF), x_prompt.dtype),
            p, rel_bias)
        past_rows = cache_nsa_kv[page_table, l].reshape(Bd, past, 4, NSA_KV_HEADS, HEAD_DIM)
        ys, ss = trunk_layer(
            ys, c_sample, past, past_rows, cache_nsa_win[l], state_conv[l], state_ret[l],
            state_gdn_conv[l], state_gdn[l], state_ffn_conv[l], p, rel_bias)
        st_p.append(sp)
        st_s.append(ss)

    def stack(outs, i, axis):
        return jnp.stack([o[i] for o in outs], axis=axis)

    kv_p, kv_s = stack(st_p, 0, 1), stack(st_s, 0, 1)
    win_p, win_s = stack(st_p, 1, 0), stack(st_s, 1, 0)
    conv_p, conv_s = stack(st_p, 2, 0), stack(st_s, 2, 0)
    ret_p, ret_s = stack(st_p, 3, 0), stack(st_s, 3, 0)
    gdnc_p, gdnc_s = stack(st_p, 4, 0), stack(st_s, 4, 0)
    gdn_p, gdn_s = stack(st_p, 5, 0), stack(st_s, 5, 0)
    ffn_p, ffn_s = stack(st_p, 6, 0), stack(st_s, 6, 0)
    return (yp, ys, kv_p, kv_s, win_p, win_s, conv_p, conv_s, ret_p, ret_s, gdnc_p, gdnc_s, gdn_p, gdn_s, ffn_p, ffn_s)
```

```python
import math
import numpy as np
import concourse.bass as bass
import concourse.mybir as mybir
from concourse.bass_utils import run_bass_kernel_spmd

F32 = mybir.dt.float32
BF16 = mybir.dt.bfloat16
I32 = mybir.dt.int32
AF = mybir.ActivationFunctionType
ALU = mybir.AluOpType
AX = mybir.AxisListType

D = 1024
HD = 64
DEPTH = 2
NKV = 768
CONV_CH = 256
CONV_W = 31
D_FF = 2816
IN_COLS = 7700
EPS = 1e-6
NEG = -30000.0
O_Q, O_KV, O_GT, O_CV, O_RET, O_GDN, O_MG = 0, 256, 1024, 1036, 1548, 2572, 3604
GAM = [1.0 - 2.0 ** (-5.0 - h) for h in range(4)]


class Tile:
    def __init__(self, t, name):
        self.t = t
        self.name = name
        self.w = None
        self.r = {}
        self.psum = False

    def __getitem__(self, idx):
        return View(self, self.t[idx])

    def re(self, pat, **kw):
        return self[:].re(pat, **kw)

    def pbc(self, n):
        return self[:].pbc(n)


class View:
    def __init__(self, tile, ap):
        self.tile = tile
        self.ap = ap

    def __getitem__(self, idx):
        return View(self.tile, self.ap[idx])

    def re(self, pat, **kw):
        return View(self.tile, self.ap.rearrange(pat, **kw))

    def bc(self, shape):
        return View(self.tile, self.ap.to_broadcast(shape))

    def unsq(self, ax):
        return View(self.tile, self.ap.unsqueeze(ax))

    def pbc(self, n):
        return View(self.tile, self.ap.partition_broadcast(n))


def _ap(x):
    return x.ap if isinstance(x, View) else x


class Cx:
    def __init__(self, nc):
        self.nc = nc
        self.eng = dict(pe=nc.tensor, dve=nc.vector, act=nc.scalar, pool=nc.gpsimd, sp=nc.sync)
        self.sem = {k: nc.alloc_semaphore("cs_" + k) for k in ("pe", "dve", "act", "pool")}
        self.cnt = {k: 0 for k in self.sem}
        self.dsem = {}
        self.seen = {k: {} for k in self.eng}
        self.nops = 0

    def sb(self, name, shape, dt=F32):
        return Tile(self.nc.alloc_sbuf_tensor(name, list(shape), dt), name)

    def ps(self, name, shape, dt=F32):
        t = Tile(self.nc.alloc_psum_tensor(name, list(shape), dt), name)
        t.psum = True
        return t

    def dram(self, name, shape, dt=F32, kind="Internal"):
        return Tile(self.nc.dram_tensor(name, list(shape), dt, kind=kind), name)

    def _semobj(self, key):
        return self.sem[key[1]] if key[0] == "c" else self.dsem[key[1]]["sems"][key[2]]

    def _sync(self, e, reads, writes):
        deps = {}

        def add(kv):
            k, v = kv
            if deps.get(k, 0) < v:
                deps[k] = v

        for t in reads:
            if t.w is not None:
                add(t.w)
        for t in writes:
            if t.w is not None:
                add(t.w)
            for kv in t.r.items():
                add(kv)
        seen = self.seen[e]
        rt = set(id(t) for t in reads)
        for k, v in deps.items():
            if k == ("c", e) and e == "pe":
                continue
            if seen.get(k, 0) >= v:
                continue
            self.eng[e].wait_ge(self._semobj(k), v)
            seen[k] = v

    def op(self, e, reads, writes, fn):
        reads = [x.tile for x in reads if isinstance(x, View)]
        writes = [x.tile for x in writes if isinstance(x, View)]
        self._sync(e, reads, writes)
        self.cnt[e] += 1
        n = self.cnt[e]
        fn(self.eng[e]).then_inc(self.sem[e], 1)
        key = ("c", e)
        for t in reads:
            if t.psum:
                t.w = (key, n)
                t.r = {}
            elif t.r.get(key, 0) < n:
                t.r[key] = n
        for t in writes:
            t.w = (key, n)
            t.r = {}
        self.nops += 1

    RING = 36

    def dma(self, out, in_, q="sp", ch=None, indirect=None):
        ch = q
        if ch not in self.dsem:
            self.dsem[ch] = {"sems": [self.nc.alloc_semaphore("ds_%s%d" % (ch, i)) for i in range(self.RING)],
                             "cnt": [0] * self.RING, "n": 0}
        ring = self.dsem[ch]
        r = ring["n"] % self.RING
        ring["n"] += 1
        reads = [in_.tile]
        if indirect is not None:
            reads.append(indirect.tile)
        writes = [out.tile]
        self._sync(q, reads, writes)
        key = ("d", ch, r)
        if ring["cnt"][r] > 0 and self.seen[q].get(key, 0) < 16 * ring["cnt"][r]:
            self.eng[q].wait_ge(ring["sems"][r], 16 * ring["cnt"][r])
            self.seen[q][key] = 16 * ring["cnt"][r]
        ring["cnt"][r] += 1
        val = 16 * ring["cnt"][r]
        sem = ring["sems"][r]
        if indirect is None:
            self.eng[q].dma_start(out=out.ap, in_=in_.ap).then_inc(sem, 16)
        else:
            self.eng[q].indirect_dma_start(
                out=out.ap, out_offset=None, in_=in_.ap,
                in_offset=bass.IndirectOffsetOnAxis(ap=indirect.ap, axis=0)).then_inc(sem, 16)
        for t in reads:
            if t.r.get(key, 0) < val:
                t.r[key] = val
        for t in writes:
            t.w = (key, val)
            t.r = {}
        self.nops += 1

    def _all_sems(self):
        items = [(("c", k), self.sem[k], self.cnt[k]) for k in self.sem if self.cnt[k]]
        for ch, ring in self.dsem.items():
            for r in range(self.RING):
                if ring["cnt"][r]:
                    items.append((("d", ch, r), ring["sems"][r], 16 * ring["cnt"][r]))
        return items

    def barrier(self):
        items = self._all_sems()
        for e in self.eng:
            for key, s, v in items:
                if self.seen[e].get(key, 0) >= v:
                    continue
                self.eng[e].wait_ge(s, v)
                self.seen[e][key] = v

    def arena(self, nbytes):
        self.ar = self.nc.alloc_sbuf_tensor("arena", [128, nbytes // 4], F32)
        self.ar_bytes = nbytes
        self.ar_off = 0

    def ar_reset(self):
        self.barrier()
        self.ar_off = 0

    def ar_alloc(self, name, shape, dt=F32):
        esz = 4 if dt in (F32, I32) else 2
        n = 1
        for d_ in shape[1:]:
            n *= d_
        nb = (n * esz + 31) // 32 * 32
        assert self.ar_off + nb <= self.ar_bytes, ("arena overflow", name, self.ar_off, nb, self.ar_bytes)
        ap = self.ar[0:shape[0], self.ar_off // 4:(self.ar_off + nb) // 4]
        if dt != F32:
            ap = ap.bitcast(dt)
        ap = ap[:, 0:n]
        if len(shape) > 2:
            names = " ".join("d%d" % i for i in range(1, len(shape)))
            kw = {"d%d" % i: shape[i] for i in range(1, len(shape))}
            ap = ap.rearrange("p (%s) -> p %s" % (names, names), **kw)
        self.ar_off += nb
        return Tile(ap, name)

    def finish(self):
        for key, sem, v in self._all_sems():
            self.nc.sync.wait_ge(sem, v)

    def mm(self, out, lhsT, rhs, start=True, stop=True):
        self.op("pe", [lhsT, rhs], [out],
                lambda e: e.matmul(out.ap, lhsT=lhsT.ap, rhs=rhs.ap, start=start, stop=stop))

    def tr(self, out, in_, ident):
        self.op("pe", [in_, ident], [out], lambda e: e.transpose(out.ap, in_.ap, ident.ap))

    def act(self, out, in_, func, bias=None, scale=1.0, accum=None):
        rd = [in_] + ([bias] if isinstance(bias, View) else [])
        wr = [out] + ([accum] if accum is not None else [])
        kw = {}
        if bias is not None:
            kw["bias"] = _ap(bias)
        if accum is not None:
            kw["accum_out"] = accum.ap
        self.op("act", rd, wr, lambda e: e.activation(out=out.ap, in_=in_.ap, func=func, scale=scale, **kw))

    def tt(self, e, out, in0, in1, op):
        self.op(e, [in0, in1], [out], lambda g: g.tensor_tensor(out=out.ap, in0=in0.ap, in1=in1.ap, op=op))

    def ts(self, e, out, in0, s1, op0, s2=None, op1=None):
        rd = [in0] + [s for s in (s1, s2) if isinstance(s, View)]
        if op1 is None:
            self.op(e, rd, [out], lambda g: g.tensor_scalar(out=out.ap, in0=in0.ap, scalar1=_ap(s1), scalar2=None, op0=op0))
        else:
            self.op(e, rd, [out], lambda g: g.tensor_scalar(out=out.ap, in0=in0.ap, scalar1=_ap(s1), scalar2=_ap(s2), op0=op0, op1=op1))

    def stt(self, out, in0, scalar, in1, op0, op1):
        rd = [in0, in1] + ([scalar] if isinstance(scalar, View) else [])
        self.op("dve", rd, [out], lambda g: g.scalar_tensor_tensor(out=out.ap, in0=in0.ap, scalar=_ap(scalar), in1=in1.ap, op0=op0, op1=op1))

    def rsqrt(self, out, in_):
        self.act(out, in_, AF.Ln)
        self.act(out, out, AF.Exp, scale=-0.5)

    def cp(self, e, out, in_):
        if e == "act":
            self.act(out, in_, AF.Copy)
        else:
            self.op(e, [in_], [out], lambda g: g.tensor_copy(out=out.ap, in_=in_.ap))

    def memset(self, e, out, val):
        self.op(e, [], [out], lambda g: g.memset(out.ap, val))

    def red(self, out, in_, op=ALU.add):
        self.op("dve", [in_], [out], lambda g: g.tensor_reduce(out=out.ap, in_=in_.ap, axis=AX.X, op=op))

    def max8(self, out, in_):
        self.op("dve", [in_], [out], lambda g: g.max(out=out.ap, in_=in_.ap))

    def match_replace(self, out, rep, vals, imm):
        self.op("dve", [rep, vals], [out], lambda g: g.match_replace(out=out.ap, in_to_replace=rep.ap, in_values=vals.ap, imm_value=imm))


def t5_thresholds():
    n = np.arange(0, 400)
    exact = 16
    large = exact + (np.log(np.maximum(n, 1).astype(np.float32) / np.float32(exact)) / np.float32(math.log(128 / exact))
                     * np.float32(32 - exact)).astype(np.int32)
    b = np.where(n < exact, n, np.minimum(large, 31))
    thr = []
    for k in range(1, 32):
        thr.append(int(np.min(n[b >= k])))
    return thr


class StopBuild(Exception):
    pass


class Cfg:
    stop = None

    def __init__(self, SEQ=4096, NS=4, TS=8, PAST=16384, NPOOL=5120):
        self.SEQ, self.NS, self.TS, self.PAST, self.NPOOL = SEQ, NS, TS, PAST, NPOOL
        self.NPG = PAST // 128
        self.NT = SEQ // 128
        self.NBP = SEQ // 64
        self.WIN = 512


def build(cfg):
    nc = bass.Bass("TRN2", target_bir_lowering=False)
    cx = Cx(nc)
    SEQ, NS, TS, PAST, NPG, NT, NBP = cfg.SEQ, cfg.NS, cfg.TS, cfg.PAST, cfg.NPG, cfg.NT, cfg.NBP
    NSTOK = NS * TS
    J0 = NBP - 2
    NBS = PAST // 64
    WLC = NBP + 16
    NTOK = SEQ + NSTOK
    NR2 = cfg.NPOOL * DEPTH * 128 * 2

    def chk(n):
        if cfg.stop is not None and cfg.stop == n:
            raise StopBuild()

    def din(name, shape, dt=F32):
        return cx.dram(name, shape, dt, kind="ExternalInput")

    def dout(name, shape, dt=F32):
        return cx.dram(name, shape, dt, kind="ExternalOutput")

    xp = din("xp", [SEQ, D]); cpd = din("cp", [1, D]); xs = din("xs", [NSTOK, D]); csd = din("cs", [NS, D])
    pool_d = din("pool", [NR2, 256])
    ptab = din("ptab", [NS, NPG], I32)
    win_in = din("win_in", [DEPTH, NS, 512, 256])
    st_conv = din("st_conv", [DEPTH, NS, 30, 256]); st_ret = din("st_ret", [DEPTH, NS, 4, 64, 64])
    st_gconv = din("st_gconv", [DEPTH, NS, 3, 768]); st_gdn = din("st_gdn", [DEPTH, NS, 4, 64, 64])
    st_ffn = din("st_ffn", [DEPTH, NS, 2, D_FF])
    w_ada = din("w_ada", [DEPTH, D, 6 * D]); b_ada = din("b_ada", [DEPTH, 6 * D]); norms = din("norms", [DEPTH, 4, D])
    w_in = din("w_in", [DEPTH, D, IN_COLS]); cmp_pool = din("cmp_pool", [DEPTH, 2, 64]); cmp_pe = din("cmp_pe", [DEPTH, 2, 64, 64])
    rel_bias = din("rel_bias", [32, 4]); conv_dw = din("conv_dw", [DEPTH, 31, 256]); conv_dw_b = din("conv_dw_b", [DEPTH, 256])
    conv_ln_g = din("conv_ln_g", [DEPTH, 256]); conv_ln_b = din("conv_ln_b", [DEPTH, 256]); ret_gn = din("ret_gn", [DEPTH, 256])
    gdn_conv_w = din("gdn_conv_w", [DEPTH, 4, 768]); gdn_A_log = din("gdn_A_log", [DEPTH, 4]); gdn_dt_bias = din("gdn_dt_bias", [DEPTH, 4])
    gdn_norm = din("gdn_norm", [DEPTH, 64]); w_branch = din("w_branch", [DEPTH, 4, 256, D]); w_out = din("w_out", [DEPTH, D, D])
    ffn_up = din("ffn_up", [DEPTH, D, 2 * D_FF]); ffn_dw = din("ffn_dw", [DEPTH, 3, D_FF]); ffn_down = din("ffn_down", [DEPTH, D_FF, D])
    c_ident = din("c_ident", [128, 128]); c_triu = din("c_triu", [128, 256]); c_tril = din("c_tril", [128, 128])
    c_U = din("c_U", [128, 128]); c_last = din("c_last", [2, 128, 128])
    c_rot_p = din("c_rot_p", [SEQ, 64]); c_rot_s = din("c_rot_s", [TS, 64])
    c_retD = din("c_retD", [2, 4, 128, 128]); c_retxi = din("c_retxi", [2, 4, 64, 128]); c_retz = din("c_retz", [2, 128, 4])
    c_dbt = din("c_dbt", [128, 256]); c_dlc = din("c_dlc", [128, WLC]); c_keep = din("c_keep", [128, 2 * WLC])
    c_keeps = din("c_keeps", [8, 2 * (NBS + 1)])
    c_wm4 = din("c_wm4", [128, 128]); c_iota = din("c_iota", [128, 1])

    yp = dout("yp", [SEQ, D]); ys = dout("ys", [NSTOK, D])
    kvp = dout("kvp", [DEPTH, SEQ, 512]); kvs = dout("kvs", [NS, DEPTH, TS, 512])
    winp = dout("winp", [DEPTH, 512, 256]); wins = dout("wins", [DEPTH, NS, 512, 256])
    convp = dout("convp", [DEPTH, 30, 256]); convs = dout("convs", [DEPTH, NS, 30, 256])
    retp = dout("retp", [DEPTH, 4, 64, 64]); rets = dout("rets", [DEPTH, NS, 4, 64, 64])
    gcp = dout("gcp", [DEPTH, 3, 768]); gcs = dout("gcs", [DEPTH, NS, 3, 768])
    gdnp = dout("gdnp", [DEPTH, 4, 64, 64]); gdns = dout("gdns", [DEPTH, NS, 4, 64, 64])
    ffnp = dout("ffnp", [DEPTH, 2, D_FF]); ffns = dout("ffns", [DEPTH, NS, 2, D_FF])

    xres = cx.dram("xres", [8, 128, NTOK])
    brd = cx.dram("brd", [8, 128, NTOK], BF16)
    bt_d = cx.dram("bt_d", [128, 4 * 256]); lc_d = cx.dram("lc_d", [128, 4 * WLC])

    MTN = 256
    ident = cx.sb("ident", [128, 128]); cx.dma(ident[:], c_ident[:])
    ones_b = cx.sb("ones_b", [128, 128], BF16); cx.memset("dve", ones_b[:], 1.0)
    ones_f = cx.sb("ones_f", [128, 128]); cx.memset("dve", ones_f[:], 1.0)
    rb = cx.sb("rb", [128, 128]); cx.dma(rb[:], rel_bias.re("b h -> (b h)").pbc(128))
    c31 = rb[:, 124:128]
    xT = cx.sb("xT", [128, 8, MTN]); hT = cx.sb("hT", [128, 8, MTN], BF16)
    sqb = cx.sb("sqb", [128, MTN], BF16); rstd = cx.sb("rstd", [128, MTN]); tmpN = cx.sb("tmpN", [128, MTN])
    xtok = cx.sb("xtok", [128, D]); stg = cx.sb("stg", [128, 128]); sttok = cx.sb("sttok", [32, 128])
    g1 = cx.sb("g1", [128, MTN]); g2 = cx.sb("g2", [128, MTN]); g3 = cx.sb("g3", [128, MTN])
    NG = NS + 1
    modT = cx.sb("modT", [128, 48, NG]); cT = cx.sb("cT", [128, 8, NG], BF16); nrm = cx.sb("nrm", [128, 4, 8])
    badaT = cx.sb("badaT", [128, 48]); mv = cx.sb("mv", [128, NG, 6, 8])
    cx.arena(178 * 1024)

    ps_tmp = [cx.ps("pt%d" % i, [128, 512]) for i in range(5)]
    ps_acc = [cx.ps("pa%d" % i, [128, 512]) for i in range(3)]
    rr = {"tmp": 0, "acc": 0, "ev": 0, "eb": 0}

    def ptmp():
        rr["tmp"] = (rr["tmp"] + 1) % len(ps_tmp)
        return ps_tmp[rr["tmp"]]

    def pacc():
        rr["acc"] = (rr["acc"] + 1) % len(ps_acc)
        return ps_acc[rr["acc"]]

    def evac(out, in_):
        rr["ev"] ^= 1
        cx.cp("dve" if rr["ev"] else "act", out, in_)

    def transpose_to(out_sb, in_sb, rows, cols):
        p = ptmp()
        cx.tr(p[0:cols, 0:rows], in_sb, ident[0:rows, 0:rows])
        evac(out_sb, p[0:cols, 0:rows])

    def load_T(dst, src2d, r):
        cx.dma(stg[0:r, :], src2d)
        transpose_to(dst, stg[0:r, :], r, 128)

    def wview(tile, kc, n):
        return tile[:].re("p (k n) -> p k n", k=kc)

    def load_w(dst, src2d, ncols, chunk=2048):
        v = src2d.re("(k p) n -> p k n", p=128)
        for c0 in range(0, ncols, chunk):
            c1 = min(ncols, c0 + chunk)
            cx.dma(dst[:, :, c0:c1], v[:, :, c0:c1], q="pool")

    groups = [dict(kind="s", s=s, T=TS, ntile=1, tok0=SEQ + s * TS, gi=1) for s in range(NS)]
    groups.append(dict(kind="p", s=0, T=128, ntile=NT, tok0=0, gi=0))

    thr = t5_thresholds()

    def build_tables():
        cx.ar_reset()
        rbd = cx.ar_alloc("rbd", [128, 128]); dbt = cx.ar_alloc("dbt", [128, 256]); dlc = cx.ar_alloc("dlc", [128, WLC])
        BTt = cx.ar_alloc("BTt", [128, 4, 256]); LCt = cx.ar_alloc("LCt", [128, 4, WLC]); btmp = cx.ar_alloc("btmp", [128, 256])
        cx.tt("dve", rbd[:, 4:128], rb[:, 4:128], rb[:, 0:124], ALU.subtract)
        cx.dma(dbt[:], c_dbt[:]); cx.dma(dlc[:], c_dlc[:])
        chk(0.1)
        for (tab, dtab, W) in ((BTt, dbt, 256), (LCt, dlc, WLC)):
            for h in range(4):
                dst = tab[:, h, :]
                cx.ts("dve", dst, dtab[:, 0:W], 0.0, ALU.is_lt, NEG, ALU.mult)
                cx.ts("dve", dst, dst, rb[:, h:h + 1], ALU.add)
                for b in range(1, 32):
                    cx.ts("dve", btmp[:, 0:W], dtab[:, 0:W], float(thr[b - 1]), ALU.is_ge, rbd[:, 4 * b + h:4 * b + h + 1], ALU.mult)
                    cx.tt("dve", dst, dst, btmp[:, 0:W], ALU.add)
                chk(0.2)
        chk(0.3)
        cx.dma(bt_d[:], BTt[:].re("p h w -> p (h w)")); cx.dma(lc_d[:], LCt[:].re("p h w -> p (h w)"))

    def load_cT():
        ctok = cx.ar_alloc("ctok", [16, D])
        cx.dma(ctok[0:NS, :], csd[:]); cx.dma(ctok[NS:NS + 1, :], cpd[:])
        cx.act(ctok[0:NG, :], ctok[0:NG, :], AF.Silu)
        for kc in range(8):
            p = ptmp()
            cx.tr(p[:, 0:NG], ctok[0:NG, kc * 128:(kc + 1) * 128], ident[0:NG, 0:NG])
            evac(cT[:, kc, :], p[:, 0:NG])

    def compute_mod(l):
        cx.ar_reset()
        adaw = [wview(cx.ar_alloc("adaw%d" % i, [128, 8 * 512], BF16), 8, 512) for i in range(2)]
        cx.dma(stg[0:48, :], b_ada[l].re("(c p) -> c p", p=128))
        transpose_to(badaT[:, 0:48], stg[0:48, :], 48, 128)
        cx.dma(stg[0:32, :], norms[l].re("j (c p) -> (j c) p", p=128))
        transpose_to(nrm[:].re("p j c -> p (j c)"), stg[0:32, :], 32, 128)
        for g in range(12):
            wt = adaw[g % 2]
            cx.dma(wt, w_ada[l].re("(kc p) n -> p kc n", p=128)[:, :, g * 512:(g + 1) * 512], q="pool")
            for b in range(4):
                p = ptmp()
                for kc in range(8):
                    cx.mm(p[:, 0:NG], wt[:, kc, b * 128:(b + 1) * 128], cT[:, kc, :], start=(kc == 0), stop=(kc == 7))
                blk = g * 4 + b
                cx.ts("dve", modT[:, blk, :], p[:, 0:NG], badaT[:, blk:blk + 1], ALU.add)
        for g in range(NG):
            m = lambda j: modT[:, j * 8:(j + 1) * 8, g]
            cx.stt(mv[:, g, 0, :], m(1), 1.0, nrm[:, 0, :], ALU.add, ALU.mult)
            cx.cp("dve", mv[:, g, 1, :], m(0))
            cx.tt("dve", mv[:, g, 2, :], m(2), nrm[:, 1, :], ALU.mult)
            cx.stt(mv[:, g, 3, :], m(4), 1.0, nrm[:, 2, :], ALU.add, ALU.mult)
            cx.cp("dve", mv[:, g, 4, :], m(3))
            cx.tt("dve", mv[:, g, 5, :], m(5), nrm[:, 3, :], ALU.mult)

    def rms_rstd(src, N, nchunk=8):
        p = ptmp()
        for kc in range(nchunk):
            cx.act(sqb[:, 0:N], src[:, kc, 0:N], AF.Square)
            cx.mm(p[:, 0:N], ones_b[:, :], sqb[:, 0:N], start=(kc == 0), stop=(kc == nchunk - 1))
        cx.ts("dve", rstd[:, 0:N], p[:, 0:N], 1.0 / D, ALU.mult, EPS, ALU.add)
        cx.rsqrt(rstd[:, 0:N], rstd[:, 0:N])

    def mod_norm(g, N, jg, js):
        rms_rstd(xT, N)
        for kc in range(8):
            cx.stt(tmpN[:, 0:N], xT[:, kc, 0:N], mv[:, g, jg, kc:kc + 1], rstd[:, 0:N], ALU.mult, ALU.mult)
            cx.act(hT[:, kc, 0:N], tmpN[:, 0:N], AF.Identity, bias=mv[:, g, js, kc:kc + 1])

    def resid_add(g, N, src, jg):
        rms_rstd(src, N)
        for kc in range(8):
            cx.stt(tmpN[:, 0:N], src[:, kc, 0:N], mv[:, g, jg, kc:kc + 1], rstd[:, 0:N], ALU.mult, ALU.mult)
            cx.tt("pool", xT[:, kc, 0:N], xT[:, kc, 0:N], tmpN[:, 0:N], ALU.add)

    def load_x(l, grp, t0, N):
        if l == 0:
            src = xp if grp["kind"] == "p" else xs
            base = t0 if grp["kind"] == "p" else grp["s"] * TS + t0
            for j in range(0, N, 128):
                n = min(128, N - j)
                cx.dma(xtok[0:n, :], src[base + j:base + j + n, :])
                for kc in range(8):
                    p = ptmp()
                    cx.tr(p[:, 0:n], xtok[0:n, kc * 128:(kc + 1) * 128], ident[0:n, 0:n])
                    evac(xT[:, kc, j:j + n], p[:, 0:n])
        else:
            a = grp["tok0"] + t0
            cx.dma(xT[:, :, 0:N], xres[:, :, a:a + N].re("c p n -> p c n"))

    def store_x(grp, t0, N, final):
        a = grp["tok0"] + t0
        if not final:
            cx.dma(xres[:, :, a:a + N].re("c p n -> p c n"), xT[:, :, 0:N])
        else:
            dst = yp if grp["kind"] == "p" else ys
            base = t0 if grp["kind"] == "p" else grp["s"] * TS + t0
            for j in range(0, N, 128):
                n = min(128, N - j)
                for kc in range(8):
                    p = ptmp()
                    cx.tr(p[0:n, 0:128], xT[:, kc, j:j + n], ident[:, :])
                    evac(xtok[0:n, kc * 128:(kc + 1) * 128], p[0:n, 0:128])
                cx.dma(dst[base + j:base + j + n, :], xtok[0:n, :])

    def run_gens(gens):
        live = list(gens)
        while live:
            for g in list(live):
                try:
                    next(g)
                except StopIteration:
                    live.remove(g)

    def mts(grp, mtn):
        ntok = grp["T"] * grp["ntile"]
        return [(t0, min(mtn, ntok - t0)) for t0 in range(0, ntok, mtn)]

    def phase2(l):
        cx.ar_reset()
        W_UP = wview(cx.ar_alloc("W_UP", [128, 8 * 5632], BF16), 8, 5632)
        W_DN = wview(cx.ar_alloc("W_DN", [128, 22 * 1024], BF16), 22, 1024)
        ffw = cx.ar_alloc("ffw", [128, 22, 3]); ffhalo = cx.ar_alloc("ffhalo", [128, 22, 2]); fft = cx.ar_alloc("fft", [128, 2 + MTN])
        actT = cx.ar_alloc("actT", [128, 22, MTN], BF16); fT = cx.ar_alloc("fT", [128, 8, MTN])
        load_w(W_UP, ffn_up[l], 5632); load_w(W_DN, ffn_down[l], 1024)
        for c in range(22):
            load_T(ffw[:, c, :], ffn_dw[l, :, c * 128:(c + 1) * 128], 3)
        for gi, grp in enumerate(groups):
            if grp["kind"] == "p":
                cx.memset("pool", ffhalo[:], 0.0)
            else:
                for c in range(22):
                    cx.dma(sttok[0:2, :], st_ffn[l, grp["s"], :, c * 128:(c + 1) * 128])
                    transpose_to(ffhalo[:, c, :], sttok[0:2, :], 2, 128)
            for (t0, N) in mts(grp, MTN):
                load_x(1, grp, t0, N)
                mod_norm(gi, N, 3, 4)
                for c in range(22):
                    pg = ptmp()
                    for kc in range(8):
                        cx.mm(pg[:, 0:N], W_UP[:, kc, c * 128:(c + 1) * 128], hT[:, kc, 0:N], start=(kc == 0), stop=(kc == 7))
                    pv = ptmp()
                    for kc in range(8):
                        cx.mm(pv[:, 0:N], W_UP[:, kc, D_FF + c * 128:D_FF + (c + 1) * 128], hT[:, kc, 0:N], start=(kc == 0), stop=(kc == 7))
                    cx.cp("pool", fft[:, 0:2], ffhalo[:, c, :])
                    cx.cp("act", fft[:, 2:2 + N], pg[:, 0:N])
                    cx.cp("pool", ffhalo[:, c, :], fft[:, N:N + 2])
                    cx.ts("dve", g1[:, 0:N], fft[:, 0:N], ffw[:, c, 0:1], ALU.mult)
                    cx.stt(g1[:, 0:N], fft[:, 1:1 + N], ffw[:, c, 1:2], g1[:, 0:N], ALU.mult, ALU.add)
                    cx.stt(g1[:, 0:N], fft[:, 2:2 + N], ffw[:, c, 2:3], g1[:, 0:N], ALU.mult, ALU.add)
                    cx.tt("pool", g2[:, 0:N], g1[:, 0:N], g1[:, 0:N], ALU.mult)
                    cx.ts("pool", g2[:, 0:N], g2[:, 0:N], 0.044715, ALU.mult, 1.0, ALU.add)
                    cx.tt("pool", g2[:, 0:N], g2[:, 0:N], g1[:, 0:N], ALU.mult)
                    cx.act(g3[:, 0:N], g2[:, 0:N], AF.Sigmoid, scale=1.5957691216)
                    cx.tt("dve", g3[:, 0:N], g3[:, 0:N], g1[:, 0:N], ALU.mult)
                    cx.tt("dve", actT[:, c, 0:N], g3[:, 0:N], pv[:, 0:N], ALU.mult)
                for ob in range(8):
                    p = ptmp()
                    for c in range(22):
                        cx.mm(p[:, 0:N], W_DN[:, c, ob * 128:(ob + 1) * 128], actT[:, c, 0:N], start=(c == 0), stop=(c == 21))
                    evac(fT[:, ob, 0:N], p[:, 0:N])
                resid_add(gi, N, fT, 5)
                store_x(grp, t0, N, final=(l == DEPTH - 1))
            dst = ffnp[l] if grp["kind"] == "p" else ffns[l, grp["s"]]
            for c in range(22):
                transpose_to(sttok[0:2, :], ffhalo[:, c, :], 128, 2)
                cx.dma(dst[:, c * 128:(c + 1) * 128], sttok[0:2, :])

    def phase1b(l):
        cx.ar_reset()
        W_MG = wview(cx.ar_alloc("W_MG", [128, 8 * 4096], BF16), 8, 4096)
        W_BR = wview(cx.ar_alloc("W_BR", [128, 8 * 1024], BF16), 8, 1024)
        W_OUT = wview(cx.ar_alloc("W_OUT", [128, 8 * 1024], BF16), 8, 1024)
        BRT = cx.ar_alloc("BRTb", [128, 8, MTN], BF16); mT = cx.ar_alloc("mT", [128, 8, MTN], BF16); fT = cx.ar_alloc("fTb", [128, 8, MTN])
        wv = w_in[l].re("(k p) n -> p k n", p=128)
        for c0 in range(0, 4096, 2048):
            cx.dma(W_MG[:, :, c0:c0 + 2048], wv[:, :, O_MG + c0:O_MG + c0 + 2048], q="pool")
        load_w(W_BR, w_branch[l].re("n k d -> (n k) d"), 1024)
        load_w(W_OUT, w_out[l], 1024)
        for gi, grp in enumerate(groups):
            for (t0, N) in mts(grp, MTN):
                a0 = grp["tok0"] + t0
                load_x(1, grp, t0, N)
                cx.dma(BRT[:, :, 0:N], brd[:, :, a0:a0 + N].re("c p n -> p c n"))
                mod_norm(gi, N, 0, 1)
                for ob in range(8):
                    for n in range(4):
                        pb = ptmp()
                        for kc in range(2):
                            cx.mm(pb[:, 0:N], W_BR[:, n * 2 + kc, ob * 128:(ob + 1) * 128], BRT[:, n * 2 + kc, 0:N], start=(kc == 0), stop=(kc == 1))
                        pg = ptmp()
                        for kc in range(8):
                            cx.mm(pg[:, 0:N], W_MG[:, kc, n * 1024 + ob * 128:n * 1024 + (ob + 1) * 128], hT[:, kc, 0:N], start=(kc == 0), stop=(kc == 7))
                        cx.act(g2[:, 0:N], pg[:, 0:N], AF.Sigmoid)
                        if n == 0:
                            cx.tt("dve", g1[:, 0:N], g2[:, 0:N], pb[:, 0:N], ALU.mult)
                        else:
                            cx.tt("dve", g3[:, 0:N], g2[:, 0:N], pb[:, 0:N], ALU.mult)
                            cx.tt("pool", g1[:, 0:N], g1[:, 0:N], g3[:, 0:N], ALU.add)
                    cx.cp("act", mT[:, ob, 0:N], g1[:, 0:N])
                for ob in range(8):
                    p = ptmp()
                    for kc in range(8):
                        cx.mm(p[:, 0:N], W_OUT[:, kc, ob * 128:(ob + 1) * 128], mT[:, kc, 0:N], start=(kc == 0), stop=(kc == 7))
                    evac(fT[:, ob, 0:N], p[:, 0:N])
                resid_add(gi, N, fT, 2)
                store_x(grp, t0, N, final=False)


    def phase1a(l):
        cx.ar_reset()
        W1 = wview(cx.ar_alloc("W1", [128, 8 * O_MG], BF16), 8, O_MG)
        triu = cx.ar_alloc("triu", [128, 256]); cx.dma(triu[:], c_triu[:])
        tril = cx.ar_alloc("tril", [128, 128]); cx.dma(tril[:], c_tril[:])
        Umat = cx.ar_alloc("Umat", [128, 128]); cx.dma(Umat[:], c_U[:])
        lastmg = [cx.ar_alloc("lastm%d" % g, [128, 128]) for g in range(2)]
        for g in range(2):
            cx.dma(lastmg[g][:], c_last[g])
        wm4 = cx.ar_alloc("wm4", [128, 128]); cx.dma(wm4[:], c_wm4[:])
        iota = cx.ar_alloc("iota", [128, 1]); cx.dma(iota[:], c_iota[:])
        retDg = [cx.ar_alloc("retD0", [128, 4, 128]), cx.ar_alloc("retD1", [8, 4, 8])]
        cx.dma(retDg[0][:], c_retD[0].re("h s c -> s h c")); cx.dma(retDg[1][:], c_retD[1, :, 0:8, 0:8].re("h s c -> s h c"))
        retxig = [cx.ar_alloc("retxi0", [64, 4, 128]), cx.ar_alloc("retxi1", [64, 4, 8])]
        cx.dma(retxig[0][:], c_retxi[0].re("h d c -> d h c")); cx.dma(retxig[1][:], c_retxi[1, :, :, 0:8].re("h d c -> d h c"))
        retzg = [cx.ar_alloc("retz0", [128, 4]), cx.ar_alloc("retz1", [128, 4])]
        cx.dma(retzg[0][:], c_retz[0]); cx.dma(retzg[1][:], c_retz[1])
        BT = cx.ar_alloc("BT", [128, 4, 256]); cx.dma(BT[:].re("p h w -> p (h w)"), bt_d[:])
        LC = cx.ar_alloc("LC", [128, 4, WLC]); cx.dma(LC[:].re("p h w -> p (h w)"), lc_d[:])
        keepadd = cx.ar_alloc("keepadd", [128, 2 * WLC]); cx.dma(keepadd[:], c_keep[:])
        MT1 = 128
        convw = cx.ar_alloc("convw", [128, 2, 31]); cvp = cx.ar_alloc("cvp", [128, 6]); gdw = cx.ar_alloc("gdw", [128, 6, 4])
        gnb = cx.ar_alloc("gnb", [128, 256]); gnorm = cx.ar_alloc("gnorm", [128, 64]); negA = cx.ar_alloc("negA", [128, 4]); dtb = cx.ar_alloc("dtb", [128, 4])
        PEK = cx.ar_alloc("PEK", [128, 128]); PEV = cx.ar_alloc("PEV", [128, 128])
        plf = cx.ar_alloc("plf", [128, 258]); POOLK2 = cx.ar_alloc("POOLK2", [128, 2], BF16); BIGV = cx.ar_alloc("BIGV", [128, 256], BF16)

        def load_layer_params(l):
            for c in range(2):
                load_T(convw[:, c, :], conv_dw[l, :, c * 128:(c + 1) * 128], 31)
            for j, src in enumerate((conv_dw_b, conv_ln_g, conv_ln_b)):
                cx.dma(stg[2 * j:2 * j + 2, :], src[l].re("(c p) -> c p", p=128))
            transpose_to(cvp[:, 0:6], stg[0:6, :], 6, 128)
            for c in range(6):
                load_T(gdw[:, c, :], gdn_conv_w[l, :, c * 128:(c + 1) * 128], 4)
            cx.dma(gnb[:], ret_gn[l].pbc(128)); cx.dma(gnorm[:], gdn_norm[l].pbc(128))
            cx.dma(negA[:], gdn_A_log[l].pbc(128)); cx.dma(dtb[:], gdn_dt_bias[l].pbc(128))
            cx.act(negA[:], negA[:], AF.Exp)
            cx.ts("dve", negA[:], negA[:], -1.0, ALU.mult)
            for half in range(2):
                for kv in range(2):
                    cx.dma(PEK[half * 64:(half + 1) * 64, kv * 64:(kv + 1) * 64], cmp_pe[l, 0])
                    cx.dma(PEV[half * 64:(half + 1) * 64, kv * 64:(kv + 1) * 64], cmp_pe[l, 1])
            cx.memset("dve", plf[:], 0.0)
            pk = cmp_pool[l, 0].re("(p o) -> p o", o=1); pv = cmp_pool[l, 1].re("(p o) -> p o", o=1)
            cx.dma(plf[0:64, 0:1], pk); cx.dma(plf[64:128, 1:2], pk)
            cx.dma(plf[0:64, 128:129], pv); cx.dma(plf[64:128, 129:130], pv)
            cx.cp("dve", POOLK2[:], plf[:, 0:2]); cx.cp("dve", BIGV[:], plf[:, 2:258])

        UB = cx.ar_alloc("UB", [128, 2, 30 + MT1]); GB = cx.ar_alloc("GB", [128, 6, 3 + MT1])
        CY = cx.ar_alloc("CY", [128, 2, MT1]); CY2 = cx.ar_alloc("CY2", [128, 2, MT1]); GY = cx.ar_alloc("GY", [128, 6, MT1])
        SR = cx.ar_alloc("SR", [64, 4, 64]); SGs = cx.ar_alloc("SGs", [64, 4, 64])
        KSEL = cx.ar_alloc("KSEL", [128, SEQ], BF16); VSEL = cx.ar_alloc("VSEL", [128, NT, 2, 65], BF16)
        KWIN = cx.ar_alloc("KWIN", [128, 5, 128], BF16); VWIN = cx.ar_alloc("VWIN", [128, 5, 2, 65], BF16)
        NBK = max(NBP, NBS); NVT = (NBK + 127) // 128
        KC = cx.ar_alloc("KC", [128, NBK], BF16); VC = cx.ar_alloc("VC", [128, NVT, 2, 65], BF16); VCf = cx.ar_alloc("VCf", [128, 128])
        KNEW = cx.ar_alloc("KNEW", [128, 8], BF16); VNEW = cx.ar_alloc("VNEW", [8, 2, 65], BF16)
        for t_ in (VSEL, VWIN, VC, VNEW):
            cx.memset("pool", t_[:], 1.0)
        QA = cx.ar_alloc("QA", [128, MT1], BF16); QB = cx.ar_alloc("QB", [128, MT1], BF16)
        BRT = cx.ar_alloc("BRT", [128, 4, 2, MT1], BF16)
        KVT = cx.ar_alloc("KVT", [128, 780]); RETT = cx.ar_alloc("RETT", [128, 1024]); ZAB = cx.ar_alloc("ZAB", [128, 264])
        RK = cx.ar_alloc("RK", [128, 128], BF16); RV = cx.ar_alloc("RV", [128, 128], BF16)
        Ebuf = [cx.ar_alloc("Eb%d" % k, [128, 512]) for k in range(2)]
        PTb = [cx.ar_alloc("PTb%d" % k, [128, 4, 128], BF16) for k in range(2)]
        OACC = cx.ar_alloc("OACC", [128, 3, 4, 65])
        WS = max(NBK + 1, 16)
        PCN = cx.ar_alloc("PCN", [128, 4, NBK]); SC = cx.ar_alloc("SC", [128, WS]); SC2 = cx.ar_alloc("SC2", [128, WS]); SEL = cx.ar_alloc("SEL", [128, 2, WS])
        M8 = cx.ar_alloc("M8", [128, 16]); rs4 = cx.ar_alloc("rs4", [128, 8]); GT = cx.ar_alloc("GT", [128, 12]); FF = cx.ar_alloc("FF", [128, 12])
        ONSA = cx.ar_alloc("ONSA", [128, 256]); OTMP = cx.ar_alloc("OTMP", [128, 256])
        ROT = cx.ar_alloc("ROT", [128, 64]); QR = cx.ar_alloc("QR", [128, 256]); KR = cx.ar_alloc("KR", [128, 256]); RT1 = cx.ar_alloc("RT1", [128, 128]); RT2 = cx.ar_alloc("RT2", [128, 128])
        QT = cx.ar_alloc("QT", [64, 128]); QXT = cx.ar_alloc("QXT", [64, 128]); KT = cx.ar_alloc("KT", [64, 128]); KZ = cx.ar_alloc("KZ", [128, 64]); ATT = cx.ar_alloc("ATT", [128, 128])
        ST8 = cx.ar_alloc("ST8", [128, 8]); SIL = cx.ar_alloc("SIL", [128, 256])
        GQ = cx.ar_alloc("GQ", [128, 256]); GK = cx.ar_alloc("GK", [128, 256]); GV = cx.ar_alloc("GV", [128, 256])
        GG = cx.ar_alloc("GG", [128, 4]); GBETA = cx.ar_alloc("GBETA", [128, 4]); GC = cx.ar_alloc("GC", [128, 4]); GLB = cx.ar_alloc("GLB", [128, 4])
        EGC = cx.ar_alloc("EGC", [128, 4]); EKD = cx.ar_alloc("EKD", [128, 4]); EGL = cx.ar_alloc("EGL", [128, 4]); G4 = cx.ar_alloc("G4", [128, 4]); G4b = cx.ar_alloc("G4b", [128, 4])
        KBt = cx.ar_alloc("KBt", [128, 64]); RU = cx.ar_alloc("RU", [128, 64]); KBE = cx.ar_alloc("KBE", [128, 64]); QD = cx.ar_alloc("QD", [128, 64]); KD = cx.ar_alloc("KD", [128, 64])
        gKT = cx.ar_alloc("gKT", [64, 128]); gKBT = cx.ar_alloc("gKBT", [64, 128]); gQT = cx.ar_alloc("gQT", [64, 128]); gQDT = cx.ar_alloc("gQDT", [64, 128])
        GREP = cx.ar_alloc("GREP", [128, 128]); LMU = cx.ar_alloc("LMU", [128, 128]); LML = cx.ar_alloc("LML", [128, 128]); LMT = cx.ar_alloc("LMT", [128, 128])
        GATT = cx.ar_alloc("GATT", [128, 128]); YM = cx.ar_alloc("YM", [128, 128])
        PQ = [[cx.ar_alloc("PQ%d%d" % (a, b), [128, 128]) for b in range(2)] for a in range(2)]
        USB = cx.ar_alloc("USB", [128, 64]); WT = cx.ar_alloc("WT", [64, 128]); VN = cx.ar_alloc("VN", [128, 64]); OG = cx.ar_alloc("OG", [128, 256])
        PTI0 = cx.ar_alloc("PTI0", [128, NPG], I32); PTI1 = cx.ar_alloc("PTI1", [128, NPG], I32); PTF = cx.ar_alloc("PTF", [128, NPG])
        PGS = [cx.ar_alloc("PGS%d" % k, [128, 256]) for k in range(8)]
        RKs = [RK, cx.ar_alloc("RK1", [128, 128], BF16)]; RVs = [RV, cx.ar_alloc("RV1", [128, 128], BF16)]
        KPGs = [cx.ar_alloc("KPG%d" % k, [128, 4, 128], BF16) for k in range(2)]
        VPGs = [cx.ar_alloc("VPG%d" % k, [128, 4, 2, 65], BF16) for k in range(2)]
        for t_ in VPGs:
            cx.memset("pool", t_[:], 1.0)
        LCS2 = cx.ar_alloc("LCS2", [8, 4, 128]); keeps = cx.ar_alloc("keeps", [8, 2 * (NBS + 1)]); cx.dma(keeps[:], c_keeps[0:8, :])
        WINT = cx.ar_alloc("WINT", [128, 256])
        for h in range(4):
            cx.ts("dve", LCS2[0:8, h, :], ones_f[0:8, 0:128], c31[0:8, h:h + 1], ALU.mult)
            cx.cp("dve", LCS2[0:8, h, 126:128], LC[0:8, h, J0 - 2:J0])
        eb = {"i": 0}

        def proj_fm(col0, ncols, N):
            p = ptmp()
            for kc in range(8):
                cx.mm(p[0:ncols, 0:N], W1[:, kc, col0:col0 + ncols], hT[:, kc, 0:N], start=(kc == 0), stop=(kc == 7))
            return p[0:ncols, 0:N]

        def proj_tm(col0, ncols, c0, T):
            p = ptmp()
            for kc in range(8):
                cx.mm(p[0:T, 0:ncols], hT[:, kc, c0:c0 + T], W1[:, kc, col0:col0 + ncols], start=(kc == 0), stop=(kc == 7))
            return p[0:T, 0:ncols]

        def hprt(h):
            return (0, 64) if h < 2 else (64, 128)

        def attend(T, h, br, qv, tiles, near_bt=None):
            ntot = sum(t["nk"] for t in tiles)
            p = ptmp(); off = 0
            for t in tiles:
                cx.mm(p[0:T, off:off + t["nk"]], qv, t["kT"]); off += t["nk"]
            eb["i"] ^= 1
            E = Ebuf[eb["i"]]; PT = PTb[eb["i"]]
            if near_bt is None:
                cx.act(E[0:T, 0:ntot], p[0:T, 0:ntot], AF.Exp, bias=c31[0:T, h:h + 1])
            else:
                cx.tt("dve", E[0:T, 0:ntot], p[0:T, 0:ntot], near_bt, ALU.add)
                cx.act(E[0:T, 0:ntot], E[0:T, 0:ntot], AF.Exp)
            off = 0
            for t in tiles:
                if t.get("mask") is not None:
                    ev_ = E[0:T, off:off + t["nk"]]
                    if t.get("mre"):
                        ev_ = ev_.re("p (b k) -> p b k", k=t["mre"])
                    cx.tt("pool", ev_, ev_, t["mask"], ALU.mult)
                off += t["nk"]
            p2 = ptmp(); off = 0
            for k, t in enumerate(tiles):
                cx.tr(p2[0:t["nk"], k * T:(k + 1) * T], E[0:T, off:off + t["nk"]], ident[0:T, 0:T]); off += t["nk"]
            nt = len(tiles)
            evac(PT[:, 0:nt, 0:T], p2[:, 0:nt * T].re("p (k t) -> p k t", t=T))
            p3 = ptmp()
            for k, t in enumerate(tiles):
                cx.mm(p3[0:T, 0:65], PT[0:t["nk"], k, 0:T], t["v"], start=(k == 0), stop=(k == nt - 1))
            cx.tt("dve", OACC[0:T, br, h, :], OACC[0:T, br, h, :], p3[0:T, 0:65], ALU.add)
            return E

        def nsa_select(T, nbv, W, keepv, addv):
            for kv in range(2):
                cx.memset("pool", SC[0:T, 0:W], 0.0)
                cx.tt("dve", SC[0:T, 0:nbv], PCN[0:T, 2 * kv, 0:nbv], PCN[0:T, 2 * kv + 1, 0:nbv], ALU.add)
                cx.tt("dve", SC[0:T, 0:W], SC[0:T, 0:W], keepv, ALU.mult)
                cx.tt("dve", SC[0:T, 0:W], SC[0:T, 0:W], addv, ALU.add)
                cx.memset("dve", SC[0:T, 0:1], 2.0)
                cx.max8(M8[0:T, 0:8], SC[0:T, 0:W])
                cx.match_replace(SC2[0:T, 0:W], M8[0:T, 0:8], SC[0:T, 0:W], -5.0)
                cx.max8(M8[0:T, 8:16], SC2[0:T, 0:W])
                cx.ts("dve", SEL[0:T, kv, 0:W], SC[0:T, 0:W], M8[0:T, 15:16], ALU.is_ge)
                cx.stt(SEL[0:T, kv, 0:W], SC[0:T, 0:W], -0.5, SEL[0:T, kv, 0:W], ALU.is_gt, ALU.mult)

        def nsa_compressed(T, c0, nbv, lcv):
            for h in range(4):
                a, b = hprt(h); kv = h // 2
                qv = (QA if h % 2 == 0 else QB)[a:b, c0:c0 + T]
                for t0 in range(0, nbv, 128):
                    nk = min(128, nbv - t0)
                    tl = [dict(kT=KC[a:b, t0:t0 + nk], v=VC[0:nk, t0 // 128, kv, :], nk=nk)]
                    E = attend(T, h, 0, qv, tl, near_bt=lcv(h, t0, nk))
                    cx.cp("pool", PCN[0:T, h, t0:t0 + nk], E[0:T, 0:nk])
                cx.red(rs4[0:T, h:h + 1], PCN[0:T, h, 0:nbv])
                cx.ts("dve", rs4[0:T, h:h + 1], rs4[0:T, h:h + 1], 1e-30, ALU.max)
                cx.op("dve", [rs4[0:T, h:h + 1]], [rs4[0:T, 4 + h:5 + h]],
                      lambda g, hh=h: g.reciprocal(out=rs4[0:T, 4 + hh:5 + hh].ap, in_=rs4[0:T, hh:hh + 1].ap))
                cx.ts("dve", PCN[0:T, h, 0:nbv], PCN[0:T, h, 0:nbv], rs4[0:T, 4 + h:5 + h], ALU.mult)

        def nsa_combine(T, c0):
            cx.act(GT[0:T, :], KVT[0:T, 768:780], AF.Sigmoid)
            den = OACC[0:T, :, :, 64]
            cx.ts("dve", FF[0:T, :].re("p (b h) -> p b h", h=4), den, 1e-30, ALU.max)
            cx.op("dve", [FF[0:T, :]], [FF[0:T, :]], lambda g: g.reciprocal(out=FF[0:T, :].ap, in_=FF[0:T, :].ap))
            cx.tt("dve", FF[0:T, :], FF[0:T, :], GT[0:T, :], ALU.mult)
            o3 = ONSA[0:T, :].re("p (h e) -> p h e", e=64); t3 = OTMP[0:T, :].re("p (h e) -> p h e", e=64)
            for br in range(3):
                f = FF[0:T, br * 4:(br + 1) * 4].unsq(2).bc([T, 4, 64])
                if br == 0:
                    cx.tt("dve", o3, OACC[0:T, br, :, 0:64], f, ALU.mult)
                else:
                    cx.tt("pool", t3, OACC[0:T, br, :, 0:64], f, ALU.mult)
                    cx.tt("pool", o3, o3, t3, ALU.add)
            for c in range(2):
                transpose_to(BRT[:, 0, c, c0:c0 + T], ONSA[0:T, c * 128:(c + 1) * 128], T, 128)

        def nsa_prompt_tile(l, i, c0):
            T = 128; p0 = i * 128
            cx.cp("pool", VSEL[0:T, i, :, 0:64], KVT[0:T, 384:512].re("p (k d) -> p k d", d=64))
            cx.cp("pool", VWIN[0:T, i % 5, :, 0:64], KVT[0:T, 640:768].re("p (k d) -> p k d", d=64))
            cx.tt("pool", RK[0:T, :], KVT[0:T, 0:128], PEK[0:T, :], ALU.add)
            cx.tt("pool", RV[0:T, :], KVT[0:T, 128:256], PEV[0:T, :], ALU.add)
            p = ptmp(); cx.mm(p[:, 0:2], RK[:, :], POOLK2[:, :]); cx.cp("dve", KC[:, 2 * i:2 * i + 2], p[:, 0:2])
            p = ptmp(); cx.mm(p[:, 0:128], BIGV[:, 126 - 2 * i:254 - 2 * i], RV[:, :])
            cx.tt("dve", VCf[:], VCf[:], p[:, 0:128], ALU.add)
            cx.cp("pool", VC[:, 0, :, 0:64], VCf[:].re("p (k d) -> p k d", d=64))
            cx.memset("pool", OACC[0:T], 0.0)
            nbv = 2 * i + 2; W = max(nbv, 16); jc = J0 - 2 * i
            yield
            nsa_compressed(T, c0, nbv, lambda h, t0, nk: LC[0:T, h, jc + t0:jc + t0 + nk])
            yield
            nsa_select(T, nbv, W, keepadd[0:T, jc:jc + W], keepadd[0:T, WLC + jc:WLC + jc + W])
            yield
            for h in range(4):
                a, b = hprt(h); kv = h // 2
                qv = (QA if h % 2 == 0 else QB)[a:b, c0:c0 + T]
                far = [dict(kT=KSEL[a:b, kt * 128:(kt + 1) * 128], v=VSEL[:, kt, kv, :], nk=128,
                            mask=SEL[0:T, kv, 2 * kt:2 * kt + 2].unsq(2).bc([T, 2, 64]), mre=64) for kt in range(0, i - 1)]
                for g0 in range(0, len(far), 4):
                    attend(T, h, 1, qv, far[g0:g0 + 4])
                    yield
                kts = [kt for kt in (i - 1, i) if kt >= 0]
                near = [dict(kT=KSEL[a:b, kt * 128:(kt + 1) * 128], v=VSEL[:, kt, kv, :], nk=128,
                             mask=SEL[0:T, kv, 2 * kt:2 * kt + 2].unsq(2).bc([T, 2, 64]), mre=64) for kt in kts]
                bt = BT[0:T, h, 256 - 128 * len(kts):256]
                attend(T, h, 1, qv, near, near_bt=bt)
                yield
                farw = [dict(kT=KWIN[a:b, kt % 5, :], v=VWIN[:, kt % 5, kv, :], nk=128,
                             mask=(wm4[0:T, :] if kt == i - 4 else None)) for kt in (i - 4, i - 3, i - 2) if kt >= 0]
                if farw:
                    attend(T, h, 2, qv, farw)
                    yield
                nearw = [dict(kT=KWIN[a:b, kt % 5, :], v=VWIN[:, kt % 5, kv, :], nk=128) for kt in kts]
                attend(T, h, 2, qv, nearw, near_bt=bt)
                yield
            nsa_combine(T, c0)

        def sample_prep(l, s):
            cx.dma(PTI0[:], ptab[s].pbc(128))
            cx.cp("dve", PTF[:], PTI0[:])
            cx.ts("dve", PTF[:], PTF[:], float(DEPTH * 128), ALU.mult, float(l * 128), ALU.add)
            cx.ts("dve", PTF[:], PTF[:], iota[:, 0:1], ALU.add)
            cx.ts("dve", PTF[:], PTF[:], 2.0, ALU.mult)
            cx.cp("dve", PTI0[:], PTF[:])
            cx.ts("dve", PTF[:], PTF[:], 1.0, ALU.add)
            cx.cp("dve", PTI1[:], PTF[:])
            PF = 6

            def issue1(pg):
                cx.dma(PGS[pg % 8][:], pool_d[:, :], q="pool", indirect=PTI0[:, pg:pg + 1])
            for pg in range(min(PF, NPG)):
                issue1(pg)
            pv = None
            for pg in range(NPG):
                if pg + PF < NPG:
                    issue1(pg + PF)
                buf = PGS[pg % 8]; rk = RKs[pg % 2]; rv = RVs[pg % 2]
                cx.tt("dve", rk[:, :], buf[:, 0:128], PEK[:, :], ALU.add)
                cx.tt("dve", rv[:, :], buf[:, 128:256], PEV[:, :], ALU.add)
                p = ptmp(); cx.mm(p[:, 0:2], rk[:, :], POOLK2[:, :]); cx.cp("act", KC[:, 2 * pg:2 * pg + 2], p[:, 0:2])
                r = pg % 64
                if r == 0:
                    pv = pacc()
                cx.mm(pv[:, 0:128], BIGV[:, 126 - 2 * r:254 - 2 * r], rv[:, :], start=(r == 0), stop=(r == 63 or pg == NPG - 1))
                if r == 63 or pg == NPG - 1:
                    cx.cp("act", VC[:, pg // 64, :, 0:64], pv[:, 0:128].re("p (k d) -> p k d", d=64))
            for kt in range(4):
                cx.dma(WINT[:], win_in[l, s, kt * 128:(kt + 1) * 128, :])
                transpose_to(KWIN[:, kt, :], WINT[:, 0:128], 128, 128)
                cx.cp("pool", VWIN[:, kt, :, 0:64], WINT[:, 128:256].re("p (k d) -> p k d", d=64))
            cx.dma(wins[l, s, 0:512 - TS, :], win_in[l, s, TS:512, :])

        def nsa_sample_tile(l, s, c0):
            T = TS
            cx.cp("pool", VNEW[0:T, :, 0:64], KVT[0:T, 384:512].re("p (k d) -> p k d", d=64))
            cx.cp("pool", VWIN[0:T, 4, :, 0:64], KVT[0:T, 640:768].re("p (k d) -> p k d", d=64))
            cx.memset("pool", OACC[0:T], 0.0)
            nbv = NBS; W = NBS + 1
            nsa_compressed(T, c0, nbv, lambda h, t0, nk: (LCS2[0:T, h, 128 - nk:128] if t0 + nk == nbv else None))
            nsa_select(T, nbv, W, keeps[0:T, 0:W], keeps[0:T, W:2 * W])
            qv = lambda h: (QA if h % 2 == 0 else QB)[hprt(h)[0]:hprt(h)[1], c0:c0 + T]
            def gath(g0):
                for k in range(min(4, NPG - g0)):
                    cx.dma(PGS[(g0 + k) % 8][:], pool_d[:, :], q="pool", indirect=PTI1[:, g0 + k:g0 + k + 1])

            def prep(g0):
                bsel = (g0 // 4) % 2
                npg_ = min(4, NPG - g0)
                for k in range(npg_):
                    buf = PGS[(g0 + k) % 8]
                    transpose_to(KPGs[bsel][:, k, :], buf[:, 0:128], 128, 128)
                    cx.cp("pool", VPGs[bsel][:, k, :, 0:64], buf[:, 128:256].re("p (k d) -> p k d", d=64))
            gath(0)
            prep(0)
            yield
            for g0 in range(0, NPG, 4):
                npg = min(4, NPG - g0)
                KPG = KPGs[(g0 // 4) % 2]; VPG = VPGs[(g0 // 4) % 2]
                if g0 + 4 < NPG:
                    gath(g0 + 4)
                for h in range(4):
                    a, b = hprt(h); kv = h // 2
                    tiles = [dict(kT=KPG[a:b, k, :], v=VPG[:, k, kv, :], nk=128,
                                  mask=SEL[0:T, kv, 2 * (g0 + k):2 * (g0 + k) + 2].unsq(2).bc([T, 2, 64]), mre=64) for k in range(npg)]
                    if g0 + npg == NPG:
                        last = tiles[-1]; tiles = tiles[:-1]
                        newt = dict(kT=KNEW[a:b, 0:T], v=VNEW[0:T, kv, :], nk=T, mask=SEL[0:T, kv, NBS:NBS + 1].bc([T, T]))
                        if tiles:
                            attend(T, h, 1, qv(h), tiles)
                        attend(T, h, 1, qv(h), [last, newt], near_bt=BT[0:T, h, 0:128 + T])
                    else:
                        attend(T, h, 1, qv(h), tiles)
                    yield
                if g0 + 4 < NPG:
                    prep(g0 + 4)
            for h in range(4):
                a, b = hprt(h); kv = h // 2
                farw = [dict(kT=KWIN[a:b, kt, :], v=VWIN[:, kt, kv, :], nk=128, mask=(wm4[0:T, :] if kt == 0 else None)) for kt in range(3)]
                attend(T, h, 2, qv(h), farw)
                nearw = [dict(kT=KWIN[a:b, 3, :], v=VWIN[:, 3, kv, :], nk=128), dict(kT=KWIN[a:b, 4, 0:T], v=VWIN[0:T, 4, kv, :], nk=T)]
                attend(T, h, 2, qv(h), nearw, near_bt=BT[0:T, h, 0:128 + T])
                yield
            nsa_combine(T, c0)

        def conformer_mt(N):
            for c in range(2):
                y = CY[:, c, 0:N]
                cx.ts("dve", y, UB[:, c, 0:N], convw[:, c, 0:1], ALU.mult)
                for k in range(1, 31):
                    cx.stt(y, UB[:, c, k:k + N], convw[:, c, k:k + 1], y, ALU.mult, ALU.add)
                    if k % 6 == 0:
                        yield
                cx.ts("dve", y, y, cvp[:, c:c + 1], ALU.add)
                cx.tt("pool", CY2[:, c, 0:N], y, y, ALU.mult)
            pm = ptmp()
            for c in range(2):
                cx.mm(pm[:, 0:N], ones_f[:, :], CY[:, c, 0:N], start=(c == 0), stop=(c == 1))
            pq = ptmp()
            for c in range(2):
                cx.mm(pq[:, 0:N], ones_f[:, :], CY2[:, c, 0:N], start=(c == 0), stop=(c == 1))
            cx.ts("dve", g1[:, 0:N], pm[:, 0:N], 1.0 / 256, ALU.mult)
            cx.tt("dve", g2[:, 0:N], g1[:, 0:N], g1[:, 0:N], ALU.mult)
            cx.stt(g2[:, 0:N], pq[:, 0:N], 1.0 / 256, g2[:, 0:N], ALU.mult, ALU.subtract)
            cx.ts("dve", g2[:, 0:N], g2[:, 0:N], 0.0, ALU.max, EPS, ALU.add)
            cx.rsqrt(g2[:, 0:N], g2[:, 0:N])
            for c in range(2):
                cx.tt("pool", g3[:, 0:N], CY[:, c, 0:N], g1[:, 0:N], ALU.subtract)
                cx.tt("pool", g3[:, 0:N], g3[:, 0:N], g2[:, 0:N], ALU.mult)
                cx.ts("dve", g3[:, 0:N], g3[:, 0:N], cvp[:, 2 + c:3 + c], ALU.mult, cvp[:, 4 + c:5 + c], ALU.add)
                cx.act(BRT[:, 1, c, 0:N], g3[:, 0:N], AF.Silu)

        def gdn_conv_mt(N):
            for c in range(6):
                y = GY[:, c, 0:N]
                cx.ts("dve", y, GB[:, c, 0:N], gdw[:, c, 0:1], ALU.mult)
                for k in range(1, 4):
                    cx.stt(y, GB[:, c, k:k + N], gdw[:, c, k:k + 1], y, ALU.mult, ALU.add)
                cx.act(y, y, AF.Silu)
                yield

        def bc4(v, T):
            return v.unsq(2).bc([T, 4, 64])

        def h3(t, T):
            return t[0:T, :].re("p (h e) -> p h e", e=64)

        def retention_tile(grp, pos, c0):
            T = grp["T"]; g = grp["gi"]
            src = c_rot_p[pos:pos + T, :] if grp["kind"] == "p" else c_rot_s[0:T, :]
            cx.dma(ROT[0:T, :], src)
            cosb = ROT[0:T, 0:32].unsq(1).bc([T, 4, 32]); sinb = ROT[0:T, 32:64].unsq(1).bc([T, 4, 32])
            ta = RT1[0:T, :].re("p (h f) -> p h f", f=32); tb = RT2[0:T, :].re("p (h f) -> p h f", f=32)
            for (off, dst) in ((0, QR), (256, KR)):
                s4 = RETT[0:T, off:off + 256].re("p (h w f) -> p h w f", h=4, w=2)
                d4 = dst[0:T, :].re("p (h w f) -> p h w f", h=4, w=2)
                x1 = s4[:, :, 0, :]; x2 = s4[:, :, 1, :]
                cx.tt("pool", d4[:, :, 0, :], x1, cosb, ALU.mult); cx.tt("pool", ta, x2, sinb, ALU.mult)
                cx.tt("pool", d4[:, :, 0, :], d4[:, :, 0, :], ta, ALU.subtract)
                cx.tt("dve", d4[:, :, 1, :], x1, sinb, ALU.mult); cx.tt("dve", tb, x2, cosb, ALU.mult)
                cx.tt("dve", d4[:, :, 1, :], d4[:, :, 1, :], tb, ALU.add)
            if T == 128:
                chk(7.41)
            po = pacc()
            for h in range(4):
                qh = QR[0:T, h * 64:(h + 1) * 64]; kh = KR[0:T, h * 64:(h + 1) * 64]; vh = RETT[0:T, 512 + h * 64:512 + (h + 1) * 64]
                p = ptmp(); cx.tr(p[0:64, 0:T], qh, ident[0:T, 0:T])
                cx.cp("act", QT[0:64, 0:T], p[0:64, 0:T]); cx.tt("dve", QXT[0:64, 0:T], QT[0:64, 0:T], retxig[g][0:64, h, 0:T], ALU.mult)
                p = ptmp(); cx.tr(p[0:64, 0:T], kh, ident[0:T, 0:T]); cx.act(KT[0:64, 0:T], p[0:64, 0:T], AF.Identity, scale=0.125)
                cx.ts("pool", KZ[0:T, :], kh, retzg[g][0:T, h:h + 1], ALU.mult)
                if T == 128:
                    chk(7.42)
                p = ptmp(); cx.mm(p[0:T, 0:T], KT[0:64, 0:T], QT[0:64, 0:T])
                cx.tt("dve", ATT[0:T, 0:T], p[0:T, 0:T], retDg[g][0:T, h, 0:T], ALU.mult)
                if T == 128:
                    chk(7.43)
                cx.mm(po[0:T, h * 64:(h + 1) * 64], ATT[0:T, 0:T], vh, start=True, stop=False)
                cx.mm(po[0:T, h * 64:(h + 1) * 64], QXT[0:64, 0:T], SR[0:64, h, :], start=False, stop=True)
                if T == 128:
                    chk(7.44)
                pS = ptmp(); cx.mm(pS[0:64, 0:64], KZ[0:T, 0:64], vh)
                cx.stt(SR[0:64, h, :], SR[0:64, h, :], float(GAM[h] ** T), pS[0:64, 0:64], ALU.mult, ALU.add)
                yield
            cx.cp("act", OG[0:T, :], po[0:T, 0:256])
            cx.red(ST8[0:T, 0:4], h3(OG, T))
            cx.tt("pool", OTMP[0:T, :], OG[0:T, :], OG[0:T, :], ALU.mult)
            cx.red(ST8[0:T, 4:8], h3(OTMP, T))
            cx.ts("dve", ST8[0:T, 0:8], ST8[0:T, 0:8], 1.0 / 64, ALU.mult)
            cx.tt("dve", G4[0:T, :], ST8[0:T, 0:4], ST8[0:T, 0:4], ALU.mult)
            cx.tt("dve", G4[0:T, :], ST8[0:T, 4:8], G4[0:T, :], ALU.subtract)
            cx.ts("dve", G4[0:T, :], G4[0:T, :], 0.0, ALU.max, EPS, ALU.add)
            cx.rsqrt(G4[0:T, :], G4[0:T, :])
            cx.tt("dve", h3(OG, T), h3(OG, T), bc4(ST8[0:T, 0:4], T), ALU.subtract)
            cx.tt("dve", h3(OG, T), h3(OG, T), bc4(G4[0:T, :], T), ALU.mult)
            cx.tt("pool", OG[0:T, :], OG[0:T, :], gnb[0:T, :], ALU.mult)
            cx.act(SIL[0:T, :], RETT[0:T, 768:1024], AF.Silu)
            cx.tt("pool", OG[0:T, :], OG[0:T, :], SIL[0:T, :], ALU.mult)
            for c in range(2):
                transpose_to(BRT[:, 2, c, c0:c0 + T], OG[0:T, c * 128:(c + 1) * 128], T, 128)

        def gdn_tile(grp, c0):
            T = grp["T"]; g = grp["gi"]
            for part, dst in enumerate((GQ, GK, GV)):
                for h in range(4):
                    blk = part * 2 + h // 2; a = (h % 2) * 64
                    p = ptmp(); cx.tr(p[0:T, 0:64], GY[a:a + 64, blk, c0:c0 + T], ident[a:a + 64, a:a + 64])
                    evac(dst[0:T, h * 64:(h + 1) * 64], p[0:T, 0:64])
            for X, sc in ((GQ, 0.125), (GK, 1.0)):
                cx.tt("pool", OTMP[0:T, :], X[0:T, :], X[0:T, :], ALU.mult)
                cx.red(G4[0:T, :], h3(OTMP, T))
                cx.ts("dve", G4[0:T, :], G4[0:T, :], 0.0, ALU.max, EPS, ALU.add)
                cx.rsqrt(G4[0:T, :], G4[0:T, :])
                if sc != 1.0:
                    cx.ts("dve", G4[0:T, :], G4[0:T, :], sc, ALU.mult)
                cx.tt("dve", h3(X, T), h3(X, T), bc4(G4[0:T, :], T), ALU.mult)
            cx.tt("dve", G4[0:T, :], ZAB[0:T, 256:260], dtb[0:T, :], ALU.add)
            cx.ts("dve", G4b[0:T, :], G4[0:T, :], -1.0, ALU.mult)
            cx.tt("dve", G4b[0:T, :], G4b[0:T, :], G4[0:T, :], ALU.max)
            cx.act(G4b[0:T, :], G4b[0:T, :], AF.Exp, scale=-1.0)
            cx.act(G4b[0:T, :], G4b[0:T, :], AF.Ln, bias=ones_f[0:T, 0:1])
            cx.ts("dve", G4[0:T, :], G4[0:T, :], 0.0, ALU.max)
            cx.tt("dve", G4[0:T, :], G4[0:T, :], G4b[0:T, :], ALU.add)
            cx.tt("dve", GG[0:T, :], G4[0:T, :], negA[0:T, :], ALU.mult)
            cx.act(GBETA[0:T, :], ZAB[0:T, 260:264], AF.Sigmoid)
            p = ptmp(); cx.mm(p[0:T, 0:4], Umat[0:T, 0:T], GG[0:T, 0:4]); cx.cp("dve", GC[0:T, :], p[0:T, 0:4])
            p = ptmp(); cx.mm(p[0:128, 0:4], lastmg[g][0:T, :], GC[0:T, 0:4]); cx.cp("dve", GLB[:, :], p[0:128, 0:4])
            cx.act(EGC[0:T, :], GC[0:T, :], AF.Exp)
            cx.tt("dve", EKD[0:T, :], GLB[0:T, :], GC[0:T, :], ALU.subtract); cx.act(EKD[0:T, :], EKD[0:T, :], AF.Exp)
            cx.act(EGL[:, :], GLB[:, :], AF.Exp)
            po = pacc()
            nlev = int(round(math.log2(T)))
            for h in range(4):
                hs = slice(h * 64, (h + 1) * 64)
                cx.ts("dve", KBt[0:T, :], GK[0:T, hs], GBETA[0:T, h:h + 1], ALU.mult)
                cx.ts("pool", RU[0:T, :], GV[0:T, hs], GBETA[0:T, h:h + 1], ALU.mult)
                cx.ts("dve", KBE[0:T, :], KBt[0:T, :], EGC[0:T, h:h + 1], ALU.mult)
                cx.ts("pool", QD[0:T, :], GQ[0:T, hs], EGC[0:T, h:h + 1], ALU.mult)
                cx.ts("pool", KD[0:T, :], GK[0:T, hs], EKD[0:T, h:h + 1], ALU.mult)
                for srcv, dstv in ((GK[0:T, hs], gKT), (KBt[0:T, :], gKBT), (GQ[0:T, hs], gQT), (QD[0:T, :], gQDT)):
                    transpose_to(dstv[0:64, 0:T], srcv, T, 64)
                cx.ts("pool", GREP[0:T, 0:T], ones_f[0:T, 0:T], GG[0:T, h:h + 1], ALU.mult)
                pR = ptmp(); cx.mm(pR[0:T, 0:T], GREP[0:T, 0:T], Umat[0:T, 0:T])
                cx.ts("dve", LMU[0:T, 0:T], pR[0:T, 0:T], GC[0:T, h:h + 1], ALU.subtract, 0.0, ALU.min)
                cx.act(LMU[0:T, 0:T], LMU[0:T, 0:T], AF.Exp)
                cx.ts("dve", LML[0:T, 0:T], pR[0:T, 0:T], GC[0:T, h:h + 1], ALU.subtract, 0.0, ALU.max)
                cx.act(LML[0:T, 0:T], LML[0:T, 0:T], AF.Exp, scale=-1.0)
                cx.tt("pool", LMT[0:T, 0:T], LMU[0:T, 0:T], triu[0:T, 0:T], ALU.mult)
                cx.tt("pool", LMU[0:T, 0:T], LMU[0:T, 0:T], triu[0:T, 128:128 + T], ALU.mult)
                cx.tt("pool", LML[0:T, 0:T], LML[0:T, 0:T], tril[0:T, 0:T], ALU.mult)
                p = ptmp(); cx.mm(p[0:T, 0:T], gKT[0:64, 0:T], gQT[0:64, 0:T]); cx.tt("dve", GATT[0:T, 0:T], p[0:T, 0:T], LMT[0:T, 0:T], ALU.mult)
                P_, Q_ = PQ[0]
                p = ptmp(); cx.mm(p[0:T, 0:T], gKT[0:64, 0:T], gKBT[0:64, 0:T]); cx.tt("dve", P_[0:T, 0:T], p[0:T, 0:T], LMU[0:T, 0:T], ALU.mult)
                p = ptmp(); cx.mm(p[0:T, 0:T], gKBT[0:64, 0:T], gKT[0:64, 0:T]); cx.tt("dve", Q_[0:T, 0:T], p[0:T, 0:T], LML[0:T, 0:T], ALU.mult)
                cx.tt("pool", YM[0:T, 0:T], ident[0:T, 0:T], P_[0:T, 0:T], ALU.subtract)
                yield
                cur = 0
                for k in range(1, nlev):
                    P2, Q2 = PQ[1 - cur]
                    pP = ptmp(); cx.mm(pP[0:T, 0:T], Q_[0:T, 0:T], P_[0:T, 0:T]); cx.cp("dve", P2[0:T, 0:T], pP[0:T, 0:T])
                    pQ = ptmp(); cx.mm(pQ[0:T, 0:T], P_[0:T, 0:T], Q_[0:T, 0:T]); cx.cp("act", Q2[0:T, 0:T], pQ[0:T, 0:T])
                    pY = ptmp(); cx.mm(pY[0:T, 0:T], Q2[0:T, 0:T], YM[0:T, 0:T]); cx.tt("dve", YM[0:T, 0:T], YM[0:T, 0:T], pY[0:T, 0:T], ALU.add)
                    P_, Q_ = P2, Q2; cur = 1 - cur
                    yield
                pu = ptmp(); cx.mm(pu[0:T, 0:64], YM[0:T, 0:T], RU[0:T, 0:64]); cx.cp("act", USB[0:T, :], pu[0:T, 0:64])
                pw = ptmp(); cx.mm(pw[0:64, 0:T], KBE[0:T, 0:64], YM[0:T, 0:T]); cx.cp("dve", WT[0:64, 0:T], pw[0:64, 0:T])
                pv = ptmp(); cx.mm(pv[0:T, 0:64], WT[0:64, 0:T], SGs[0:64, h, :]); cx.tt("dve", VN[0:T, :], USB[0:T, :], pv[0:T, 0:64], ALU.subtract)
                cx.mm(po[0:T, hs], gQDT[0:64, 0:T], SGs[0:64, h, :], start=True, stop=False)
                cx.mm(po[0:T, hs], GATT[0:T, 0:T], VN[0:T, 0:64], start=False, stop=True)
                pS = ptmp(); cx.mm(pS[0:64, 0:64], KD[0:T, 0:64], VN[0:T, 0:64])
                cx.stt(SGs[0:64, h, :], SGs[0:64, h, :], EGL[0:64, h:h + 1], pS[0:64, 0:64], ALU.mult, ALU.add)
                yield
            cx.cp("act", OG[0:T, :], po[0:T, 0:256])
            cx.tt("pool", OTMP[0:T, :], OG[0:T, :], OG[0:T, :], ALU.mult)
            cx.red(G4[0:T, :], h3(OTMP, T))
            cx.ts("dve", G4[0:T, :], G4[0:T, :], 1.0 / 64, ALU.mult, EPS, ALU.add)
            cx.rsqrt(G4[0:T, :], G4[0:T, :])
            cx.tt("dve", h3(OG, T), h3(OG, T), bc4(G4[0:T, :], T), ALU.mult)
            cx.tt("pool", h3(OG, T), h3(OG, T), gnorm[0:T, :].unsq(1).bc([T, 4, 64]), ALU.mult)
            cx.act(SIL[0:T, :], ZAB[0:T, 0:256], AF.Silu)
            cx.tt("pool", OG[0:T, :], OG[0:T, :], SIL[0:T, :], ALU.mult)
            for c in range(2):
                transpose_to(BRT[:, 3, c, c0:c0 + T], OG[0:T, c * 128:(c + 1) * 128], T, 128)

        wv = w_in[l].re("(k p) n -> p k n", p=128)
        for j, hh in enumerate((0, 2, 1, 3)):
            cx.dma(W1[:, :, j * 64:(j + 1) * 64], wv[:, :, hh * 64:(hh + 1) * 64], q="pool")
        for c0_ in range(256, O_MG, 2048):
            c1_ = min(O_MG, c0_ + 2048)
            cx.dma(W1[:, :, c0_:c1_], wv[:, :, c0_:c1_], q="pool")
        load_layer_params(l)
        chk(3)
        for gi, grp in enumerate(groups):
            T = grp["T"]; s = grp["s"]; isp = grp["kind"] == "p"
            if isp:
                chk(6)
                cx.memset("pool", UB[:, :, 0:30], 0.0); cx.memset("pool", GB[:, :, 0:3], 0.0)
                cx.memset("pool", SR[:], 0.0); cx.memset("pool", SGs[:], 0.0); cx.memset("pool", VCf[:], 0.0)
            else:
                for c in range(2):
                    load_T(UB[:, c, 0:30], st_conv[l, s, :, c * 128:(c + 1) * 128], 30)
                for c in range(6):
                    load_T(GB[:, c, 0:3], st_gconv[l, s, :, c * 128:(c + 1) * 128], 3)
                cx.dma(SR[:], st_ret[l, s].re("h d e -> d h e")); cx.dma(SGs[:], st_gdn[l, s].re("h d e -> d h e"))
                sample_prep(l, s)
                chk(4)
            mlist = mts(grp, MT1 if isp else T)
            for (t0, N) in mlist:
                a0 = grp["tok0"] + t0
                load_x(l, grp, t0, N)
                if l == 0:
                    store_x(grp, t0, N, final=False)
                mod_norm(gi, N, 0, 1)
                for blk, dst in ((0, QA), (1, QB)):
                    cx.act(dst[:, 0:N], proj_fm(blk * 128, 128, N), AF.Identity, scale=0.125)
                pk_ = proj_fm(O_KV + 256, 128, N)
                if isp:
                    evac(KSEL[:, t0:t0 + N], pk_)
                else:
                    evac(KNEW[:, 0:N], pk_)
                pk_ = proj_fm(O_KV + 512, 128, N)
                slot = (t0 // 128) % 5 if isp else 4
                evac(KWIN[:, slot, 0:T], pk_)
                for c in range(2):
                    pg_ = proj_fm(O_CV + 256 + c * 128, 128, N)
                    cx.act(g1[:, 0:N], pg_, AF.Sigmoid)
                    pa_ = proj_fm(O_CV + c * 128, 128, N)
                    cx.tt("dve", UB[:, c, 30:30 + N], pa_, g1[:, 0:N], ALU.mult)
                for c in range(6):
                    evac(GB[:, c, 3:3 + N], proj_fm(O_GDN + c * 128, 128, N))
                if isp:
                    chk(7.1)
                pos = t0; j = 0
                evac(KVT[0:T, 0:512], proj_tm(O_KV, 512, j, T))
                evac(KVT[0:T, 512:780], proj_tm(O_KV + 512, 268, j, T))
                evac(RETT[0:T, 0:512], proj_tm(O_RET, 512, j, T))
                evac(RETT[0:T, 512:1024], proj_tm(O_RET + 512, 512, j, T))
                evac(ZAB[0:T, :], proj_tm(O_GDN + 768, 264, j, T))
                def gdn_chain():
                    yield from gdn_conv_mt(N)
                    yield from gdn_tile(grp, j)
                if isp:
                    cx.dma(kvp[l, pos:pos + T, :], KVT[0:T, 0:512])
                    if pos >= SEQ - 512:
                        cx.dma(winp[l, pos - (SEQ - 512):pos - (SEQ - 512) + T, :], KVT[0:T, 512:768])
                    run_gens([nsa_prompt_tile(l, pos // 128, j), gdn_chain(), retention_tile(grp, pos, j), conformer_mt(N)])
                else:
                    cx.dma(kvs[s, l, :, :], KVT[0:T, 0:512])
                    cx.dma(wins[l, s, 512 - TS:512, :], KVT[0:T, 512:768])
                    run_gens([nsa_sample_tile(l, s, j), gdn_chain(), retention_tile(grp, pos, j), conformer_mt(N)])
                cx.dma(brd[:, :, a0:a0 + N].re("c p n -> p c n"), BRT[:].re("p n c t -> p (n c) t")[:, :, 0:N])
                last = (t0 + N >= T * grp["ntile"])
                if last:
                    for c in range(2):
                        transpose_to(stg[0:30, :], UB[:, c, N:N + 30], 128, 30)
                        dst = convp[l] if isp else convs[l, s]
                        cx.dma(dst[:, c * 128:(c + 1) * 128], stg[0:30, :])
                    for c in range(6):
                        transpose_to(stg[0:3, :], GB[:, c, N:N + 3], 128, 3)
                        dst = gcp[l] if isp else gcs[l, s]
                        cx.dma(dst[:, c * 128:(c + 1) * 128], stg[0:3, :])
                    cx.dma((retp[l] if isp else rets[l, s]).re("h d e -> d h e"), SR[:])
                    cx.dma((gdnp[l] if isp else gdns[l, s]).re("h d e -> d h e"), SGs[:])
                else:
                    cx.cp("pool", UB[:, :, 0:30], UB[:, :, N:N + 30])
                    cx.cp("pool", GB[:, :, 0:3], GB[:, :, N:N + 3])
                chk(5 if not isp else 7)

    try:
        chk(0)
        build_tables()
        chk(1)
        cx.ar_reset()
        load_cT()
        for l in range(DEPTH):
            compute_mod(l)
            chk(2)
            phase1a(l)
            chk(8)
            phase1b(l)
            chk(9)
            phase2(l)
            chk(10)
    except StopBuild:
        pass
    cx.finish()
    return nc


def make_consts(cfg):
    SEQ, TS, PAST, NBP = cfg.SEQ, cfg.TS, cfg.PAST, cfg.NBP
    NBS = PAST // 64
    J0 = NBP - 2
    WLC = NBP + 16
    f = np.float32
    r = np.arange(128)
    c = {}
    c["c_ident"] = np.eye(128, dtype=f)
    c["c_triu"] = np.concatenate([(r[None, :] >= r[:, None]).astype(f), (r[None, :] > r[:, None]).astype(f)], axis=1)
    c["c_tril"] = (r[:, None] > r[None, :]).astype(f)
    c["c_U"] = (r[:, None] <= r[None, :]).astype(f)
    last = np.zeros((2, 128, 128), f); last[0, 127, :] = 1.0; last[1, TS - 1, :] = 1.0
    c["c_last"] = last
    half = 32
    inv = (np.float32(10000.0) ** (-np.arange(half, dtype=f) / f(half))).astype(f)

    def rot(pos):
        ang = pos.astype(f)[:, None] * inv[None, :]
        return np.concatenate([np.cos(ang), np.sin(ang)], axis=1).astype(f)
    c["c_rot_p"] = rot(np.arange(SEQ))
    c["c_rot_s"] = rot(PAST + np.arange(TS))
    lg = np.log1p(-np.exp2(-5.0 - np.arange(4, dtype=np.float64)))
    diff = (r[None, :] - r[:, None]).astype(np.float64)
    Dm = np.where(diff >= 0, np.exp(np.maximum(diff, 0.0)[None] * lg[:, None, None]), 0.0)
    c["c_retD"] = np.stack([Dm, Dm]).astype(f)
    xi = np.exp((r[None, :] + 1.0) * lg[:, None])
    c["c_retxi"] = np.broadcast_to(xi[None, :, None, :], (2, 4, 64, 128)).astype(f).copy()
    z = np.zeros((2, 128, 4), np.float64)
    for g, C in enumerate((128, TS)):
        t = np.arange(C)
        z[g, :C, :] = np.exp((C - 1.0 - t)[:, None] * lg[None, :]) / 8.0
    c["c_retz"] = z.astype(f)
    cc = np.arange(256)
    c["c_dbt"] = (r[:, None] + 128 - cc[None, :]).astype(f)
    j = np.arange(WLC)
    c["c_dlc"] = (r[:, None] - 63 + 64 * (J0 - j[None, :])).astype(f)
    mp = (j - J0)[None, :]
    ct = (r // 64)[:, None]
    forced = (mp == ct) | (mp == ct - 1)
    after = mp > ct
    keep = (~forced & ~after).astype(f)
    add = 2.0 * forced.astype(f) - after.astype(f)
    c["c_keep"] = np.concatenate([keep, add], axis=1).astype(f)
    W = NBS + 1
    ks = np.ones((8, W), f); ad = np.zeros((8, W), f)
    ks[:, NBS - 1:] = 0.0; ad[:, NBS - 1:] = 2.0
    c["c_keeps"] = np.concatenate([ks, ad], axis=1)
    c["c_wm4"] = (r[None, :] > r[:, None]).astype(f)
    c["c_iota"] = r.astype(f).reshape(128, 1)
    return c


_NC_CACHE = {}


def run_cfg(cfg, inputs, n_cores, n_prompt):
    key = (cfg.SEQ, cfg.NS, cfg.TS, cfg.PAST, cfg.NPOOL)
    if key not in _NC_CACHE:
        _NC_CACHE[key] = build(cfg)
    nc = _NC_CACHE[key]
    consts = make_consts(cfg)
    NS, TS, SEQ = cfg.NS, cfg.TS, cfg.SEQ
    A = lambda k: np.ascontiguousarray(np.asarray(inputs[k]))
    pool = A("cache_nsa_kv").reshape(-1, 256)
    wnames = ["w_ada", "b_ada", "norms", "w_in", "cmp_pool", "cmp_pe", "rel_bias", "conv_dw", "conv_dw_b", "conv_ln_g",
              "conv_ln_b", "ret_gn", "gdn_conv_w", "gdn_A_log", "gdn_dt_bias", "gdn_norm", "w_branch", "w_out", "ffn_up",
              "ffn_dw", "ffn_down"]
    shared = {k: A(k) for k in wnames}
    shared.update(consts)
    shared["pool"] = pool
    x_prompt, x_sample = A("x_prompt"), A("x_sample")
    c_prompt, c_sample = A("c_prompt"), A("c_sample")
    win = A("cache_nsa_win"); page_table = A("page_table")
    sc, sr, sgc, sg, sf = A("state_conv"), A("state_ret"), A("state_gdn_conv"), A("state_gdn"), A("state_ffn_conv")
    DB = x_sample.shape[0]
    in_maps = []
    for c in range(n_cores):
        bp = c % n_prompt
        ss = [(c * NS + i) % DB for i in range(NS)]
        m = dict(shared)
        m["xp"] = x_prompt[bp]; m["cp"] = c_prompt[bp:bp + 1]
        m["xs"] = x_sample[ss].reshape(NS * TS, D); m["cs"] = c_sample[ss]
        m["ptab"] = page_table[ss].astype(np.int32)
        m["win_in"] = np.ascontiguousarray(win[:, ss]).reshape(DEPTH, NS, 512, 256)
        m["st_conv"] = np.ascontiguousarray(sc[:, ss]); m["st_ret"] = np.ascontiguousarray(sr[:, ss])
        m["st_gconv"] = np.ascontiguousarray(sgc[:, ss]); m["st_gdn"] = np.ascontiguousarray(sg[:, ss])
        m["st_ffn"] = np.ascontiguousarray(sf[:, ss])
        in_maps.append(m)
    res = run_bass_kernel_spmd(nc, in_maps, core_ids=list(range(n_cores))).results
    B = n_prompt
    pc = [res[b] for b in range(B)]
    f = np.float32
    y_p = np.stack([pc[b]["yp"] for b in range(B)]).astype(f)
    kv_p = np.stack([pc[b]["kvp"] for b in range(B)]).reshape(B, DEPTH, SEQ, 4, 2, 64)
    win_p = np.stack([pc[b]["winp"] for b in range(B)], axis=1).reshape(DEPTH, B, 512, 2, 2, 64)
    conv_p = np.stack([pc[b]["convp"] for b in range(B)], axis=1)
    ret_p = np.stack([pc[b]["retp"] for b in range(B)], axis=1)
    gc_p = np.stack([pc[b]["gcp"] for b in range(B)], axis=1)
    gdn_p = np.stack([pc[b]["gdnp"] for b in range(B)], axis=1)
    ffn_p = np.stack([pc[b]["ffnp"] for b in range(B)], axis=1)
    ncs = DB // NS
    sc_ = [res[c] for c in range(ncs)]
    y_s = np.concatenate([r["ys"].reshape(NS, TS, D) for r in sc_], axis=0)
    kv_s = np.concatenate([r["kvs"] for r in sc_], axis=0).reshape(DB, DEPTH, TS, 4, 2, 64)
    cat1 = lambda k: np.concatenate([r[k] for r in sc_], axis=1)
    win_s = cat1("wins").reshape(DEPTH, DB, 512, 2, 2, 64)
    outs = (y_p, y_s, kv_p, kv_s, win_p, win_s, conv_p, cat1("convs"), ret_p, cat1("rets"), gc_p, cat1("gcs"),
            gdn_p, cat1("gdns"), ffn_p, cat1("ffns"))
    return tuple(np.ascontiguousarray(o, dtype=np.float32) for o in outs)


def kernel(**inputs):
    cfg = Cfg(NS=8)
    return run_cfg(cfg, inputs, n_cores=4, n_prompt=4)
```

```python
import math
import numpy as np
import concourse.bass as bass
import concourse.mybir as mybir
from concourse.bass_utils import run_bass_kernel_spmd

F32 = mybir.dt.float32
BF16 = mybir.dt.bfloat16
I32 = mybir.dt.int32
AF = mybir.ActivationFunctionType
ALU = mybir.AluOpType
AX = mybir.AxisListType

D = 1024
HD = 64
DEPTH = 2
NKV = 768
CONV_CH = 256
CONV_W = 31
D_FF = 2816
IN_COLS = 7700
EPS = 1e-6
NEG = -30000.0
O_Q, O_KV, O_GT, O_CV, O_RET, O_GDN, O_MG = 0, 256, 1024, 1036, 1548, 2572, 3604
GAM = [1.0 - 2.0 ** (-5.0 - h) for h in range(4)]


class Tile:
    def __init__(self, t, name):
        self.t = t
        self.name = name
        self.w = None
        self.r = {}
        self.psum = False

    def __getitem__(self, idx):
        return View(self, self.t[idx])

    def re(self, pat, **kw):
        return self[:].re(pat, **kw)

    def pbc(self, n):
        return self[:].pbc(n)


class View:
    def __init__(self, tile, ap):
        self.tile = tile
        self.ap = ap

    def __getitem__(self, idx):
        return View(self.tile, self.ap[idx])

    def re(self, pat, **kw):
        return View(self.tile, self.ap.rearrange(pat, **kw))

    def bc(self, shape):
        return View(self.tile, self.ap.to_broadcast(shape))

    def unsq(self, ax):
        return View(self.tile, self.ap.unsqueeze(ax))

    def pbc(self, n):
        return View(self.tile, self.ap.partition_broadcast(n))


def _ap(x):
    return x.ap if isinstance(x, View) else x


class Cx:
    def __init__(self, nc):
        self.nc = nc
        self.eng = dict(pe=nc.tensor, dve=nc.vector, act=nc.scalar, pool=nc.gpsimd, sp=nc.sync)
        self.sem = {k: nc.alloc_semaphore("cs_" + k) for k in ("pe", "dve", "act", "pool")}
        self.cnt = {k: 0 for k in self.sem}
        self.dsem = {}
        self.seen = {k: {} for k in self.eng}
        self.nops = 0

    def sb(self, name, shape, dt=F32):
        return Tile(self.nc.alloc_sbuf_tensor(name, list(shape), dt), name)

    def ps(self, name, shape, dt=F32):
        t = Tile(self.nc.alloc_psum_tensor(name, list(shape), dt), name)
        t.psum = True
        return t

    def dram(self, name, shape, dt=F32, kind="Internal"):
        return Tile(self.nc.dram_tensor(name, list(shape), dt, kind=kind), name)

    def _semobj(self, key):
        return self.sem[key[1]] if key[0] == "c" else self.dsem[key[1]]["sems"][key[2]]

    def _sync(self, e, reads, writes):
        deps = {}

        def add(kv):
            k, v = kv
            if deps.get(k, 0) < v:
                deps[k] = v

        for t in reads:
            if t.w is not None:
                add(t.w)
        for t in writes:
            if t.w is not None:
                add(t.w)
            for kv in t.r.items():
                add(kv)
        seen = self.seen[e]
        rt = set(id(t) for t in reads)
        for k, v in deps.items():
            if k == ("c", e) and e == "pe":
                continue
            if seen.get(k, 0) >= v:
                continue
            self.eng[e].wait_ge(self._semobj(k), v)
            seen[k] = v

    def op(self, e, reads, writes, fn):
        reads = [x.tile for x in reads if isinstance(x, View)]
        writes = [x.tile for x in writes if isinstance(x, View)]
        self._sync(e, reads, writes)
        self.cnt[e] += 1
        n = self.cnt[e]
        fn(self.eng[e]).then_inc(self.sem[e], 1)
        key = ("c", e)
        for t in reads:
            if t.psum:
                t.w = (key, n)
                t.r = {}
            elif t.r.get(key, 0) < n:
                t.r[key] = n
        for t in writes:
            t.w = (key, n)
            t.r = {}
        self.nops += 1

    RING = 36

    def dma(self, out, in_, q="sp", ch=None, indirect=None):
        ch = q
        if ch not in self.dsem:
            self.dsem[ch] = {"sems": [self.nc.alloc_semaphore("ds_%s%d" % (ch, i)) for i in range(self.RING)],
                             "cnt": [0] * self.RING, "n": 0}
        ring = self.dsem[ch]
        r = ring["n"] % self.RING
        ring["n"] += 1
        reads = [in_.tile]
        if indirect is not None:
            reads.append(indirect.tile)
        writes = [out.tile]
        self._sync(q, reads, writes)
        key = ("d", ch, r)
        if ring["cnt"][r] > 0 and self.seen[q].get(key, 0) < 16 * ring["cnt"][r]:
            self.eng[q].wait_ge(ring["sems"][r], 16 * ring["cnt"][r])
            self.seen[q][key] = 16 * ring["cnt"][r]
        ring["cnt"][r] += 1
        val = 16 * ring["cnt"][r]
        sem = ring["sems"][r]
        if indirect is None:
            self.eng[q].dma_start(out=out.ap, in_=in_.ap).then_inc(sem, 16)
        else:
            self.eng[q].indirect_dma_start(
                out=out.ap, out_offset=None, in_=in_.ap,
                in_offset=bass.IndirectOffsetOnAxis(ap=indirect.ap, axis=0)).then_inc(sem, 16)
        for t in reads:
            if t.r.get(key, 0) < val:
                t.r[key] = val
        for t in writes:
            t.w = (key, val)
            t.r = {}
        self.nops += 1

    def _all_sems(self):
        items = [(("c", k), self.sem[k], self.cnt[k]) for k in self.sem if self.cnt[k]]
        for ch, ring in self.dsem.items():
            for r in range(self.RING):
                if ring["cnt"][r]:
                    items.append((("d", ch, r), ring["sems"][r], 16 * ring["cnt"][r]))
        return items

    def barrier(self):
        items = self._all_sems()
        for e in self.eng:
            for key, s, v in items:
                if self.seen[e].get(key, 0) >= v:
                    continue
                self.eng[e].wait_ge(s, v)
                self.seen[e][key] = v

    def arena(self, nbytes):
        self.ar = self.nc.alloc_sbuf_tensor("arena", [128, nbytes // 4], F32)
        self.ar_bytes = nbytes
        self.ar_off = 0

    def ar_reset(self):
        self.barrier()
        self.ar_off = 0

    def ar_alloc(self, name, shape, dt=F32):
        esz = 4 if dt in (F32, I32) else 2
        n = 1
        for d_ in shape[1:]:
            n *= d_
        nb = (n * esz + 31) // 32 * 32
        assert self.ar_off + nb <= self.ar_bytes, ("arena overflow", name, self.ar_off, nb, self.ar_bytes)
        ap = self.ar[0:shape[0], self.ar_off // 4:(self.ar_off + nb) // 4]
        if dt != F32:
            ap = ap.bitcast(dt)
        ap = ap[:, 0:n]
        if len(shape) > 2:
            names = " ".join("d%d" % i for i in range(1, len(shape)))
            kw = {"d%d" % i: shape[i] for i in range(1, len(shape))}
            ap = ap.rearrange("p (%s) -> p %s" % (names, names), **kw)
        self.ar_off += nb
        return Tile(ap, name)

    def finish(self):
        for key, sem, v in self._all_sems():
            self.nc.sync.wait_ge(sem, v)

    def mm(self, out, lhsT, rhs, start=True, stop=True):
        self.op("pe", [lhsT, rhs], [out],
                lambda e: e.matmul(out.ap, lhsT=lhsT.ap, rhs=rhs.ap, start=start, stop=stop))

    def tr(self, out, in_, ident):
        self.op("pe", [in_, ident], [out], lambda e: e.transpose(out.ap, in_.ap, ident.ap))

    def act(self, out, in_, func, bias=None, scale=1.0, accum=None):
        rd = [in_] + ([bias] if isinstance(bias, View) else [])
        wr = [out] + ([accum] if accum is not None else [])
        kw = {}
        if bias is not None:
            kw["bias"] = _ap(bias)
        if accum is not None:
            kw["accum_out"] = accum.ap
        self.op("act", rd, wr, lambda e: e.activation(out=out.ap, in_=in_.ap, func=func, scale=scale, **kw))

    def tt(self, e, out, in0, in1, op):
        self.op(e, [in0, in1], [out], lambda g: g.tensor_tensor(out=out.ap, in0=in0.ap, in1=in1.ap, op=op))

    def ts(self, e, out, in0, s1, op0, s2=None, op1=None):
        rd = [in0] + [s for s in (s1, s2) if isinstance(s, View)]
        if op1 is None:
            self.op(e, rd, [out], lambda g: g.tensor_scalar(out=out.ap, in0=in0.ap, scalar1=_ap(s1), scalar2=None, op0=op0))
        else:
            self.op(e, rd, [out], lambda g: g.tensor_scalar(out=out.ap, in0=in0.ap, scalar1=_ap(s1), scalar2=_ap(s2), op0=op0, op1=op1))

    def stt(self, out, in0, scalar, in1, op0, op1):
        rd = [in0, in1] + ([scalar] if isinstance(scalar, View) else [])
        self.op("dve", rd, [out], lambda g: g.scalar_tensor_tensor(out=out.ap, in0=in0.ap, scalar=_ap(scalar), in1=in1.ap, op0=op0, op1=op1))

    def rsqrt(self, out, in_):
        self.act(out, in_, AF.Ln)
        self.act(out, out, AF.Exp, scale=-0.5)

    def cp(self, e, out, in_):
        if e == "act":
            self.act(out, in_, AF.Copy)
        else:
            self.op(e, [in_], [out], lambda g: g.tensor_copy(out=out.ap, in_=in_.ap))

    def memset(self, e, out, val):
        self.op(e, [], [out], lambda g: g.memset(out.ap, val))

    def red(self, out, in_, op=ALU.add):
        self.op("dve", [in_], [out], lambda g: g.tensor_reduce(out=out.ap, in_=in_.ap, axis=AX.X, op=op))

    def max8(self, out, in_):
        self.op("dve", [in_], [out], lambda g: g.max(out=out.ap, in_=in_.ap))

    def match_replace(self, out, rep, vals, imm):
        self.op("dve", [rep, vals], [out], lambda g: g.match_replace(out=out.ap, in_to_replace=rep.ap, in_values=vals.ap, imm_value=imm))


def t5_thresholds():
    n = np.arange(0, 400)
    exact = 16
    large = exact + (np.log(np.maximum(n, 1).astype(np.float32) / np.float32(exact)) / np.float32(math.log(128 / exact))
                     * np.float32(32 - exact)).astype(np.int32)
    b = np.where(n < exact, n, np.minimum(large, 31))
    thr = []
    for k in range(1, 32):
        thr.append(int(np.min(n[b >= k])))
    return thr


class StopBuild(Exception):
    pass


class Cfg:
    stop = None

    def __init__(self, SEQ=4096, NS=4, TS=8, PAST=16384, NPOOL=5120):
        self.SEQ, self.NS, self.TS, self.PAST, self.NPOOL = SEQ, NS, TS, PAST, NPOOL
        self.NPG = PAST // 128
        self.NT = SEQ // 128
        self.NBP = SEQ // 64
        self.WIN = 512


def build(cfg):
    nc = bass.Bass("TRN2", target_bir_lowering=False)
    cx = Cx(nc)
    SEQ, NS, TS, PAST, NPG, NT, NBP = cfg.SEQ, cfg.NS, cfg.TS, cfg.PAST, cfg.NPG, cfg.NT, cfg.NBP
    NSTOK = NS * TS
    J0 = NBP - 2
    NBS = PAST // 64
    WLC = NBP + 16
    NTOK = SEQ + NSTOK
    NR2 = cfg.NPOOL * DEPTH * 128 * 2

    def chk(n):
        if cfg.stop is not None and cfg.stop == n:
            raise StopBuild()

    def din(name, shape, dt=F32):
        return cx.dram(name, shape, dt, kind="ExternalInput")

    def dout(name, shape, dt=F32):
        return cx.dram(name, shape, dt, kind="ExternalOutput")

    xp = din("xp", [SEQ, D]); cpd = din("cp", [1, D]); xs = din("xs", [NSTOK, D]); csd = din("cs", [NS, D])
    pool_d = din("pool", [NR2, 256])
    ptab = din("ptab", [NS, NPG], I32)
    win_in = din("win_in", [DEPTH, NS, 512, 256])
    st_conv = din("st_conv", [DEPTH, NS, 30, 256]); st_ret = din("st_ret", [DEPTH, NS, 4, 64, 64])
    st_gconv = din("st_gconv", [DEPTH, NS, 3, 768]); st_gdn = din("st_gdn", [DEPTH, NS, 4, 64, 64])
    st_ffn = din("st_ffn", [DEPTH, NS, 2, D_FF])
    w_ada = din("w_ada", [DEPTH, D, 6 * D]); b_ada = din("b_ada", [DEPTH, 6 * D]); norms = din("norms", [DEPTH, 4, D])
    w_in = din("w_in", [DEPTH, D, IN_COLS]); cmp_pool = din("cmp_pool", [DEPTH, 2, 64]); cmp_pe = din("cmp_pe", [DEPTH, 2, 64, 64])
    rel_bias = din("rel_bias", [32, 4]); conv_dw = din("conv_dw", [DEPTH, 31, 256]); conv_dw_b = din("conv_dw_b", [DEPTH, 256])
    conv_ln_g = din("conv_ln_g", [DEPTH, 256]); conv_ln_b = din("conv_ln_b", [DEPTH, 256]); ret_gn = din("ret_gn", [DEPTH, 256])
    gdn_conv_w = din("gdn_conv_w", [DEPTH, 4, 768]); gdn_A_log = din("gdn_A_log", [DEPTH, 4]); gdn_dt_bias = din("gdn_dt_bias", [DEPTH, 4])
    gdn_norm = din("gdn_norm", [DEPTH, 64]); w_branch = din("w_branch", [DEPTH, 4, 256, D]); w_out = din("w_out", [DEPTH, D, D])
    ffn_up = din("ffn_up", [DEPTH, D, 2 * D_FF]); ffn_dw = din("ffn_dw", [DEPTH, 3, D_FF]); ffn_down = din("ffn_down", [DEPTH, D_FF, D])
    c_ident = din("c_ident", [128, 128]); c_triu = din("c_triu", [128, 256]); c_tril = din("c_tril", [128, 128])
    c_U = din("c_U", [128, 128]); c_last = din("c_last", [2, 128, 128])
    c_rot_p = din("c_rot_p", [SEQ, 64]); c_rot_s = din("c_rot_s", [TS, 64])
    c_retD = din("c_retD", [2, 4, 128, 128]); c_retxi = din("c_retxi", [2, 4, 64, 128]); c_retz = din("c_retz", [2, 128, 4])
    c_dbt = din("c_dbt", [128, 256]); c_dlc = din("c_dlc", [128, WLC]); c_keep = din("c_keep", [128, 2 * WLC])
    c_keeps = din("c_keeps", [8, 2 * (NBS + 1)])
    c_wm4 = din("c_wm4", [128, 128]); c_iota = din("c_iota", [128, 1])

    yp = dout("yp", [SEQ, D]); ys = dout("ys", [NSTOK, D])
    kvp = dout("kvp", [DEPTH, SEQ, 512]); kvs = dout("kvs", [NS, DEPTH, TS, 512])
    winp = dout("winp", [DEPTH, 512, 256]); wins = dout("wins", [DEPTH, NS, 512, 256])
    convp = dout("convp", [DEPTH, 30, 256]); convs = dout("convs", [DEPTH, NS, 30, 256])
    retp = dout("retp", [DEPTH, 4, 64, 64]); rets = dout("rets", [DEPTH, NS, 4, 64, 64])
    gcp = dout("gcp", [DEPTH, 3, 768]); gcs = dout("gcs", [DEPTH, NS, 3, 768])
    gdnp = dout("gdnp", [DEPTH, 4, 64, 64]); gdns = dout("gdns", [DEPTH, NS, 4, 64, 64])
    ffnp = dout("ffnp", [DEPTH, 2, D_FF]); ffns = dout("ffns", [DEPTH, NS, 2, D_FF])

    xres = cx.dram("xres", [8, 128, NTOK])
    brd = cx.dram("brd", [8, 128, NTOK], BF16)
    bt_d = cx.dram("bt_d", [128, 4 * 256]); lc_d = cx.dram("lc_d", [128, 4 * WLC])

    MTN = 256
    ident = cx.sb("ident", [128, 128]); cx.dma(ident[:], c_ident[:])
    ones_b = cx.sb("ones_b", [128, 128], BF16); cx.memset("dve", ones_b[:], 1.0)
    ones_f = cx.sb("ones_f", [128, 128]); cx.memset("dve", ones_f[:], 1.0)
    rb = cx.sb("rb", [128, 128]); cx.dma(rb[:], rel_bias.re("b h -> (b h)").pbc(128))
    c31 = rb[:, 124:128]
    xT = cx.sb("xT", [128, 8, MTN]); hT = cx.sb("hT", [128, 8, MTN], BF16)
    sqb = cx.sb("sqb", [128, MTN], BF16); rstd = cx.sb("rstd", [128, MTN]); tmpN = cx.sb("tmpN", [128, MTN])
    xtok = cx.sb("xtok", [128, D]); stg = cx.sb("stg", [128, 128]); sttok = cx.sb("sttok", [32, 128])
    g1 = cx.sb("g1", [128, MTN]); g2 = cx.sb("g2", [128, MTN]); g3 = cx.sb("g3", [128, MTN])
    NG = NS + 1
    modT = cx.sb("modT", [128, 48, NG]); cT = cx.sb("cT", [128, 8, NG], BF16); nrm = cx.sb("nrm", [128, 4, 8])
    badaT = cx.sb("badaT", [128, 48]); mv = cx.sb("mv", [128, NG, 6, 8])
    cx.arena(178 * 1024)

    ps_tmp = [cx.ps("pt%d" % i, [128, 512]) for i in range(5)]
    ps_acc = [cx.ps("pa%d" % i, [128, 512]) for i in range(3)]
    rr = {"tmp": 0, "acc": 0, "ev": 0, "eb": 0}

    def ptmp():
        rr["tmp"] = (rr["tmp"] + 1) % len(ps_tmp)
        return ps_tmp[rr["tmp"]]

    def pacc():
        rr["acc"] = (rr["acc"] + 1) % len(ps_acc)
        return ps_acc[rr["acc"]]

    def evac(out, in_):
        rr["ev"] ^= 1
        cx.cp("dve" if rr["ev"] else "act", out, in_)

    def transpose_to(out_sb, in_sb, rows, cols):
        p = ptmp()
        cx.tr(p[0:cols, 0:rows], in_sb, ident[0:rows, 0:rows])
        evac(out_sb, p[0:cols, 0:rows])

    def load_T(dst, src2d, r):
        cx.dma(stg[0:r, :], src2d)
        transpose_to(dst, stg[0:r, :], r, 128)

    def wview(tile, kc, n):
        return tile[:].re("p (k n) -> p k n", k=kc)

    def load_w(dst, src2d, ncols, chunk=2048):
        v = src2d.re("(k p) n -> p k n", p=128)
        for c0 in range(0, ncols, chunk):
            c1 = min(ncols, c0 + chunk)
            cx.dma(dst[:, :, c0:c1], v[:, :, c0:c1], q="pool")

    groups = [dict(kind="s", s=s, T=TS, ntile=1, tok0=SEQ + s * TS, gi=1) for s in range(NS)]
    groups.append(dict(kind="p", s=0, T=128, ntile=NT, tok0=0, gi=0))

    thr = t5_thresholds()

    def build_tables():
        cx.ar_reset()
        rbd = cx.ar_alloc("rbd", [128, 128]); dbt = cx.ar_alloc("dbt", [128, 256]); dlc = cx.ar_alloc("dlc", [128, WLC])
        BTt = cx.ar_alloc("BTt", [128, 4, 256]); LCt = cx.ar_alloc("LCt", [128, 4, WLC]); btmp = cx.ar_alloc("btmp", [128, 256])
        cx.tt("dve", rbd[:, 4:128], rb[:, 4:128], rb[:, 0:124], ALU.subtract)
        cx.dma(dbt[:], c_dbt[:]); cx.dma(dlc[:], c_dlc[:])
        chk(0.1)
        for (tab, dtab, W) in ((BTt, dbt, 256), (LCt, dlc, WLC)):
            for h in range(4):
                dst = tab[:, h, :]
                cx.ts("dve", dst, dtab[:, 0:W], 0.0, ALU.is_lt, NEG, ALU.mult)
                cx.ts("dve", dst, dst, rb[:, h:h + 1], ALU.add)
                for b in range(1, 32):
                    cx.ts("dve", btmp[:, 0:W], dtab[:, 0:W], float(thr[b - 1]), ALU.is_ge, rbd[:, 4 * b + h:4 * b + h + 1], ALU.mult)
                    cx.tt("dve", dst, dst, btmp[:, 0:W], ALU.add)
                chk(0.2)
        chk(0.3)
        cx.dma(bt_d[:], BTt[:].re("p h w -> p (h w)")); cx.dma(lc_d[:], LCt[:].re("p h w -> p (h w)"))

    def load_cT():
        ctok = cx.ar_alloc("ctok", [16, D])
        cx.dma(ctok[0:NS, :], csd[:]); cx.dma(ctok[NS:NS + 1, :], cpd[:])
        cx.act(ctok[0:NG, :], ctok[0:NG, :], AF.Silu)
        for kc in range(8):
            p = ptmp()
            cx.tr(p[:, 0:NG], ctok[0:NG, kc * 128:(kc + 1) * 128], ident[0:NG, 0:NG])
            evac(cT[:, kc, :], p[:, 0:NG])

    def compute_mod(l):
        cx.ar_reset()
        adaw = [wview(cx.ar_alloc("adaw%d" % i, [128, 8 * 512], BF16), 8, 512) for i in range(2)]
        cx.dma(stg[0:48, :], b_ada[l].re("(c p) -> c p", p=128))
        transpose_to(badaT[:, 0:48], stg[0:48, :], 48, 128)
        cx.dma(stg[0:32, :], norms[l].re("j (c p) -> (j c) p", p=128))
        transpose_to(nrm[:].re("p j c -> p (j c)"), stg[0:32, :], 32, 128)
        for g in range(12):
            wt = adaw[g % 2]
            cx.dma(wt, w_ada[l].re("(kc p) n -> p kc n", p=128)[:, :, g * 512:(g + 1) * 512], q="pool")
            for b in range(4):
                p = ptmp()
                for kc in range(8):
                    cx.mm(p[:, 0:NG], wt[:, kc, b * 128:(b + 1) * 128], cT[:, kc, :], start=(kc == 0), stop=(kc == 7))
                blk = g * 4 + b
                cx.ts("dve", modT[:, blk, :], p[:, 0:NG], badaT[:, blk:blk + 1], ALU.add)
        for g in range(NG):
            m = lambda j: modT[:, j * 8:(j + 1) * 8, g]
            cx.stt(mv[:, g, 0, :], m(1), 1.0, nrm[:, 0, :], ALU.add, ALU.mult)
            cx.cp("dve", mv[:, g, 1, :], m(0))
            cx.tt("dve", mv[:, g, 2, :], m(2), nrm[:, 1, :], ALU.mult)
            cx.stt(mv[:, g, 3, :], m(4), 1.0, nrm[:, 2, :], ALU.add, ALU.mult)
            cx.cp("dve", mv[:, g, 4, :], m(3))
            cx.tt("dve", mv[:, g, 5, :], m(5), nrm[:, 3, :], ALU.mult)

    def rms_rstd(src, N, nchunk=8):
        p = ptmp()
        for kc in range(nchunk):
            cx.act(sqb[:, 0:N], src[:, kc, 0:N], AF.Square)
            cx.mm(p[:, 0:N], ones_b[:, :], sqb[:, 0:N], start=(kc == 0), stop=(kc == nchunk - 1))
        cx.ts("dve", rstd[:, 0:N], p[:, 0:N], 1.0 / D, ALU.mult, EPS, ALU.add)
        cx.rsqrt(rstd[:, 0:N], rstd[:, 0:N])

    def mod_norm(g, N, jg, js):
        rms_rstd(xT, N)
        for kc in range(8):
            cx.stt(tmpN[:, 0:N], xT[:, kc, 0:N], mv[:, g, jg, kc:kc + 1], rstd[:, 0:N], ALU.mult, ALU.mult)
            cx.act(hT[:, kc, 0:N], tmpN[:, 0:N], AF.Identity, bias=mv[:, g, js, kc:kc + 1])

    def resid_add(g, N, src, jg):
        rms_rstd(src, N)
        for kc in range(8):
            cx.stt(tmpN[:, 0:N], src[:, kc, 0:N], mv[:, g, jg, kc:kc + 1], rstd[:, 0:N], ALU.mult, ALU.mult)
            cx.tt("pool", xT[:, kc, 0:N], xT[:, kc, 0:N], tmpN[:, 0:N], ALU.add)

    def load_x(l, grp, t0, N):
        if l == 0:
            src = xp if grp["kind"] == "p" else xs
            base = t0 if grp["kind"] == "p" else grp["s"] * TS + t0
            for j in range(0, N, 128):
                n = min(128, N - j)
                cx.dma(xtok[0:n, :], src[base + j:base + j + n, :])
                for kc in range(8):
                    p = ptmp()
                    cx.tr(p[:, 0:n], xtok[0:n, kc * 128:(kc + 1) * 128], ident[0:n, 0:n])
                    evac(xT[:, kc, j:j + n], p[:, 0:n])
        else:
            a = grp["tok0"] + t0
            cx.dma(xT[:, :, 0:N], xres[:, :, a:a + N].re("c p n -> p c n"))

    def store_x(grp, t0, N, final):
        a = grp["tok0"] + t0
        if not final:
            cx.dma(xres[:, :, a:a + N].re("c p n -> p c n"), xT[:, :, 0:N])
        else:
            dst = yp if grp["kind"] == "p" else ys
            base = t0 if grp["kind"] == "p" else grp["s"] * TS + t0
            for j in range(0, N, 128):
                n = min(128, N - j)
                for kc in range(8):
                    p = ptmp()
                    cx.tr(p[0:n, 0:128], xT[:, kc, j:j + n], ident[:, :])
                    evac(xtok[0:n, kc * 128:(kc + 1) * 128], p[0:n, 0:128])
                cx.dma(dst[base + j:base + j + n, :], xtok[0:n, :])

    def run_gens(gens):
        live = list(gens)
        while live:
            for g in list(live):
                try:
                    next(g)
                except StopIteration:
                    live.remove(g)

    def mts(grp, mtn):
        ntok = grp["T"] * grp["ntile"]
        return [(t0, min(mtn, ntok - t0)) for t0 in range(0, ntok, mtn)]

    def phase2(l):
        cx.ar_reset()
        W_UP = wview(cx.ar_alloc("W_UP", [128, 8 * 5632], BF16), 8, 5632)
        W_DN = wview(cx.ar_alloc("W_DN", [128, 22 * 1024], BF16), 22, 1024)
        ffw = cx.ar_alloc("ffw", [128, 22, 3]); ffhalo = cx.ar_alloc("ffhalo", [128, 22, 2]); fft = cx.ar_alloc("fft", [128, 2 + MTN])
        actT = cx.ar_alloc("actT", [128, 22, MTN], BF16); fT = cx.ar_alloc("fT", [128, 8, MTN])
        load_w(W_UP, ffn_up[l], 5632); load_w(W_DN, ffn_down[l], 1024)
        for c in range(22):
            load_T(ffw[:, c, :], ffn_dw[l, :, c * 128:(c + 1) * 128], 3)
        for gi, grp in enumerate(groups):
            if grp["kind"] == "p":
                cx.memset("pool", ffhalo[:], 0.0)
            else:
                for c in range(22):
                    cx.dma(sttok[0:2, :], st_ffn[l, grp["s"], :, c * 128:(c + 1) * 128])
                    transpose_to(ffhalo[:, c, :], sttok[0:2, :], 2, 128)
            for (t0, N) in mts(grp, MTN):
                load_x(1, grp, t0, N)
                mod_norm(gi, N, 3, 4)
                for c in range(22):
                    pg = ptmp()
                    for kc in range(8):
                        cx.mm(pg[:, 0:N], W_UP[:, kc, c * 128:(c + 1) * 128], hT[:, kc, 0:N], start=(kc == 0), stop=(kc == 7))
                    pv = ptmp()
                    for kc in range(8):
                        cx.mm(pv[:, 0:N], W_UP[:, kc, D_FF + c * 128:D_FF + (c + 1) * 128], hT[:, kc, 0:N], start=(kc == 0), stop=(kc == 7))
                    cx.cp("pool", fft[:, 0:2], ffhalo[:, c, :])
                    cx.cp("act", fft[:, 2:2 + N], pg[:, 0:N])
                    cx.cp("pool", ffhalo[:, c, :], fft[:, N:N + 2])
                    cx.ts("dve", g1[:, 0:N], fft[:, 0:N], ffw[:, c, 0:1], ALU.mult)
                    cx.stt(g1[:, 0:N], fft[:, 1:1 + N], ffw[:, c, 1:2], g1[:, 0:N], ALU.mult, ALU.add)
                    cx.stt(g1[:, 0:N], fft[:, 2:2 + N], ffw[:, c, 2:3], g1[:, 0:N], ALU.mult, ALU.add)
                    cx.tt("pool", g2[:, 0:N], g1[:, 0:N], g1[:, 0:N], ALU.mult)
                    cx.ts("pool", g2[:, 0:N], g2[:, 0:N], 0.044715, ALU.mult, 1.0, ALU.add)
                    cx.tt("pool", g2[:, 0:N], g2[:, 0:N], g1[:, 0:N], ALU.mult)
                    cx.act(g3[:, 0:N], g2[:, 0:N], AF.Sigmoid, scale=1.5957691216)
                    cx.tt("dve", g3[:, 0:N], g3[:, 0:N], g1[:, 0:N], ALU.mult)
                    cx.tt("dve", actT[:, c, 0:N], g3[:, 0:N], pv[:, 0:N], ALU.mult)
                for ob in range(8):
                    p = ptmp()
                    for c in range(22):
                        cx.mm(p[:, 0:N], W_DN[:, c, ob * 128:(ob + 1) * 128], actT[:, c, 0:N], start=(c == 0), stop=(c == 21))
                    evac(fT[:, ob, 0:N], p[:, 0:N])
                resid_add(gi, N, fT, 5)
                store_x(grp, t0, N, final=(l == DEPTH - 1))
            dst = ffnp[l] if grp["kind"] == "p" else ffns[l, grp["s"]]
            for c in range(22):
                transpose_to(sttok[0:2, :], ffhalo[:, c, :], 128, 2)
                cx.dma(dst[:, c * 128:(c + 1) * 128], sttok[0:2, :])

    def phase1b(l):
        cx.ar_reset()
        W_MG = wview(cx.ar_alloc("W_MG", [128, 8 * 4096], BF16), 8, 4096)
        W_BR = wview(cx.ar_alloc("W_BR", [128, 8 * 1024], BF16), 8, 1024)
        W_OUT = wview(cx.ar_alloc("W_OUT", [128, 8 * 1024], BF16), 8, 1024)
        BRT = cx.ar_alloc("BRTb", [128, 8, MTN], BF16); mT = cx.ar_alloc("mT", [128, 8, MTN], BF16); fT = cx.ar_alloc("fTb", [128, 8, MTN])
        wv = w_in[l].re("(k p) n -> p k n", p=128)
        for c0 in range(0, 4096, 2048):
            cx.dma(W_MG[:, :, c0:c0 + 2048], wv[:, :, O_MG + c0:O_MG + c0 + 2048], q="pool")
        load_w(W_BR, w_branch[l].re("n k d -> (n k) d"), 1024)
        load_w(W_OUT, w_out[l], 1024)
        for gi, grp in enumerate(groups):
            for (t0, N) in mts(grp, MTN):
                a0 = grp["tok0"] + t0
                load_x(1, grp, t0, N)
                cx.dma(BRT[:, :, 0:N], brd[:, :, a0:a0 + N].re("c p n -> p c n"))
                mod_norm(gi, N, 0, 1)
                for ob in range(8):
                    for n in range(4):
                        pb = ptmp()
                        for kc in range(2):
                            cx.mm(pb[:, 0:N], W_BR[:, n * 2 + kc, ob * 128:(ob + 1) * 128], BRT[:, n * 2 + kc, 0:N], start=(kc == 0), stop=(kc == 1))
                        pg = ptmp()
                        for kc in range(8):
                            cx.mm(pg[:, 0:N], W_MG[:, kc, n * 1024 + ob * 128:n * 1024 + (ob + 1) * 128], hT[:, kc, 0:N], start=(kc == 0), stop=(kc == 7))
                        cx.act(g2[:, 0:N], pg[:, 0:N], AF.Sigmoid)
                        if n == 0:
                            cx.tt("dve", g1[:, 0:N], g2[:, 0:N], pb[:, 0:N], ALU.mult)
                        else:
                            cx.tt("dve", g3[:, 0:N], g2[:, 0:N], pb[:, 0:N], ALU.mult)
                            cx.tt("pool", g1[:, 0:N], g1[:, 0:N], g3[:, 0:N], ALU.add)
                    cx.cp("act", mT[:, ob, 0:N], g1[:, 0:N])
                for ob in range(8):
                    p = ptmp()
                    for kc in range(8):
                        cx.mm(p[:, 0:N], W_OUT[:, kc, ob * 128:(ob + 1) * 128], mT[:, kc, 0:N], start=(kc == 0), stop=(kc == 7))
                    evac(fT[:, ob, 0:N], p[:, 0:N])
                resid_add(gi, N, fT, 2)
                store_x(grp, t0, N, final=False)


    def phase1a(l):
        cx.ar_reset()
        W1 = wview(cx.ar_alloc("W1", [128, 8 * O_MG], BF16), 8, O_MG)
        triu = cx.ar_alloc("triu", [128, 256]); cx.dma(triu[:], c_triu[:])
        tril = cx.ar_alloc("tril", [128, 128]); cx.dma(tril[:], c_tril[:])
        Umat = cx.ar_alloc("Umat", [128, 128]); cx.dma(Umat[:], c_U[:])
        lastmg = [cx.ar_alloc("lastm%d" % g, [128, 128]) for g in range(2)]
        for g in range(2):
            cx.dma(lastmg[g][:], c_last[g])
        wm4 = cx.ar_alloc("wm4", [128, 128]); cx.dma(wm4[:], c_wm4[:])
        iota = cx.ar_alloc("iota", [128, 1]); cx.dma(iota[:], c_iota[:])
        retDg = [cx.ar_alloc("retD0", [128, 4, 128]), cx.ar_alloc("retD1", [8, 4, 8])]
        cx.dma(retDg[0][:], c_retD[0].re("h s c -> s h c")); cx.dma(retDg[1][:], c_retD[1, :, 0:8, 0:8].re("h s c -> s h c"))
        retxig = [cx.ar_alloc("retxi0", [64, 4, 128]), cx.ar_alloc("retxi1", [64, 4, 8])]
        cx.dma(retxig[0][:], c_retxi[0].re("h d c -> d h c")); cx.dma(retxig[1][:], c_retxi[1, :, :, 0:8].re("h d c -> d h c"))
        retzg = [cx.ar_alloc("retz0", [128, 4]), cx.ar_alloc("retz1", [128, 4])]
        cx.dma(retzg[0][:], c_retz[0]); cx.dma(retzg[1][:], c_retz[1])
        BT = cx.ar_alloc("BT", [128, 4, 256]); cx.dma(BT[:].re("p h w -> p (h w)"), bt_d[:])
        LC = cx.ar_alloc("LC", [128, 4, WLC]); cx.dma(LC[:].re("p h w -> p (h w)"), lc_d[:])
        keepadd = cx.ar_alloc("keepadd", [128, 2 * WLC]); cx.dma(keepadd[:], c_keep[:])
        MT1 = 128
        convw = cx.ar_alloc("convw", [128, 2, 31]); cvp = cx.ar_alloc("cvp", [128, 6]); gdw = cx.ar_alloc("gdw", [128, 6, 4])
        gnb = cx.ar_alloc("gnb", [128, 256]); gnorm = cx.ar_alloc("gnorm", [128, 64]); negA = cx.ar_alloc("negA", [128, 4]); dtb = cx.ar_alloc("dtb", [128, 4])
        PEK = cx.ar_alloc("PEK", [128, 128]); PEV = cx.ar_alloc("PEV", [128, 128])
        plf = cx.ar_alloc("plf", [128, 258]); POOLK2 = cx.ar_alloc("POOLK2", [128, 2], BF16); BIGV = cx.ar_alloc("BIGV", [128, 256], BF16)

        def load_layer_params(l):
            for c in range(2):
                load_T(convw[:, c, :], conv_dw[l, :, c * 128:(c + 1) * 128], 31)
            for j, src in enumerate((conv_dw_b, conv_ln_g, conv_ln_b)):
                cx.dma(stg[2 * j:2 * j + 2, :], src[l].re("(c p) -> c p", p=128))
            transpose_to(cvp[:, 0:6], stg[0:6, :], 6, 128)
            for c in range(6):
                load_T(gdw[:, c, :], gdn_conv_w[l, :, c * 128:(c + 1) * 128], 4)
            cx.dma(gnb[:], ret_gn[l].pbc(128)); cx.dma(gnorm[:], gdn_norm[l].pbc(128))
            cx.dma(negA[:], gdn_A_log[l].pbc(128)); cx.dma(dtb[:], gdn_dt_bias[l].pbc(128))
            cx.act(negA[:], negA[:], AF.Exp)
            cx.ts("dve", negA[:], negA[:], -1.0, ALU.mult)
            for half in range(2):
                for kv in range(2):
                    cx.dma(PEK[half * 64:(half + 1) * 64, kv * 64:(kv + 1) * 64], cmp_pe[l, 0])
                    cx.dma(PEV[half * 64:(half + 1) * 64, kv * 64:(kv + 1) * 64], cmp_pe[l, 1])
            cx.memset("dve", plf[:], 0.0)
            pk = cmp_pool[l, 0].re("(p o) -> p o", o=1); pv = cmp_pool[l, 1].re("(p o) -> p o", o=1)
            cx.dma(plf[0:64, 0:1], pk); cx.dma(plf[64:128, 1:2], pk)
            cx.dma(plf[0:64, 128:129], pv); cx.dma(plf[64:128, 129:130], pv)
            cx.cp("dve", POOLK2[:], plf[:, 0:2]); cx.cp("dve", BIGV[:], plf[:, 2:258])

        UB = cx.ar_alloc("UB", [128, 2, 30 + MT1]); GB = cx.ar_alloc("GB", [128, 6, 3 + MT1])
        CY = cx.ar_alloc("CY", [128, 2, MT1]); CY2 = cx.ar_alloc("CY2", [128, 2, MT1]); GY = cx.ar_alloc("GY", [128, 6, MT1])
        SR = cx.ar_alloc("SR", [64, 4, 64]); SGs = cx.ar_alloc("SGs", [64, 4, 64])
        KSEL = cx.ar_alloc("KSEL", [128, SEQ], BF16); VSEL = cx.ar_alloc("VSEL", [128, NT, 2, 65], BF16)
        KWIN = cx.ar_alloc("KWIN", [128, 5, 128], BF16); VWIN = cx.ar_alloc("VWIN", [128, 5, 2, 65], BF16)
        NBK = max(NBP, NBS); NVT = (NBK + 127) // 128
        KC = cx.ar_alloc("KC", [128, NBK], BF16); VC = cx.ar_alloc("VC", [128, NVT, 2, 65], BF16); VCf = cx.ar_alloc("VCf", [128, 128])
        KNEW = cx.ar_alloc("KNEW", [128, 8], BF16); VNEW = cx.ar_alloc("VNEW", [8, 2, 65], BF16)
        for t_ in (VSEL, VWIN, VC, VNEW):
            cx.memset("pool", t_[:], 1.0)
        QA = cx.ar_alloc("QA", [128, MT1], BF16); QB = cx.ar_alloc("QB", [128, MT1], BF16)
        BRT = cx.ar_alloc("BRT", [128, 4, 2, MT1], BF16)
        KVT = cx.ar_alloc("KVT", [128, 780]); RETT = cx.ar_alloc("RETT", [128, 1024]); ZAB = cx.ar_alloc("ZAB", [128, 264])
        RK = cx.ar_alloc("RK", [128, 128], BF16); RV = cx.ar_alloc("RV", [128, 128], BF16)
        Ebuf = [cx.ar_alloc("Eb%d" % k, [128, 512]) for k in range(2)]
        PTb = [cx.ar_alloc("PTb%d" % k, [128, 4, 128], BF16) for k in range(2)]
        OACC = cx.ar_alloc("OACC", [128, 3, 4, 65])
        WS = max(NBK + 1, 16)
        PCN = cx.ar_alloc("PCN", [128, 4, NBK]); SC = cx.ar_alloc("SC", [128, WS]); SC2 = cx.ar_alloc("SC2", [128, WS]); SEL = cx.ar_alloc("SEL", [128, 2, WS])
        M8 = cx.ar_alloc("M8", [128, 16]); rs4 = cx.ar_alloc("rs4", [128, 8]); GT = cx.ar_alloc("GT", [128, 12]); FF = cx.ar_alloc("FF", [128, 12])
        ONSA = cx.ar_alloc("ONSA", [128, 256]); OTMP = cx.ar_alloc("OTMP", [128, 256])
        ROT = cx.ar_alloc("ROT", [128, 64]); QR = cx.ar_alloc("QR", [128, 256]); KR = cx.ar_alloc("KR", [128, 256]); RT1 = cx.ar_alloc("RT1", [128, 128]); RT2 = cx.ar_alloc("RT2", [128, 128])
        QT = cx.ar_alloc("QT", [64, 128]); QXT = cx.ar_alloc("QXT", [64, 128]); KT = cx.ar_alloc("KT", [64, 128]); KZ = cx.ar_alloc("KZ", [128, 64]); ATT = cx.ar_alloc("ATT", [128, 128])
        ST8 = cx.ar_alloc("ST8", [128, 8]); SIL = cx.ar_alloc("SIL", [128, 256])
        GQ = cx.ar_alloc("GQ", [128, 256]); GK = cx.ar_alloc("GK", [128, 256]); GV = cx.ar_alloc("GV", [128, 256])
        GG = cx.ar_alloc("GG", [128, 4]); GBETA = cx.ar_alloc("GBETA", [128, 4]); GC = cx.ar_alloc("GC", [128, 4]); GLB = cx.ar_alloc("GLB", [128, 4])
        EGC = cx.ar_alloc("EGC", [128, 4]); EKD = cx.ar_alloc("EKD", [128, 4]); EGL = cx.ar_alloc("EGL", [128, 4]); G4 = cx.ar_alloc("G4", [128, 4]); G4b = cx.ar_alloc("G4b", [128, 4])
        KBt = cx.ar_alloc("KBt", [128, 64]); RU = cx.ar_alloc("RU", [128, 64]); KBE = cx.ar_alloc("KBE", [128, 64]); QD = cx.ar_alloc("QD", [128, 64]); KD = cx.ar_alloc("KD", [128, 64])
        gKT = cx.ar_alloc("gKT", [64, 128]); gKBT = cx.ar_alloc("gKBT", [64, 128]); gQT = cx.ar_alloc("gQT", [64, 128]); gQDT = cx.ar_alloc("gQDT", [64, 128])
        GREP = cx.ar_alloc("GREP", [128, 128]); LMU = cx.ar_alloc("LMU", [128, 128]); LML = cx.ar_alloc("LML", [128, 128]); LMT = cx.ar_alloc("LMT", [128, 128])
        GATT = cx.ar_alloc("GATT", [128, 128]); YM = cx.ar_alloc("YM", [128, 128])
        PQ = [[cx.ar_alloc("PQ%d%d" % (a, b), [128, 128]) for b in range(2)] for a in range(2)]
        USB = cx.ar_alloc("USB", [128, 64]); WT = cx.ar_alloc("WT", [64, 128]); VN = cx.ar_alloc("VN", [128, 64]); OG = cx.ar_alloc("OG", [128, 256])
        PTI0 = cx.ar_alloc("PTI0", [128, NPG], I32); PTI1 = cx.ar_alloc("PTI1", [128, NPG], I32); PTF = cx.ar_alloc("PTF", [128, NPG])
        PGS = [cx.ar_alloc("PGS%d" % k, [128, 256]) for k in range(8)]
        RKs = [RK, cx.ar_alloc("RK1", [128, 128], BF16)]; RVs = [RV, cx.ar_alloc("RV1", [128, 128], BF16)]
        KPGs = [cx.ar_alloc("KPG%d" % k, [128, 4, 128], BF16) for k in range(2)]
        VPGs = [cx.ar_alloc("VPG%d" % k, [128, 4, 2, 65], BF16) for k in range(2)]
        for t_ in VPGs:
            cx.memset("pool", t_[:], 1.0)
        LCS2 = cx.ar_alloc("LCS2", [8, 4, 128]); keeps = cx.ar_alloc("keeps", [8, 2 * (NBS + 1)]); cx.dma(keeps[:], c_keeps[0:8, :])
        WINT = cx.ar_alloc("WINT", [128, 256])
        for h in range(4):
            cx.ts("dve", LCS2[0:8, h, :], ones_f[0:8, 0:128], c31[0:8, h:h + 1], ALU.mult)
            cx.cp("dve", LCS2[0:8, h, 126:128], LC[0:8, h, J0 - 2:J0])
        eb = {"i": 0}

        def proj_fm(col0, ncols, N):
            p = ptmp()
            for kc in range(8):
                cx.mm(p[0:ncols, 0:N], W1[:, kc, col0:col0 + ncols], hT[:, kc, 0:N], start=(kc == 0), stop=(kc == 7))
            return p[0:ncols, 0:N]

        def proj_tm(col0, ncols, c0, T):
            p = ptmp()
            for kc in range(8):
                cx.mm(p[0:T, 0:ncols], hT[:, kc, c0:c0 + T], W1[:, kc, col0:col0 + ncols], start=(kc == 0), stop=(kc == 7))
            return p[0:T, 0:ncols]

        def hprt(h):
            return (0, 64) if h < 2 else (64, 128)

        def attend(T, h, br, qv, tiles, near_bt=None):
            ntot = sum(t["nk"] for t in tiles)
            p = ptmp(); off = 0
            for t in tiles:
                cx.mm(p[0:T, off:off + t["nk"]], qv, t["kT"]); off += t["nk"]
            eb["i"] ^= 1
            E = Ebuf[eb["i"]]; PT = PTb[eb["i"]]
            if near_bt is None:
                cx.act(E[0:T, 0:ntot], p[0:T, 0:ntot], AF.Exp, bias=c31[0:T, h:h + 1])
            else:
                cx.tt("dve", E[0:T, 0:ntot], p[0:T, 0:ntot], near_bt, ALU.add)
                cx.act(E[0:T, 0:ntot], E[0:T, 0:ntot], AF.Exp)
            off = 0
            for t in tiles:
                if t.get("mask") is not None:
                    ev_ = E[0:T, off:off + t["nk"]]
                    if t.get("mre"):
                        ev_ = ev_.re("p (b k) -> p b k", k=t["mre"])
                    cx.tt("dve" if T <= 8 else "pool", ev_, ev_, t["mask"], ALU.mult)
                off += t["nk"]
            p2 = ptmp(); off = 0
            for k, t in enumerate(tiles):
                cx.tr(p2[0:t["nk"], k * T:(k + 1) * T], E[0:T, off:off + t["nk"]], ident[0:T, 0:T]); off += t["nk"]
            nt = len(tiles)
            evac(PT[:, 0:nt, 0:T], p2[:, 0:nt * T].re("p (k t) -> p k t", t=T))
            p3 = ptmp()
            for k, t in enumerate(tiles):
                cx.mm(p3[0:T, 0:65], PT[0:t["nk"], k, 0:T], t["v"], start=(k == 0), stop=(k == nt - 1))
            cx.tt("dve", OACC[0:T, br, h, :], OACC[0:T, br, h, :], p3[0:T, 0:65], ALU.add)
            return E

        def nsa_select(T, nbv, W, keepv, addv):
            for kv in range(2):
                cx.memset("pool", SC[0:T, 0:W], 0.0)
                cx.tt("dve", SC[0:T, 0:nbv], PCN[0:T, 2 * kv, 0:nbv], PCN[0:T, 2 * kv + 1, 0:nbv], ALU.add)
                cx.tt("dve", SC[0:T, 0:W], SC[0:T, 0:W], keepv, ALU.mult)
                cx.tt("dve", SC[0:T, 0:W], SC[0:T, 0:W], addv, ALU.add)
                cx.memset("dve", SC[0:T, 0:1], 2.0)
                cx.max8(M8[0:T, 0:8], SC[0:T, 0:W])
                cx.match_replace(SC2[0:T, 0:W], M8[0:T, 0:8], SC[0:T, 0:W], -5.0)
                cx.max8(M8[0:T, 8:16], SC2[0:T, 0:W])
                cx.ts("dve", SEL[0:T, kv, 0:W], SC[0:T, 0:W], M8[0:T, 15:16], ALU.is_ge)
                cx.stt(SEL[0:T, kv, 0:W], SC[0:T, 0:W], -0.5, SEL[0:T, kv, 0:W], ALU.is_gt, ALU.mult)

        def nsa_compressed(T, c0, nbv, lcv):
            for h in range(4):
                a, b = hprt(h); kv = h // 2
                qv = (QA if h % 2 == 0 else QB)[a:b, c0:c0 + T]
                for t0 in range(0, nbv, 128):
                    nk = min(128, nbv - t0)
                    tl = [dict(kT=KC[a:b, t0:t0 + nk], v=VC[0:nk, t0 // 128, kv, :], nk=nk)]
                    E = attend(T, h, 0, qv, tl, near_bt=lcv(h, t0, nk))
                    cx.cp("pool", PCN[0:T, h, t0:t0 + nk], E[0:T, 0:nk])
                cx.red(rs4[0:T, h:h + 1], PCN[0:T, h, 0:nbv])
                cx.ts("dve", rs4[0:T, h:h + 1], rs4[0:T, h:h + 1], 1e-30, ALU.max)
                cx.op("dve", [rs4[0:T, h:h + 1]], [rs4[0:T, 4 + h:5 + h]],
                      lambda g, hh=h: g.reciprocal(out=rs4[0:T, 4 + hh:5 + hh].ap, in_=rs4[0:T, hh:hh + 1].ap))
                cx.ts("dve", PCN[0:T, h, 0:nbv], PCN[0:T, h, 0:nbv], rs4[0:T, 4 + h:5 + h], ALU.mult)

        def nsa_combine(T, c0):
            cx.act(GT[0:T, :], KVT[0:T, 768:780], AF.Sigmoid)
            den = OACC[0:T, :, :, 64]
            cx.ts("dve", FF[0:T, :].re("p (b h) -> p b h", h=4), den, 1e-30, ALU.max)
            cx.op("dve", [FF[0:T, :]], [FF[0:T, :]], lambda g: g.reciprocal(out=FF[0:T, :].ap, in_=FF[0:T, :].ap))
            cx.tt("dve", FF[0:T, :], FF[0:T, :], GT[0:T, :], ALU.mult)
            o3 = ONSA[0:T, :].re("p (h e) -> p h e", e=64); t3 = OTMP[0:T, :].re("p (h e) -> p h e", e=64)
            for br in range(3):
                f = FF[0:T, br * 4:(br + 1) * 4].unsq(2).bc([T, 4, 64])
                if br == 0:
                    cx.tt("dve", o3, OACC[0:T, br, :, 0:64], f, ALU.mult)
                else:
                    cx.tt("pool", t3, OACC[0:T, br, :, 0:64], f, ALU.mult)
                    cx.tt("pool", o3, o3, t3, ALU.add)
            for c in range(2):
                transpose_to(BRT[:, 0, c, c0:c0 + T], ONSA[0:T, c * 128:(c + 1) * 128], T, 128)

        def nsa_prompt_tile(l, i, c0):
            T = 128; p0 = i * 128
            cx.cp("pool", VSEL[0:T, i, :, 0:64], KVT[0:T, 384:512].re("p (k d) -> p k d", d=64))
            cx.cp("pool", VWIN[0:T, i % 5, :, 0:64], KVT[0:T, 640:768].re("p (k d) -> p k d", d=64))
            cx.tt("pool", RK[0:T, :], KVT[0:T, 0:128], PEK[0:T, :], ALU.add)
            cx.tt("pool", RV[0:T, :], KVT[0:T, 128:256], PEV[0:T, :], ALU.add)
            p = ptmp(); cx.mm(p[:, 0:2], RK[:, :], POOLK2[:, :]); cx.cp("dve", KC[:, 2 * i:2 * i + 2], p[:, 0:2])
            p = ptmp(); cx.mm(p[:, 0:128], BIGV[:, 126 - 2 * i:254 - 2 * i], RV[:, :])
            cx.tt("dve", VCf[:], VCf[:], p[:, 0:128], ALU.add)
            cx.cp("pool", VC[:, 0, :, 0:64], VCf[:].re("p (k d) -> p k d", d=64))
            cx.memset("pool", OACC[0:T], 0.0)
            nbv = 2 * i + 2; W = max(nbv, 16); jc = J0 - 2 * i
            yield
            nsa_compressed(T, c0, nbv, lambda h, t0, nk: LC[0:T, h, jc + t0:jc + t0 + nk])
            yield
            nsa_select(T, nbv, W, keepadd[0:T, jc:jc + W], keepadd[0:T, WLC + jc:WLC + jc + W])
            yield
            for h in range(4):
                a, b = hprt(h); kv = h // 2
                qv = (QA if h % 2 == 0 else QB)[a:b, c0:c0 + T]
                far = [dict(kT=KSEL[a:b, kt * 128:(kt + 1) * 128], v=VSEL[:, kt, kv, :], nk=128,
                            mask=SEL[0:T, kv, 2 * kt:2 * kt + 2].unsq(2).bc([T, 2, 64]), mre=64) for kt in range(0, i - 1)]
                for g0 in range(0, len(far), 4):
                    attend(T, h, 1, qv, far[g0:g0 + 4])
                    yield
                kts = [kt for kt in (i - 1, i) if kt >= 0]
                near = [dict(kT=KSEL[a:b, kt * 128:(kt + 1) * 128], v=VSEL[:, kt, kv, :], nk=128,
                             mask=SEL[0:T, kv, 2 * kt:2 * kt + 2].unsq(2).bc([T, 2, 64]), mre=64) for kt in kts]
                bt = BT[0:T, h, 256 - 128 * len(kts):256]
                attend(T, h, 1, qv, near, near_bt=bt)
                yield
                farw = [dict(kT=KWIN[a:b, kt % 5, :], v=VWIN[:, kt % 5, kv, :], nk=128,
                             mask=(wm4[0:T, :] if kt == i - 4 else None)) for kt in (i - 4, i - 3, i - 2) if kt >= 0]
                if farw:
                    attend(T, h, 2, qv, farw)
                    yield
                nearw = [dict(kT=KWIN[a:b, kt % 5, :], v=VWIN[:, kt % 5, kv, :], nk=128) for kt in kts]
                attend(T, h, 2, qv, nearw, near_bt=bt)
                yield
            nsa_combine(T, c0)

        def sample_prep(l, s):
            cx.dma(PTI0[:], ptab[s].pbc(128))
            cx.cp("dve", PTF[:], PTI0[:])
            cx.ts("dve", PTF[:], PTF[:], float(DEPTH * 128), ALU.mult, float(l * 128), ALU.add)
            cx.ts("dve", PTF[:], PTF[:], iota[:, 0:1], ALU.add)
            cx.ts("dve", PTF[:], PTF[:], 2.0, ALU.mult)
            cx.cp("dve", PTI0[:], PTF[:])
            cx.ts("dve", PTF[:], PTF[:], 1.0, ALU.add)
            cx.cp("dve", PTI1[:], PTF[:])
            PF = 6

            def issue1(pg):
                cx.dma(PGS[pg % 8][:], pool_d[:, :], q="pool", indirect=PTI0[:, pg:pg + 1])
            for pg in range(min(PF, NPG)):
                issue1(pg)
            pv = None
            for pg in range(NPG):
                if pg + PF < NPG:
                    issue1(pg + PF)
                buf = PGS[pg % 8]; rk = RKs[pg % 2]; rv = RVs[pg % 2]
                cx.tt("dve", rk[:, :], buf[:, 0:128], PEK[:, :], ALU.add)
                cx.tt("dve", rv[:, :], buf[:, 128:256], PEV[:, :], ALU.add)
                p = ptmp(); cx.mm(p[:, 0:2], rk[:, :], POOLK2[:, :]); cx.cp("act", KC[:, 2 * pg:2 * pg + 2], p[:, 0:2])
                r = pg % 64
                if r == 0:
                    pv = pacc()
                cx.mm(pv[:, 0:128], BIGV[:, 126 - 2 * r:254 - 2 * r], rv[:, :], start=(r == 0), stop=(r == 63 or pg == NPG - 1))
                if r == 63 or pg == NPG - 1:
                    cx.cp("act", VC[:, pg // 64, :, 0:64], pv[:, 0:128].re("p (k d) -> p k d", d=64))
            for kt in range(4):
                cx.dma(WINT[:], win_in[l, s, kt * 128:(kt + 1) * 128, :])
                transpose_to(KWIN[:, kt, :], WINT[:, 0:128], 128, 128)
                cx.cp("pool", VWIN[:, kt, :, 0:64], WINT[:, 128:256].re("p (k d) -> p k d", d=64))
            cx.dma(wins[l, s, 0:512 - TS, :], win_in[l, s, TS:512, :])

        def nsa_sample_tile(l, s, c0):
            T = TS
            cx.cp("pool", VNEW[0:T, :, 0:64], KVT[0:T, 384:512].re("p (k d) -> p k d", d=64))
            cx.cp("pool", VWIN[0:T, 4, :, 0:64], KVT[0:T, 640:768].re("p (k d) -> p k d", d=64))
            cx.memset("pool", OACC[0:T], 0.0)
            nbv = NBS; W = NBS + 1
            nsa_compressed(T, c0, nbv, lambda h, t0, nk: (LCS2[0:T, h, 128 - nk:128] if t0 + nk == nbv else None))
            nsa_select(T, nbv, W, keeps[0:T, 0:W], keeps[0:T, W:2 * W])
            qv = lambda h: (QA if h % 2 == 0 else QB)[hprt(h)[0]:hprt(h)[1], c0:c0 + T]
            def gath(g0):
                for k in range(min(4, NPG - g0)):
                    cx.dma(PGS[(g0 + k) % 8][:], pool_d[:, :], q="pool", indirect=PTI1[:, g0 + k:g0 + k + 1])

            def prep(g0):
                bsel = (g0 // 4) % 2
                npg_ = min(4, NPG - g0)
                for k in range(npg_):
                    buf = PGS[(g0 + k) % 8]
                    transpose_to(KPGs[bsel][:, k, :], buf[:, 0:128], 128, 128)
                    cx.cp("pool", VPGs[bsel][:, k, :, 0:64], buf[:, 128:256].re("p (k d) -> p k d", d=64))
            gath(0)
            prep(0)
            yield
            for g0 in range(0, NPG, 4):
                npg = min(4, NPG - g0)
                KPG = KPGs[(g0 // 4) % 2]; VPG = VPGs[(g0 // 4) % 2]
                if g0 + 4 < NPG:
                    gath(g0 + 4)
                for h in range(4):
                    a, b = hprt(h); kv = h // 2
                    tiles = [dict(kT=KPG[a:b, k, :], v=VPG[:, k, kv, :], nk=128,
                                  mask=SEL[0:T, kv, 2 * (g0 + k):2 * (g0 + k) + 2].unsq(2).bc([T, 2, 64]), mre=64) for k in range(npg)]
                    if g0 + npg == NPG:
                        last = tiles[-1]; tiles = tiles[:-1]
                        newt = dict(kT=KNEW[a:b, 0:T], v=VNEW[0:T, kv, :], nk=T, mask=SEL[0:T, kv, NBS:NBS + 1].bc([T, T]))
                        if tiles:
                            attend(T, h, 1, qv(h), tiles)
                        attend(T, h, 1, qv(h), [last, newt], near_bt=BT[0:T, h, 0:128 + T])
                    else:
                        attend(T, h, 1, qv(h), tiles)
                    yield
                if g0 + 4 < NPG:
                    prep(g0 + 4)
            for h in range(4):
                a, b = hprt(h); kv = h // 2
                farw = [dict(kT=KWIN[a:b, kt, :], v=VWIN[:, kt, kv, :], nk=128, mask=(wm4[0:T, :] if kt == 0 else None)) for kt in range(3)]
                attend(T, h, 2, qv(h), farw)
                nearw = [dict(kT=KWIN[a:b, 3, :], v=VWIN[:, 3, kv, :], nk=128), dict(kT=KWIN[a:b, 4, 0:T], v=VWIN[0:T, 4, kv, :], nk=T)]
                attend(T, h, 2, qv(h), nearw, near_bt=BT[0:T, h, 0:128 + T])
                yield
            nsa_combine(T, c0)

        def conformer_mt(N):
            for c in range(2):
                y = CY[:, c, 0:N]
                cx.ts("dve", y, UB[:, c, 0:N], convw[:, c, 0:1], ALU.mult)
                for k in range(1, 31):
                    cx.stt(y, UB[:, c, k:k + N], convw[:, c, k:k + 1], y, ALU.mult, ALU.add)
                    if k % 6 == 0:
                        yield
                cx.ts("dve", y, y, cvp[:, c:c + 1], ALU.add)
                cx.tt("pool", CY2[:, c, 0:N], y, y, ALU.mult)
            pm = ptmp()
            for c in range(2):
                cx.mm(pm[:, 0:N], ones_f[:, :], CY[:, c, 0:N], start=(c == 0), stop=(c == 1))
            pq = ptmp()
            for c in range(2):
                cx.mm(pq[:, 0:N], ones_f[:, :], CY2[:, c, 0:N], start=(c == 0), stop=(c == 1))
            cx.ts("dve", g1[:, 0:N], pm[:, 0:N], 1.0 / 256, ALU.mult)
            cx.tt("dve", g2[:, 0:N], g1[:, 0:N], g1[:, 0:N], ALU.mult)
            cx.stt(g2[:, 0:N], pq[:, 0:N], 1.0 / 256, g2[:, 0:N], ALU.mult, ALU.subtract)
            cx.ts("dve", g2[:, 0:N], g2[:, 0:N], 0.0, ALU.max, EPS, ALU.add)
            cx.rsqrt(g2[:, 0:N], g2[:, 0:N])
            for c in range(2):
                cx.tt("pool", g3[:, 0:N], CY[:, c, 0:N], g1[:, 0:N], ALU.subtract)
                cx.tt("pool", g3[:, 0:N], g3[:, 0:N], g2[:, 0:N], ALU.mult)
                cx.ts("dve", g3[:, 0:N], g3[:, 0:N], cvp[:, 2 + c:3 + c], ALU.mult, cvp[:, 4 + c:5 + c], ALU.add)
                cx.act(BRT[:, 1, c, 0:N], g3[:, 0:N], AF.Silu)

        def gdn_conv_mt(N):
            for c in range(6):
                y = GY[:, c, 0:N]
                cx.ts("dve", y, GB[:, c, 0:N], gdw[:, c, 0:1], ALU.mult)
                for k in range(1, 4):
                    cx.stt(y, GB[:, c, k:k + N], gdw[:, c, k:k + 1], y, ALU.mult, ALU.add)
                cx.act(y, y, AF.Silu)
                yield

        def bc4(v, T):
            return v.unsq(2).bc([T, 4, 64])

        def h3(t, T):
            return t[0:T, :].re("p (h e) -> p h e", e=64)

        def retention_tile(grp, pos, c0):
            T = grp["T"]; g = grp["gi"]
            src = c_rot_p[pos:pos + T, :] if grp["kind"] == "p" else c_rot_s[0:T, :]
            cx.dma(ROT[0:T, :], src)
            cosb = ROT[0:T, 0:32].unsq(1).bc([T, 4, 32]); sinb = ROT[0:T, 32:64].unsq(1).bc([T, 4, 32])
            ta = RT1[0:T, :].re("p (h f) -> p h f", f=32); tb = RT2[0:T, :].re("p (h f) -> p h f", f=32)
            for (off, dst) in ((0, QR), (256, KR)):
                s4 = RETT[0:T, off:off + 256].re("p (h w f) -> p h w f", h=4, w=2)
                d4 = dst[0:T, :].re("p (h w f) -> p h w f", h=4, w=2)
                x1 = s4[:, :, 0, :]; x2 = s4[:, :, 1, :]
                cx.tt("pool", d4[:, :, 0, :], x1, cosb, ALU.mult); cx.tt("pool", ta, x2, sinb, ALU.mult)
                cx.tt("pool", d4[:, :, 0, :], d4[:, :, 0, :], ta, ALU.subtract)
                cx.tt("dve", d4[:, :, 1, :], x1, sinb, ALU.mult); cx.tt("dve", tb, x2, cosb, ALU.mult)
                cx.tt("dve", d4[:, :, 1, :], d4[:, :, 1, :], tb, ALU.add)
            if T == 128:
                chk(7.41)
            po = pacc()
            for h in range(4):
                qh = QR[0:T, h * 64:(h + 1) * 64]; kh = KR[0:T, h * 64:(h + 1) * 64]; vh = RETT[0:T, 512 + h * 64:512 + (h + 1) * 64]
                p = ptmp(); cx.tr(p[0:64, 0:T], qh, ident[0:T, 0:T])
                cx.cp("act", QT[0:64, 0:T], p[0:64, 0:T]); cx.tt("dve", QXT[0:64, 0:T], QT[0:64, 0:T], retxig[g][0:64, h, 0:T], ALU.mult)
                p = ptmp(); cx.tr(p[0:64, 0:T], kh, ident[0:T, 0:T]); cx.act(KT[0:64, 0:T], p[0:64, 0:T], AF.Identity, scale=0.125)
                cx.ts("pool", KZ[0:T, :], kh, retzg[g][0:T, h:h + 1], ALU.mult)
                if T == 128:
                    chk(7.42)
                p = ptmp(); cx.mm(p[0:T, 0:T], KT[0:64, 0:T], QT[0:64, 0:T])
                cx.tt("dve", ATT[0:T, 0:T], p[0:T, 0:T], retDg[g][0:T, h, 0:T], ALU.mult)
                if T == 128:
                    chk(7.43)
                cx.mm(po[0:T, h * 64:(h + 1) * 64], ATT[0:T, 0:T], vh, start=True, stop=False)
                cx.mm(po[0:T, h * 64:(h + 1) * 64], QXT[0:64, 0:T], SR[0:64, h, :], start=False, stop=True)
                if T == 128:
                    chk(7.44)
                pS = ptmp(); cx.mm(pS[0:64, 0:64], KZ[0:T, 0:64], vh)
                cx.stt(SR[0:64, h, :], SR[0:64, h, :], float(GAM[h] ** T), pS[0:64, 0:64], ALU.mult, ALU.add)
                yield
            cx.cp("act", OG[0:T, :], po[0:T, 0:256])
            cx.red(ST8[0:T, 0:4], h3(OG, T))
            cx.tt("pool", OTMP[0:T, :], OG[0:T, :], OG[0:T, :], ALU.mult)
            cx.red(ST8[0:T, 4:8], h3(OTMP, T))
            cx.ts("dve", ST8[0:T, 0:8], ST8[0:T, 0:8], 1.0 / 64, ALU.mult)
            cx.tt("dve", G4[0:T, :], ST8[0:T, 0:4], ST8[0:T, 0:4], ALU.mult)
            cx.tt("dve", G4[0:T, :], ST8[0:T, 4:8], G4[0:T, :], ALU.subtract)
            cx.ts("dve", G4[0:T, :], G4[0:T, :], 0.0, ALU.max, EPS, ALU.add)
            cx.rsqrt(G4[0:T, :], G4[0:T, :])
            cx.tt("dve", h3(OG, T), h3(OG, T), bc4(ST8[0:T, 0:4], T), ALU.subtract)
            cx.tt("dve", h3(OG, T), h3(OG, T), bc4(G4[0:T, :], T), ALU.mult)
            cx.tt("pool", OG[0:T, :], OG[0:T, :], gnb[0:T, :], ALU.mult)
            cx.act(SIL[0:T, :], RETT[0:T, 768:1024], AF.Silu)
            cx.tt("pool", OG[0:T, :], OG[0:T, :], SIL[0:T, :], ALU.mult)
            for c in range(2):
                transpose_to(BRT[:, 2, c, c0:c0 + T], OG[0:T, c * 128:(c + 1) * 128], T, 128)

        def gdn_tile(grp, c0):
            T = grp["T"]; g = grp["gi"]
            for part, dst in enumerate((GQ, GK, GV)):
                for h in range(4):
                    blk = part * 2 + h // 2; a = (h % 2) * 64
                    p = ptmp(); cx.tr(p[0:T, 0:64], GY[a:a + 64, blk, c0:c0 + T], ident[a:a + 64, a:a + 64])
                    evac(dst[0:T, h * 64:(h + 1) * 64], p[0:T, 0:64])
            for X, sc in ((GQ, 0.125), (GK, 1.0)):
                cx.tt("pool", OTMP[0:T, :], X[0:T, :], X[0:T, :], ALU.mult)
                cx.red(G4[0:T, :], h3(OTMP, T))
                cx.ts("dve", G4[0:T, :], G4[0:T, :], 0.0, ALU.max, EPS, ALU.add)
                cx.rsqrt(G4[0:T, :], G4[0:T, :])
                if sc != 1.0:
                    cx.ts("dve", G4[0:T, :], G4[0:T, :], sc, ALU.mult)
                cx.tt("dve", h3(X, T), h3(X, T), bc4(G4[0:T, :], T), ALU.mult)
            cx.tt("dve", G4[0:T, :], ZAB[0:T, 256:260], dtb[0:T, :], ALU.add)
            cx.ts("dve", G4b[0:T, :], G4[0:T, :], -1.0, ALU.mult)
            cx.tt("dve", G4b[0:T, :], G4b[0:T, :], G4[0:T, :], ALU.max)
            cx.act(G4b[0:T, :], G4b[0:T, :], AF.Exp, scale=-1.0)
            cx.act(G4b[0:T, :], G4b[0:T, :], AF.Ln, bias=ones_f[0:T, 0:1])
            cx.ts("dve", G4[0:T, :], G4[0:T, :], 0.0, ALU.max)
            cx.tt("dve", G4[0:T, :], G4[0:T, :], G4b[0:T, :], ALU.add)
            cx.tt("dve", GG[0:T, :], G4[0:T, :], negA[0:T, :], ALU.mult)
            cx.act(GBETA[0:T, :], ZAB[0:T, 260:264], AF.Sigmoid)
            p = ptmp(); cx.mm(p[0:T, 0:4], Umat[0:T, 0:T], GG[0:T, 0:4]); cx.cp("dve", GC[0:T, :], p[0:T, 0:4])
            p = ptmp(); cx.mm(p[0:128, 0:4], lastmg[g][0:T, :], GC[0:T, 0:4]); cx.cp("dve", GLB[:, :], p[0:128, 0:4])
            cx.act(EGC[0:T, :], GC[0:T, :], AF.Exp)
            cx.tt("dve", EKD[0:T, :], GLB[0:T, :], GC[0:T, :], ALU.subtract); cx.act(EKD[0:T, :], EKD[0:T, :], AF.Exp)
            cx.act(EGL[:, :], GLB[:, :], AF.Exp)
            po = pacc()
            nlev = int(round(math.log2(T)))
            for h in range(4):
                hs = slice(h * 64, (h + 1) * 64)
                cx.ts("dve", KBt[0:T, :], GK[0:T, hs], GBETA[0:T, h:h + 1], ALU.mult)
                cx.ts("pool", RU[0:T, :], GV[0:T, hs], GBETA[0:T, h:h + 1], ALU.mult)
                cx.ts("dve", KBE[0:T, :], KBt[0:T, :], EGC[0:T, h:h + 1], ALU.mult)
                cx.ts("pool", QD[0:T, :], GQ[0:T, hs], EGC[0:T, h:h + 1], ALU.mult)
                cx.ts("pool", KD[0:T, :], GK[0:T, hs], EKD[0:T, h:h + 1], ALU.mult)
                for srcv, dstv in ((GK[0:T, hs], gKT), (KBt[0:T, :], gKBT), (GQ[0:T, hs], gQT), (QD[0:T, :], gQDT)):
                    transpose_to(dstv[0:64, 0:T], srcv, T, 64)
                cx.ts("pool", GREP[0:T, 0:T], ones_f[0:T, 0:T], GG[0:T, h:h + 1], ALU.mult)
                pR = ptmp(); cx.mm(pR[0:T, 0:T], GREP[0:T, 0:T], Umat[0:T, 0:T])
                cx.ts("dve", LMU[0:T, 0:T], pR[0:T, 0:T], GC[0:T, h:h + 1], ALU.subtract, 0.0, ALU.min)
                cx.act(LMU[0:T, 0:T], LMU[0:T, 0:T], AF.Exp)
                cx.ts("dve", LML[0:T, 0:T], pR[0:T, 0:T], GC[0:T, h:h + 1], ALU.subtract, 0.0, ALU.max)
                cx.act(LML[0:T, 0:T], LML[0:T, 0:T], AF.Exp, scale=-1.0)
                cx.tt("pool", LMT[0:T, 0:T], LMU[0:T, 0:T], triu[0:T, 0:T], ALU.mult)
                cx.tt("pool", LMU[0:T, 0:T], LMU[0:T, 0:T], triu[0:T, 128:128 + T], ALU.mult)
                cx.tt("pool", LML[0:T, 0:T], LML[0:T, 0:T], tril[0:T, 0:T], ALU.mult)
                p = ptmp(); cx.mm(p[0:T, 0:T], gKT[0:64, 0:T], gQT[0:64, 0:T]); cx.tt("dve", GATT[0:T, 0:T], p[0:T, 0:T], LMT[0:T, 0:T], ALU.mult)
                P_, Q_ = PQ[0]
                p = ptmp(); cx.mm(p[0:T, 0:T], gKT[0:64, 0:T], gKBT[0:64, 0:T]); cx.tt("dve", P_[0:T, 0:T], p[0:T, 0:T], LMU[0:T, 0:T], ALU.mult)
                p = ptmp(); cx.mm(p[0:T, 0:T], gKBT[0:64, 0:T], gKT[0:64, 0:T]); cx.tt("dve", Q_[0:T, 0:T], p[0:T, 0:T], LML[0:T, 0:T], ALU.mult)
                cx.tt("pool", YM[0:T, 0:T], ident[0:T, 0:T], P_[0:T, 0:T], ALU.subtract)
                yield
                cur = 0
                for k in range(1, nlev):
                    P2, Q2 = PQ[1 - cur]
                    pP = ptmp(); cx.mm(pP[0:T, 0:T], Q_[0:T, 0:T], P_[0:T, 0:T]); cx.cp("dve", P2[0:T, 0:T], pP[0:T, 0:T])
                    pQ = ptmp(); cx.mm(pQ[0:T, 0:T], P_[0:T, 0:T], Q_[0:T, 0:T]); cx.cp("act", Q2[0:T, 0:T], pQ[0:T, 0:T])
                    pY = ptmp(); cx.mm(pY[0:T, 0:T], Q2[0:T, 0:T], YM[0:T, 0:T]); cx.tt("dve", YM[0:T, 0:T], YM[0:T, 0:T], pY[0:T, 0:T], ALU.add)
                    P_, Q_ = P2, Q2; cur = 1 - cur
                    yield
                pu = ptmp(); cx.mm(pu[0:T, 0:64], YM[0:T, 0:T], RU[0:T, 0:64]); cx.cp("act", USB[0:T, :], pu[0:T, 0:64])
                pw = ptmp(); cx.mm(pw[0:64, 0:T], KBE[0:T, 0:64], YM[0:T, 0:T]); cx.cp("dve", WT[0:64, 0:T], pw[0:64, 0:T])
                pv = ptmp(); cx.mm(pv[0:T, 0:64], WT[0:64, 0:T], SGs[0:64, h, :]); cx.tt("dve", VN[0:T, :], USB[0:T, :], pv[0:T, 0:64], ALU.subtract)
                cx.mm(po[0:T, hs], gQDT[0:64, 0:T], SGs[0:64, h, :], start=True, stop=False)
                cx.mm(po[0:T, hs], GATT[0:T, 0:T], VN[0:T, 0:64], start=False, stop=True)
                pS = ptmp(); cx.mm(pS[0:64, 0:64], KD[0:T, 0:64], VN[0:T, 0:64])
                cx.stt(SGs[0:64, h, :], SGs[0:64, h, :], EGL[0:64, h:h + 1], pS[0:64, 0:64], ALU.mult, ALU.add)
                yield
            cx.cp("act", OG[0:T, :], po[0:T, 0:256])
            cx.tt("pool", OTMP[0:T, :], OG[0:T, :], OG[0:T, :], ALU.mult)
            cx.red(G4[0:T, :], h3(OTMP, T))
            cx.ts("dve", G4[0:T, :], G4[0:T, :], 1.0 / 64, ALU.mult, EPS, ALU.add)
            cx.rsqrt(G4[0:T, :], G4[0:T, :])
            cx.tt("dve", h3(OG, T), h3(OG, T), bc4(G4[0:T, :], T), ALU.mult)
            cx.tt("pool", h3(OG, T), h3(OG, T), gnorm[0:T, :].unsq(1).bc([T, 4, 64]), ALU.mult)
            cx.act(SIL[0:T, :], ZAB[0:T, 0:256], AF.Silu)
            cx.tt("pool", OG[0:T, :], OG[0:T, :], SIL[0:T, :], ALU.mult)
            for c in range(2):
                transpose_to(BRT[:, 3, c, c0:c0 + T], OG[0:T, c * 128:(c + 1) * 128], T, 128)

        wv = w_in[l].re("(k p) n -> p k n", p=128)
        for j, hh in enumerate((0, 2, 1, 3)):
            cx.dma(W1[:, :, j * 64:(j + 1) * 64], wv[:, :, hh * 64:(hh + 1) * 64], q="pool")
        for c0_ in range(256, O_MG, 2048):
            c1_ = min(O_MG, c0_ + 2048)
            cx.dma(W1[:, :, c0_:c1_], wv[:, :, c0_:c1_], q="pool")
        load_layer_params(l)
        chk(3)
        for gi, grp in enumerate(groups):
            T = grp["T"]; s = grp["s"]; isp = grp["kind"] == "p"
            if isp:
                chk(6)
                cx.memset("pool", UB[:, :, 0:30], 0.0); cx.memset("pool", GB[:, :, 0:3], 0.0)
                cx.memset("pool", SR[:], 0.0); cx.memset("pool", SGs[:], 0.0); cx.memset("pool", VCf[:], 0.0)
            else:
                for c in range(2):
                    load_T(UB[:, c, 0:30], st_conv[l, s, :, c * 128:(c + 1) * 128], 30)
                for c in range(6):
                    load_T(GB[:, c, 0:3], st_gconv[l, s, :, c * 128:(c + 1) * 128], 3)
                cx.dma(SR[:], st_ret[l, s].re("h d e -> d h e")); cx.dma(SGs[:], st_gdn[l, s].re("h d e -> d h e"))
                sample_prep(l, s)
                chk(4)
            mlist = mts(grp, MT1 if isp else T)
            for (t0, N) in mlist:
                a0 = grp["tok0"] + t0
                load_x(l, grp, t0, N)
                if l == 0:
                    store_x(grp, t0, N, final=False)
                mod_norm(gi, N, 0, 1)
                for blk, dst in ((0, QA), (1, QB)):
                    cx.act(dst[:, 0:N], proj_fm(blk * 128, 128, N), AF.Identity, scale=0.125)
                pk_ = proj_fm(O_KV + 256, 128, N)
                if isp:
                    evac(KSEL[:, t0:t0 + N], pk_)
                else:
                    evac(KNEW[:, 0:N], pk_)
                pk_ = proj_fm(O_KV + 512, 128, N)
                slot = (t0 // 128) % 5 if isp else 4
                evac(KWIN[:, slot, 0:T], pk_)
                for c in range(2):
                    pg_ = proj_fm(O_CV + 256 + c * 128, 128, N)
                    cx.act(g1[:, 0:N], pg_, AF.Sigmoid)
                    pa_ = proj_fm(O_CV + c * 128, 128, N)
                    cx.tt("dve", UB[:, c, 30:30 + N], pa_, g1[:, 0:N], ALU.mult)
                for c in range(6):
                    evac(GB[:, c, 3:3 + N], proj_fm(O_GDN + c * 128, 128, N))
                if isp:
                    chk(7.1)
                pos = t0; j = 0
                evac(KVT[0:T, 0:512], proj_tm(O_KV, 512, j, T))
                evac(KVT[0:T, 512:780], proj_tm(O_KV + 512, 268, j, T))
                evac(RETT[0:T, 0:512], proj_tm(O_RET, 512, j, T))
                evac(RETT[0:T, 512:1024], proj_tm(O_RET + 512, 512, j, T))
                evac(ZAB[0:T, :], proj_tm(O_GDN + 768, 264, j, T))
                def gdn_chain():
                    yield from gdn_conv_mt(N)
                    yield from gdn_tile(grp, j)
                if isp:
                    cx.dma(kvp[l, pos:pos + T, :], KVT[0:T, 0:512])
                    if pos >= SEQ - 512:
                        cx.dma(winp[l, pos - (SEQ - 512):pos - (SEQ - 512) + T, :], KVT[0:T, 512:768])
                    run_gens([nsa_prompt_tile(l, pos // 128, j), gdn_chain(), retention_tile(grp, pos, j), conformer_mt(N)])
                else:
                    cx.dma(kvs[s, l, :, :], KVT[0:T, 0:512])
                    cx.dma(wins[l, s, 512 - TS:512, :], KVT[0:T, 512:768])
                    run_gens([nsa_sample_tile(l, s, j), gdn_chain(), retention_tile(grp, pos, j), conformer_mt(N)])
                cx.dma(brd[:, :, a0:a0 + N].re("c p n -> p c n"), BRT[:].re("p n c t -> p (n c) t")[:, :, 0:N])
                last = (t0 + N >= T * grp["ntile"])
                if last:
                    for c in range(2):
                        transpose_to(stg[0:30, :], UB[:, c, N:N + 30], 128, 30)
                        dst = convp[l] if isp else convs[l, s]
                        cx.dma(dst[:, c * 128:(c + 1) * 128], stg[0:30, :])
                    for c in range(6):
                        transpose_to(stg[0:3, :], GB[:, c, N:N + 3], 128, 3)
                        dst = gcp[l] if isp else gcs[l, s]
                        cx.dma(dst[:, c * 128:(c + 1) * 128], stg[0:3, :])
                    cx.dma((retp[l] if isp else rets[l, s]).re("h d e -> d h e"), SR[:])
                    cx.dma((gdnp[l] if isp else gdns[l, s]).re("h d e -> d h e"), SGs[:])
                else:
                    cx.cp("pool", UB[:, :, 0:30], UB[:, :, N:N + 30])
                    cx.cp("pool", GB[:, :, 0:3], GB[:, :, N:N + 3])
                chk(5 if not isp else 7)

    try:
        chk(0)
        build_tables()
        chk(1)
        cx.ar_reset()
        load_cT()
        for l in range(DEPTH):
            compute_mod(l)
            chk(2)
            phase1a(l)
            chk(8)
            phase1b(l)
            chk(9)
            phase2(l)
            chk(10)
    except StopBuild:
        pass
    cx.finish()
    return nc


def make_consts(cfg):
    SEQ, TS, PAST, NBP = cfg.SEQ, cfg.TS, cfg.PAST, cfg.NBP
    NBS = PAST // 64
    J0 = NBP - 2
    WLC = NBP + 16
    f = np.float32
    r = np.arange(128)
    c = {}
    c["c_ident"] = np.eye(128, dtype=f)
    c["c_triu"] = np.concatenate([(r[None, :] >= r[:, None]).astype(f), (r[None, :] > r[:, None]).astype(f)], axis=1)
    c["c_tril"] = (r[:, None] > r[None, :]).astype(f)
    c["c_U"] = (r[:, None] <= r[None, :]).astype(f)
    last = np.zeros((2, 128, 128), f); last[0, 127, :] = 1.0; last[1, TS - 1, :] = 1.0
    c["c_last"] = last
    half = 32
    inv = (np.float32(10000.0) ** (-np.arange(half, dtype=f) / f(half))).astype(f)

    def rot(pos):
        ang = pos.astype(f)[:, None] * inv[None, :]
        return np.concatenate([np.cos(ang), np.sin(ang)], axis=1).astype(f)
    c["c_rot_p"] = rot(np.arange(SEQ))
    c["c_rot_s"] = rot(PAST + np.arange(TS))
    lg = np.log1p(-np.exp2(-5.0 - np.arange(4, dtype=np.float64)))
    diff = (r[None, :] - r[:, None]).astype(np.float64)
    Dm = np.where(diff >= 0, np.exp(np.maximum(diff, 0.0)[None] * lg[:, None, None]), 0.0)
    c["c_retD"] = np.stack([Dm, Dm]).astype(f)
    xi = np.exp((r[None, :] + 1.0) * lg[:, None])
    c["c_retxi"] = np.broadcast_to(xi[None, :, None, :], (2, 4, 64, 128)).astype(f).copy()
    z = np.zeros((2, 128, 4), np.float64)
    for g, C in enumerate((128, TS)):
        t = np.arange(C)
        z[g, :C, :] = np.exp((C - 1.0 - t)[:, None] * lg[None, :]) / 8.0
    c["c_retz"] = z.astype(f)
    cc = np.arange(256)
    c["c_dbt"] = (r[:, None] + 128 - cc[None, :]).astype(f)
    j = np.arange(WLC)
    c["c_dlc"] = (r[:, None] - 63 + 64 * (J0 - j[None, :])).astype(f)
    mp = (j - J0)[None, :]
    ct = (r // 64)[:, None]
    forced = (mp == ct) | (mp == ct - 1)
    after = mp > ct
    keep = (~forced & ~after).astype(f)
    add = 2.0 * forced.astype(f) - after.astype(f)
    c["c_keep"] = np.concatenate([keep, add], axis=1).astype(f)
    W = NBS + 1
    ks = np.ones((8, W), f); ad = np.zeros((8, W), f)
    ks[:, NBS - 1:] = 0.0; ad[:, NBS - 1:] = 2.0
    c["c_keeps"] = np.concatenate([ks, ad], axis=1)
    c["c_wm4"] = (r[None, :] > r[:, None]).astype(f)
    c["c_iota"] = r.astype(f).reshape(128, 1)
    return c


_NC_CACHE = {}


def run_cfg(cfg, inputs, n_cores, n_prompt):
    key = (cfg.SEQ, cfg.NS, cfg.TS, cfg.PAST, cfg.NPOOL)
    if key not in _NC_CACHE:
        _NC_CACHE[key] = build(cfg)
    nc = _NC_CACHE[key]
    consts = make_consts(cfg)
    NS, TS, SEQ = cfg.NS, cfg.TS, cfg.SEQ
    A = lambda k: np.ascontiguousarray(np.asarray(inputs[k]))
    pool = A("cache_nsa_kv").reshape(-1, 256)
    wnames = ["w_ada", "b_ada", "norms", "w_in", "cmp_pool", "cmp_pe", "rel_bias", "conv_dw", "conv_dw_b", "conv_ln_g",
              "conv_ln_b", "ret_gn", "gdn_conv_w", "gdn_A_log", "gdn_dt_bias", "gdn_norm", "w_branch", "w_out", "ffn_up",
              "ffn_dw", "ffn_down"]
    shared = {k: A(k) for k in wnames}
    shared.update(consts)
    shared["pool"] = pool
    x_prompt, x_sample = A("x_prompt"), A("x_sample")
    c_prompt, c_sample = A("c_prompt"), A("c_sample")
    win = A("cache_nsa_win"); page_table = A("page_table")
    sc, sr, sgc, sg, sf = A("state_conv"), A("state_ret"), A("state_gdn_conv"), A("state_gdn"), A("state_ffn_conv")
    DB = x_sample.shape[0]
    in_maps = []
    for c in range(n_cores):
        bp = c % n_prompt
        ss = [(c * NS + i) % DB for i in range(NS)]
        m = dict(shared)
        m["xp"] = x_prompt[bp]; m["cp"] = c_prompt[bp:bp + 1]
        m["xs"] = x_sample[ss].reshape(NS * TS, D); m["cs"] = c_sample[ss]
        m["ptab"] = page_table[ss].astype(np.int32)
        m["win_in"] = np.ascontiguousarray(win[:, ss]).reshape(DEPTH, NS, 512, 256)
        m["st_conv"] = np.ascontiguousarray(sc[:, ss]); m["st_ret"] = np.ascontiguousarray(sr[:, ss])
        m["st_gconv"] = np.ascontiguousarray(sgc[:, ss]); m["st_gdn"] = np.ascontiguousarray(sg[:, ss])
        m["st_ffn"] = np.ascontiguousarray(sf[:, ss])
        in_maps.append(m)
    res = run_bass_kernel_spmd(nc, in_maps, core_ids=list(range(n_cores))).results
    B = n_prompt
    pc = [res[b] for b in range(B)]
    f = np.float32
    y_p = np.stack([pc[b]["yp"] for b in range(B)]).astype(f)
    kv_p = np.stack([pc[b]["kvp"] for b in range(B)]).reshape(B, DEPTH, SEQ, 4, 2, 64)
    win_p = np.stack([pc[b]["winp"] for b in range(B)], axis=1).reshape(DEPTH, B, 512, 2, 2, 64)
    conv_p = np.stack([pc[b]["convp"] for b in range(B)], axis=1)
    ret_p = np.stack([pc[b]["retp"] for b in range(B)], axis=1)
    gc_p = np.stack([pc[b]["gcp"] for b in range(B)], axis=1)
    gdn_p = np.stack([pc[b]["gdnp"] for b in range(B)], axis=1)
    ffn_p = np.stack([pc[b]["ffnp"] for b in range(B)], axis=1)
    ncs = DB // NS
    sc_ = [res[c] for c in range(ncs)]
    y_s = np.concatenate([r["ys"].reshape(NS, TS, D) for r in sc_], axis=0)
    kv_s = np.concatenate([r["kvs"] for r in sc_], axis=0).reshape(DB, DEPTH, TS, 4, 2, 64)
    cat1 = lambda k: np.concatenate([r[k] for r in sc_], axis=1)
    win_s = cat1("wins").reshape(DEPTH, DB, 512, 2, 2, 64)
    outs = (y_p, y_s, kv_p, kv_s, win_p, win_s, conv_p, cat1("convs"), ret_p, cat1("rets"), gc_p, cat1("gcs"),
            gdn_p, cat1("gdns"), ffn_p, cat1("ffns"))
    return tuple(np.ascontiguousarray(o, dtype=np.float32) for o in outs)


def kernel(**inputs):
    cfg = Cfg(NS=4)
    return run_cfg(cfg, inputs, n_cores=8, n_prompt=4)
```

```python
import math
import numpy as np
import concourse.bass as bass
import concourse.mybir as mybir
from concourse.bass_utils import run_bass_kernel_spmd

F32 = mybir.dt.float32
BF16 = mybir.dt.bfloat16
I32 = mybir.dt.int32
AF = mybir.ActivationFunctionType
ALU = mybir.AluOpType
AX = mybir.AxisListType

D = 1024
HD = 64
DEPTH = 2
NKV = 768
CONV_CH = 256
CONV_W = 31
D_FF = 2816
IN_COLS = 7700
EPS = 1e-6
NEG = -30000.0
O_Q, O_KV, O_GT, O_CV, O_RET, O_GDN, O_MG = 0, 256, 1024, 1036, 1548, 2572, 3604
GAM = [1.0 - 2.0 ** (-5.0 - h) for h in range(4)]


class Tile:
    def __init__(self, t, name):
        self.t = t
        self.name = name
        self.w = None
        self.r = {}
        self.psum = False

    def __getitem__(self, idx):
        return View(self, self.t[idx])

    def re(self, pat, **kw):
        return self[:].re(pat, **kw)

    def pbc(self, n):
        return self[:].pbc(n)


class View:
    def __init__(self, tile, ap):
        self.tile = tile
        self.ap = ap

    def __getitem__(self, idx):
        return View(self.tile, self.ap[idx])

    def re(self, pat, **kw):
        return View(self.tile, self.ap.rearrange(pat, **kw))

    def bc(self, shape):
        return View(self.tile, self.ap.to_broadcast(shape))

    def unsq(self, ax):
        return View(self.tile, self.ap.unsqueeze(ax))

    def pbc(self, n):
        return View(self.tile, self.ap.partition_broadcast(n))


def _ap(x):
    return x.ap if isinstance(x, View) else x


class Cx:
    def __init__(self, nc):
        self.nc = nc
        self.eng = dict(pe=nc.tensor, dve=nc.vector, act=nc.scalar, pool=nc.gpsimd, sp=nc.sync)
        self.sem = {k: nc.alloc_semaphore("cs_" + k) for k in ("pe", "dve", "act", "pool")}
        self.cnt = {k: 0 for k in self.sem}
        self.dsem = {}
        self.seen = {k: {} for k in self.eng}
        self.nops = 0

    def sb(self, name, shape, dt=F32):
        return Tile(self.nc.alloc_sbuf_tensor(name, list(shape), dt), name)

    def ps(self, name, shape, dt=F32):
        t = Tile(self.nc.alloc_psum_tensor(name, list(shape), dt), name)
        t.psum = True
        return t

    def dram(self, name, shape, dt=F32, kind="Internal"):
        return Tile(self.nc.dram_tensor(name, list(shape), dt, kind=kind), name)

    def _semobj(self, key):
        return self.sem[key[1]] if key[0] == "c" else self.dsem[key[1]]["sems"][key[2]]

    def _sync(self, e, reads, writes):
        deps = {}

        def add(kv):
            k, v = kv
            if deps.get(k, 0) < v:
                deps[k] = v

        for t in reads:
            if t.w is not None:
                add(t.w)
        for t in writes:
            if t.w is not None:
                add(t.w)
            for kv in t.r.items():
                add(kv)
        seen = self.seen[e]
        rt = set(id(t) for t in reads)
        for k, v in deps.items():
            if k == ("c", e) and e == "pe":
                continue
            if seen.get(k, 0) >= v:
                continue
            self.eng[e].wait_ge(self._semobj(k), v)
            seen[k] = v

    def op(self, e, reads, writes, fn):
        reads = [x.tile for x in reads if isinstance(x, View)]
        writes = [x.tile for x in writes if isinstance(x, View)]
        self._sync(e, reads, writes)
        self.cnt[e] += 1
        n = self.cnt[e]
        fn(self.eng[e]).then_inc(self.sem[e], 1)
        key = ("c", e)
        for t in reads:
            if t.psum:
                t.w = (key, n)
                t.r = {}
            elif t.r.get(key, 0) < n:
                t.r[key] = n
        for t in writes:
            t.w = (key, n)
            t.r = {}
        self.nops += 1

    RING = 36

    def dma(self, out, in_, q="sp", ch=None, indirect=None):
        ch = q
        if ch not in self.dsem:
            self.dsem[ch] = {"sems": [self.nc.alloc_semaphore("ds_%s%d" % (ch, i)) for i in range(self.RING)],
                             "cnt": [0] * self.RING, "n": 0}
        ring = self.dsem[ch]
        r = ring["n"] % self.RING
        ring["n"] += 1
        reads = [in_.tile]
        if indirect is not None:
            reads.append(indirect.tile)
        writes = [out.tile]
        self._sync(q, reads, writes)
        key = ("d", ch, r)
        if ring["cnt"][r] > 0 and self.seen[q].get(key, 0) < 16 * ring["cnt"][r]:
            self.eng[q].wait_ge(ring["sems"][r], 16 * ring["cnt"][r])
            self.seen[q][key] = 16 * ring["cnt"][r]
        ring["cnt"][r] += 1
        val = 16 * ring["cnt"][r]
        sem = ring["sems"][r]
        if indirect is None:
            self.eng[q].dma_start(out=out.ap, in_=in_.ap).then_inc(sem, 16)
        else:
            self.eng[q].indirect_dma_start(
                out=out.ap, out_offset=None, in_=in_.ap,
                in_offset=bass.IndirectOffsetOnAxis(ap=indirect.ap, axis=0)).then_inc(sem, 16)
        for t in reads:
            if t.r.get(key, 0) < val:
                t.r[key] = val
        for t in writes:
            t.w = (key, val)
            t.r = {}
        self.nops += 1

    def _all_sems(self):
        items = [(("c", k), self.sem[k], self.cnt[k]) for k in self.sem if self.cnt[k]]
        for ch, ring in self.dsem.items():
            for r in range(self.RING):
                if ring["cnt"][r]:
                    items.append((("d", ch, r), ring["sems"][r], 16 * ring["cnt"][r]))
        return items

    def barrier(self):
        items = self._all_sems()
        for e in self.eng:
            for key, s, v in items:
                if self.seen[e].get(key, 0) >= v:
                    continue
                self.eng[e].wait_ge(s, v)
                self.seen[e][key] = v

    def arena(self, nbytes):
        self.ar = self.nc.alloc_sbuf_tensor("arena", [128, nbytes // 4], F32)
        self.ar_bytes = nbytes
        self.ar_off = 0

    def ar_reset(self):
        self.barrier()
        self.ar_off = 0

    def ar_alloc(self, name, shape, dt=F32):
        esz = 4 if dt in (F32, I32) else 2
        n = 1
        for d_ in shape[1:]:
            n *= d_
        nb = (n * esz + 31) // 32 * 32
        assert self.ar_off + nb <= self.ar_bytes, ("arena overflow", name, self.ar_off, nb, self.ar_bytes)
        ap = self.ar[0:shape[0], self.ar_off // 4:(self.ar_off + nb) // 4]
        if dt != F32:
            ap = ap.bitcast(dt)
        ap = ap[:, 0:n]
        if len(shape) > 2:
            names = " ".join("d%d" % i for i in range(1, len(shape)))
            kw = {"d%d" % i: shape[i] for i in range(1, len(shape))}
            ap = ap.rearrange("p (%s) -> p %s" % (names, names), **kw)
        self.ar_off += nb
        return Tile(ap, name)

    def finish(self):
        for key, sem, v in self._all_sems():
            self.nc.sync.wait_ge(sem, v)

    def mm(self, out, lhsT, rhs, start=True, stop=True):
        self.op("pe", [lhsT, rhs], [out],
                lambda e: e.matmul(out.ap, lhsT=lhsT.ap, rhs=rhs.ap, start=start, stop=stop))

    def tr(self, out, in_, ident):
        self.op("pe", [in_, ident], [out], lambda e: e.transpose(out.ap, in_.ap, ident.ap))

    def act(self, out, in_, func, bias=None, scale=1.0, accum=None):
        rd = [in_] + ([bias] if isinstance(bias, View) else [])
        wr = [out] + ([accum] if accum is not None else [])
        kw = {}
        if bias is not None:
            kw["bias"] = _ap(bias)
        if accum is not None:
            kw["accum_out"] = accum.ap
        self.op("act", rd, wr, lambda e: e.activation(out=out.ap, in_=in_.ap, func=func, scale=scale, **kw))

    def tt(self, e, out, in0, in1, op):
        self.op(e, [in0, in1], [out], lambda g: g.tensor_tensor(out=out.ap, in0=in0.ap, in1=in1.ap, op=op))

    def ts(self, e, out, in0, s1, op0, s2=None, op1=None):
        rd = [in0] + [s for s in (s1, s2) if isinstance(s, View)]
        if op1 is None:
            self.op(e, rd, [out], lambda g: g.tensor_scalar(out=out.ap, in0=in0.ap, scalar1=_ap(s1), scalar2=None, op0=op0))
        else:
            self.op(e, rd, [out], lambda g: g.tensor_scalar(out=out.ap, in0=in0.ap, scalar1=_ap(s1), scalar2=_ap(s2), op0=op0, op1=op1))

    def stt(self, out, in0, scalar, in1, op0, op1):
        rd = [in0, in1] + ([scalar] if isinstance(scalar, View) else [])
        self.op("dve", rd, [out], lambda g: g.scalar_tensor_tensor(out=out.ap, in0=in0.ap, scalar=_ap(scalar), in1=in1.ap, op0=op0, op1=op1))

    def rsqrt(self, out, in_):
        self.act(out, in_, AF.Ln)
        self.act(out, out, AF.Exp, scale=-0.5)

    def cp(self, e, out, in_):
        if e == "act":
            self.act(out, in_, AF.Copy)
        else:
            self.op(e, [in_], [out], lambda g: g.tensor_copy(out=out.ap, in_=in_.ap))

    def memset(self, e, out, val):
        self.op(e, [], [out], lambda g: g.memset(out.ap, val))

    def red(self, out, in_, op=ALU.add):
        self.op("dve", [in_], [out], lambda g: g.tensor_reduce(out=out.ap, in_=in_.ap, axis=AX.X, op=op))

    def max8(self, out, in_):
        self.op("dve", [in_], [out], lambda g: g.max(out=out.ap, in_=in_.ap))

    def match_replace(self, out, rep, vals, imm):
        self.op("dve", [rep, vals], [out], lambda g: g.match_replace(out=out.ap, in_to_replace=rep.ap, in_values=vals.ap, imm_value=imm))


def t5_thresholds():
    n = np.arange(0, 400)
    exact = 16
    large = exact + (np.log(np.maximum(n, 1).astype(np.float32) / np.float32(exact)) / np.float32(math.log(128 / exact))
                     * np.float32(32 - exact)).astype(np.int32)
    b = np.where(n < exact, n, np.minimum(large, 31))
    thr = []
    for k in range(1, 32):
        thr.append(int(np.min(n[b >= k])))
    return thr


class StopBuild(Exception):
    pass


class Cfg:
    stop = None

    def __init__(self, SEQ=4096, NS=4, TS=8, PAST=16384, NPOOL=5120):
        self.SEQ, self.NS, self.TS, self.PAST, self.NPOOL = SEQ, NS, TS, PAST, NPOOL
        self.NPG = PAST // 128
        self.NT = SEQ // 128
        self.NBP = SEQ // 64
        self.WIN = 512


def build(cfg):
    nc = bass.Bass("TRN2", target_bir_lowering=False)
    cx = Cx(nc)
    SEQ, NS, TS, PAST, NPG, NT, NBP = cfg.SEQ, cfg.NS, cfg.TS, cfg.PAST, cfg.NPG, cfg.NT, cfg.NBP
    NSTOK = NS * TS
    J0 = NBP - 2
    NBS = PAST // 64
    WLC = NBP + 16
    NTOK = SEQ + NSTOK
    NR2 = cfg.NPOOL * DEPTH * 128 * 2

    def chk(n):
        if cfg.stop is not None and cfg.stop == n:
            raise StopBuild()

    def din(name, shape, dt=F32):
        return cx.dram(name, shape, dt, kind="ExternalInput")

    def dout(name, shape, dt=F32):
        return cx.dram(name, shape, dt, kind="ExternalOutput")

    xp = din("xp", [SEQ, D]); cpd = din("cp", [1, D]); xs = din("xs", [NSTOK, D]); csd = din("cs", [NS, D])
    pool_d = din("pool", [NR2, 256])
    ptab = din("ptab", [NS, NPG], I32)
    win_in = din("win_in", [DEPTH, NS, 512, 256])
    st_conv = din("st_conv", [DEPTH, NS, 30, 256]); st_ret = din("st_ret", [DEPTH, NS, 4, 64, 64])
    st_gconv = din("st_gconv", [DEPTH, NS, 3, 768]); st_gdn = din("st_gdn", [DEPTH, NS, 4, 64, 64])
    st_ffn = din("st_ffn", [DEPTH, NS, 2, D_FF])
    w_ada = din("w_ada", [DEPTH, D, 6 * D]); b_ada = din("b_ada", [DEPTH, 6 * D]); norms = din("norms", [DEPTH, 4, D])
    w_in = din("w_in", [DEPTH, D, IN_COLS]); cmp_pool = din("cmp_pool", [DEPTH, 2, 64]); cmp_pe = din("cmp_pe", [DEPTH, 2, 64, 64])
    rel_bias = din("rel_bias", [32, 4]); conv_dw = din("conv_dw", [DEPTH, 31, 256]); conv_dw_b = din("conv_dw_b", [DEPTH, 256])
    conv_ln_g = din("conv_ln_g", [DEPTH, 256]); conv_ln_b = din("conv_ln_b", [DEPTH, 256]); ret_gn = din("ret_gn", [DEPTH, 256])
    gdn_conv_w = din("gdn_conv_w", [DEPTH, 4, 768]); gdn_A_log = din("gdn_A_log", [DEPTH, 4]); gdn_dt_bias = din("gdn_dt_bias", [DEPTH, 4])
    gdn_norm = din("gdn_norm", [DEPTH, 64]); w_branch = din("w_branch", [DEPTH, 4, 256, D]); w_out = din("w_out", [DEPTH, D, D])
    ffn_up = din("ffn_up", [DEPTH, D, 2 * D_FF]); ffn_dw = din("ffn_dw", [DEPTH, 3, D_FF]); ffn_down = din("ffn_down", [DEPTH, D_FF, D])
    c_ident = din("c_ident", [128, 128]); c_triu = din("c_triu", [128, 256]); c_tril = din("c_tril", [128, 128])
    c_U = din("c_U", [128, 128]); c_last = din("c_last", [2, 128, 128])
    c_rot_p = din("c_rot_p", [SEQ, 64]); c_rot_s = din("c_rot_s", [TS, 64])
    c_retD = din("c_retD", [2, 4, 128, 128]); c_retxi = din("c_retxi", [2, 4, 64, 128]); c_retz = din("c_retz", [2, 128, 4])
    c_dbt = din("c_dbt", [128, 256]); c_dlc = din("c_dlc", [128, WLC]); c_keep = din("c_keep", [128, 2 * WLC])
    c_keeps = din("c_keeps", [8, 2 * (NBS + 1)])
    c_wm4 = din("c_wm4", [128, 128]); c_iota = din("c_iota", [128, 1])

    yp = dout("yp", [SEQ, D]); ys = dout("ys", [NSTOK, D])
    kvp = dout("kvp", [DEPTH, SEQ, 512]); kvs = dout("kvs", [NS, DEPTH, TS, 512])
    winp = dout("winp", [DEPTH, 512, 256]); wins = dout("wins", [DEPTH, NS, 512, 256])
    convp = dout("convp", [DEPTH, 30, 256]); convs = dout("convs", [DEPTH, NS, 30, 256])
    retp = dout("retp", [DEPTH, 4, 64, 64]); rets = dout("rets", [DEPTH, NS, 4, 64, 64])
    gcp = dout("gcp", [DEPTH, 3, 768]); gcs = dout("gcs", [DEPTH, NS, 3, 768])
    gdnp = dout("gdnp", [DEPTH, 4, 64, 64]); gdns = dout("gdns", [DEPTH, NS, 4, 64, 64])
    ffnp = dout("ffnp", [DEPTH, 2, D_FF]); ffns = dout("ffns", [DEPTH, NS, 2, D_FF])

    xres = cx.dram("xres", [8, 128, NTOK])
    brd = cx.dram("brd", [8, 128, NTOK], BF16)
    bt_d = cx.dram("bt_d", [128, 4 * 256]); lc_d = cx.dram("lc_d", [128, 4 * WLC])

    MTN = 256
    ident = cx.sb("ident", [128, 128]); cx.dma(ident[:], c_ident[:])
    ones_b = cx.sb("ones_b", [128, 128], BF16); cx.memset("dve", ones_b[:], 1.0)
    ones_f = cx.sb("ones_f", [128, 128]); cx.memset("dve", ones_f[:], 1.0)
    rb = cx.sb("rb", [128, 128]); cx.dma(rb[:], rel_bias.re("b h -> (b h)").pbc(128))
    c31 = rb[:, 124:128]
    xT = cx.sb("xT", [128, 8, MTN]); hT = cx.sb("hT", [128, 8, MTN], BF16)
    sqb = cx.sb("sqb", [128, MTN], BF16); rstd = cx.sb("rstd", [128, MTN]); tmpN = cx.sb("tmpN", [128, MTN])
    tmpN2 = cx.sb("tmpN2", [128, MTN]); tmpNs = [tmpN, tmpN2]
    xtok = cx.sb("xtok", [128, D]); stg = cx.sb("stg", [128, 128]); sttok = cx.sb("sttok", [32, 128])
    g1 = cx.sb("g1", [128, MTN]); g2 = cx.sb("g2", [128, MTN]); g3 = cx.sb("g3", [128, MTN])
    NG = NS + 1
    modT = cx.sb("modT", [128, 48, NG]); cT = cx.sb("cT", [128, 8, NG], BF16); nrm = cx.sb("nrm", [128, 4, 8])
    badaT = cx.sb("badaT", [128, 48]); mv = cx.sb("mv", [128, NG, 6, 8])
    cx.arena(178 * 1024)

    ps_tmp = [cx.ps("pt%d" % i, [128, 512]) for i in range(5)]
    ps_acc = [cx.ps("pa%d" % i, [128, 512]) for i in range(3)]
    rr = {"tmp": 0, "acc": 0, "ev": 0, "eb": 0}

    def ptmp():
        rr["tmp"] = (rr["tmp"] + 1) % len(ps_tmp)
        return ps_tmp[rr["tmp"]]

    def pacc():
        rr["acc"] = (rr["acc"] + 1) % len(ps_acc)
        return ps_acc[rr["acc"]]

    def evac(out, in_):
        rr["ev"] ^= 1
        cx.cp("dve" if rr["ev"] else "act", out, in_)

    def transpose_to(out_sb, in_sb, rows, cols):
        p = ptmp()
        cx.tr(p[0:cols, 0:rows], in_sb, ident[0:rows, 0:rows])
        evac(out_sb, p[0:cols, 0:rows])

    def load_T(dst, src2d, r):
        cx.dma(stg[0:r, :], src2d)
        transpose_to(dst, stg[0:r, :], r, 128)

    def wview(tile, kc, n):
        return tile[:].re("p (k n) -> p k n", k=kc)

    def load_w(dst, src2d, ncols, chunk=2048):
        v = src2d.re("(k p) n -> p k n", p=128)
        for c0 in range(0, ncols, chunk):
            c1 = min(ncols, c0 + chunk)
            cx.dma(dst[:, :, c0:c1], v[:, :, c0:c1], q="pool")

    groups = [dict(kind="s", s=s, T=TS, ntile=1, tok0=SEQ + s * TS, gi=1) for s in range(NS)]
    groups.append(dict(kind="p", s=0, T=128, ntile=NT, tok0=0, gi=0))

    thr = t5_thresholds()

    def build_tables():
        cx.ar_reset()
        rbd = cx.ar_alloc("rbd", [128, 128]); dbt = cx.ar_alloc("dbt", [128, 256]); dlc = cx.ar_alloc("dlc", [128, WLC])
        BTt = cx.ar_alloc("BTt", [128, 4, 256]); LCt = cx.ar_alloc("LCt", [128, 4, WLC]); btmp = cx.ar_alloc("btmp", [128, 256])
        cx.tt("dve", rbd[:, 4:128], rb[:, 4:128], rb[:, 0:124], ALU.subtract)
        cx.dma(dbt[:], c_dbt[:]); cx.dma(dlc[:], c_dlc[:])
        chk(0.1)
        for (tab, dtab, W) in ((BTt, dbt, 256), (LCt, dlc, WLC)):
            for h in range(4):
                dst = tab[:, h, :]
                cx.ts("dve", dst, dtab[:, 0:W], 0.0, ALU.is_lt, NEG, ALU.mult)
                cx.ts("dve", dst, dst, rb[:, h:h + 1], ALU.add)
                for b in range(1, 32):
                    cx.ts("dve", btmp[:, 0:W], dtab[:, 0:W], float(thr[b - 1]), ALU.is_ge, rbd[:, 4 * b + h:4 * b + h + 1], ALU.mult)
                    cx.tt("dve", dst, dst, btmp[:, 0:W], ALU.add)
                chk(0.2)
        chk(0.3)
        cx.dma(bt_d[:], BTt[:].re("p h w -> p (h w)")); cx.dma(lc_d[:], LCt[:].re("p h w -> p (h w)"))

    def load_cT():
        ctok = cx.ar_alloc("ctok", [16, D])
        cx.dma(ctok[0:NS, :], csd[:]); cx.dma(ctok[NS:NS + 1, :], cpd[:])
        cx.act(ctok[0:NG, :], ctok[0:NG, :], AF.Silu)
        for kc in range(8):
            p = ptmp()
            cx.tr(p[:, 0:NG], ctok[0:NG, kc * 128:(kc + 1) * 128], ident[0:NG, 0:NG])
            evac(cT[:, kc, :], p[:, 0:NG])

    def compute_mod(l):
        cx.ar_reset()
        adaw = [wview(cx.ar_alloc("adaw%d" % i, [128, 8 * 512], BF16), 8, 512) for i in range(2)]
        cx.dma(stg[0:48, :], b_ada[l].re("(c p) -> c p", p=128))
        transpose_to(badaT[:, 0:48], stg[0:48, :], 48, 128)
        cx.dma(stg[0:32, :], norms[l].re("j (c p) -> (j c) p", p=128))
        transpose_to(nrm[:].re("p j c -> p (j c)"), stg[0:32, :], 32, 128)
        for g in range(12):
            wt = adaw[g % 2]
            cx.dma(wt, w_ada[l].re("(kc p) n -> p kc n", p=128)[:, :, g * 512:(g + 1) * 512], q="pool")
            for b in range(4):
                p = ptmp()
                for kc in range(8):
                    cx.mm(p[:, 0:NG], wt[:, kc, b * 128:(b + 1) * 128], cT[:, kc, :], start=(kc == 0), stop=(kc == 7))
                blk = g * 4 + b
                cx.ts("dve", modT[:, blk, :], p[:, 0:NG], badaT[:, blk:blk + 1], ALU.add)
        for g in range(NG):
            m = lambda j: modT[:, j * 8:(j + 1) * 8, g]
            cx.stt(mv[:, g, 0, :], m(1), 1.0, nrm[:, 0, :], ALU.add, ALU.mult)
            cx.cp("dve", mv[:, g, 1, :], m(0))
            cx.tt("dve", mv[:, g, 2, :], m(2), nrm[:, 1, :], ALU.mult)
            cx.stt(mv[:, g, 3, :], m(4), 1.0, nrm[:, 2, :], ALU.add, ALU.mult)
            cx.cp("dve", mv[:, g, 4, :], m(3))
            cx.tt("dve", mv[:, g, 5, :], m(5), nrm[:, 3, :], ALU.mult)

    def rms_rstd(src, N, nchunk=8):
        p = ptmp()
        for kc in range(nchunk):
            cx.act(sqb[:, 0:N], src[:, kc, 0:N], AF.Square)
            cx.mm(p[:, 0:N], ones_b[:, :], sqb[:, 0:N], start=(kc == 0), stop=(kc == nchunk - 1))
        cx.ts("dve", rstd[:, 0:N], p[:, 0:N], 1.0 / D, ALU.mult, EPS, ALU.add)
        cx.rsqrt(rstd[:, 0:N], rstd[:, 0:N])

    def mod_norm(g, N, jg, js):
        rms_rstd(xT, N)
        for kc in range(8):
            tn = tmpNs[kc % 2]
            cx.stt(tn[:, 0:N], xT[:, kc, 0:N], mv[:, g, jg, kc:kc + 1], rstd[:, 0:N], ALU.mult, ALU.mult)
            cx.act(hT[:, kc, 0:N], tn[:, 0:N], AF.Identity, bias=mv[:, g, js, kc:kc + 1])

    def resid_add(g, N, src, jg):
        rms_rstd(src, N)
        for kc in range(8):
            tn = tmpNs[kc % 2]
            cx.stt(tn[:, 0:N], src[:, kc, 0:N], mv[:, g, jg, kc:kc + 1], rstd[:, 0:N], ALU.mult, ALU.mult)
            cx.tt("pool", xT[:, kc, 0:N], xT[:, kc, 0:N], tn[:, 0:N], ALU.add)

    def load_x(l, grp, t0, N):
        if l == 0:
            src = xp if grp["kind"] == "p" else xs
            base = t0 if grp["kind"] == "p" else grp["s"] * TS + t0
            for j in range(0, N, 128):
                n = min(128, N - j)
                cx.dma(xtok[0:n, :], src[base + j:base + j + n, :])
                for kc in range(8):
                    p = ptmp()
                    cx.tr(p[:, 0:n], xtok[0:n, kc * 128:(kc + 1) * 128], ident[0:n, 0:n])
                    evac(xT[:, kc, j:j + n], p[:, 0:n])
        else:
            a = grp["tok0"] + t0
            cx.dma(xT[:, :, 0:N], xres[:, :, a:a + N].re("c p n -> p c n"))

    def store_x(grp, t0, N, final):
        a = grp["tok0"] + t0
        if not final:
            cx.dma(xres[:, :, a:a + N].re("c p n -> p c n"), xT[:, :, 0:N])
        else:
            dst = yp if grp["kind"] == "p" else ys
            base = t0 if grp["kind"] == "p" else grp["s"] * TS + t0
            for j in range(0, N, 128):
                n = min(128, N - j)
                for kc in range(8):
                    p = ptmp()
                    cx.tr(p[0:n, 0:128], xT[:, kc, j:j + n], ident[:, :])
                    evac(xtok[0:n, kc * 128:(kc + 1) * 128], p[0:n, 0:128])
                cx.dma(dst[base + j:base + j + n, :], xtok[0:n, :])

    def run_gens(gens):
        live = list(gens)
        while live:
            for g in list(live):
                try:
                    next(g)
                except StopIteration:
                    live.remove(g)

    def mts(grp, mtn):
        ntok = grp["T"] * grp["ntile"]
        return [(t0, min(mtn, ntok - t0)) for t0 in range(0, ntok, mtn)]

    def phase2(l):
        cx.ar_reset()
        W_UP = wview(cx.ar_alloc("W_UP", [128, 8 * 5632], BF16), 8, 5632)
        W_DN = wview(cx.ar_alloc("W_DN", [128, 22 * 1024], BF16), 22, 1024)
        ffw = cx.ar_alloc("ffw", [128, 22, 3]); ffhalo = cx.ar_alloc("ffhalo", [128, 22, 2])
        ffts = [cx.ar_alloc("fft%d" % k, [128, 2 + MTN]) for k in range(2)]
        gsets = [(g1, g2, g3), tuple(cx.ar_alloc("gx%d" % k, [128, MTN]) for k in range(3))]
        actT = cx.ar_alloc("actT", [128, 22, MTN], BF16); fT = cx.ar_alloc("fT", [128, 8, MTN])
        load_w(W_UP, ffn_up[l], 5632); load_w(W_DN, ffn_down[l], 1024)
        for c in range(22):
            load_T(ffw[:, c, :], ffn_dw[l, :, c * 128:(c + 1) * 128], 3)
        for gi, grp in enumerate(groups):
            if grp["kind"] == "p":
                cx.memset("pool", ffhalo[:], 0.0)
            else:
                for c in range(22):
                    cx.dma(sttok[0:2, :], st_ffn[l, grp["s"], :, c * 128:(c + 1) * 128])
                    transpose_to(ffhalo[:, c, :], sttok[0:2, :], 2, 128)
            for (t0, N) in mts(grp, MTN):
                load_x(1, grp, t0, N)
                mod_norm(gi, N, 3, 4)
                for c in range(22):
                    pg = ptmp()
                    for kc in range(8):
                        cx.mm(pg[:, 0:N], W_UP[:, kc, c * 128:(c + 1) * 128], hT[:, kc, 0:N], start=(kc == 0), stop=(kc == 7))
                    pv = ptmp()
                    for kc in range(8):
                        cx.mm(pv[:, 0:N], W_UP[:, kc, D_FF + c * 128:D_FF + (c + 1) * 128], hT[:, kc, 0:N], start=(kc == 0), stop=(kc == 7))
                    fft_ = ffts[c % 2]; ga, gb, gc_ = gsets[c % 2]
                    cx.cp("pool", fft_[:, 0:2], ffhalo[:, c, :])
                    cx.cp("act", fft_[:, 2:2 + N], pg[:, 0:N])
                    cx.cp("pool", ffhalo[:, c, :], fft_[:, N:N + 2])
                    cx.ts("dve", ga[:, 0:N], fft_[:, 0:N], ffw[:, c, 0:1], ALU.mult)
                    cx.stt(ga[:, 0:N], fft_[:, 1:1 + N], ffw[:, c, 1:2], ga[:, 0:N], ALU.mult, ALU.add)
                    cx.stt(ga[:, 0:N], fft_[:, 2:2 + N], ffw[:, c, 2:3], ga[:, 0:N], ALU.mult, ALU.add)
                    cx.tt("pool", gb[:, 0:N], ga[:, 0:N], ga[:, 0:N], ALU.mult)
                    cx.ts("dve", gb[:, 0:N], gb[:, 0:N], 0.044715, ALU.mult, 1.0, ALU.add)
                    cx.tt("pool", gb[:, 0:N], gb[:, 0:N], ga[:, 0:N], ALU.mult)
                    cx.act(gc_[:, 0:N], gb[:, 0:N], AF.Sigmoid, scale=1.5957691216)
                    cx.tt("dve", gc_[:, 0:N], gc_[:, 0:N], ga[:, 0:N], ALU.mult)
                    cx.tt("dve", actT[:, c, 0:N], gc_[:, 0:N], pv[:, 0:N], ALU.mult)
                for ob in range(8):
                    p = ptmp()
                    for c in range(22):
                        cx.mm(p[:, 0:N], W_DN[:, c, ob * 128:(ob + 1) * 128], actT[:, c, 0:N], start=(c == 0), stop=(c == 21))
                    evac(fT[:, ob, 0:N], p[:, 0:N])
                resid_add(gi, N, fT, 5)
                store_x(grp, t0, N, final=(l == DEPTH - 1))
            dst = ffnp[l] if grp["kind"] == "p" else ffns[l, grp["s"]]
            for c in range(22):
                transpose_to(sttok[0:2, :], ffhalo[:, c, :], 128, 2)
                cx.dma(dst[:, c * 128:(c + 1) * 128], sttok[0:2, :])

    def phase1b(l):
        cx.ar_reset()
        W_MG = wview(cx.ar_alloc("W_MG", [128, 8 * 4096], BF16), 8, 4096)
        W_BR = wview(cx.ar_alloc("W_BR", [128, 8 * 1024], BF16), 8, 1024)
        W_OUT = wview(cx.ar_alloc("W_OUT", [128, 8 * 1024], BF16), 8, 1024)
        BRT = cx.ar_alloc("BRTb", [128, 8, MTN], BF16); mT = cx.ar_alloc("mT", [128, 8, MTN], BF16); fT = cx.ar_alloc("fTb", [128, 8, MTN])
        gacc = [g1, cx.ar_alloc("gacc1", [128, MTN])]
        ggat = [(g2, g3), (cx.ar_alloc("ggat2", [128, MTN]), cx.ar_alloc("ggat3", [128, MTN]))]
        wv = w_in[l].re("(k p) n -> p k n", p=128)
        for c0 in range(0, 4096, 2048):
            cx.dma(W_MG[:, :, c0:c0 + 2048], wv[:, :, O_MG + c0:O_MG + c0 + 2048], q="pool")
        load_w(W_BR, w_branch[l].re("n k d -> (n k) d"), 1024)
        load_w(W_OUT, w_out[l], 1024)
        for gi, grp in enumerate(groups):
            for (t0, N) in mts(grp, MTN):
                a0 = grp["tok0"] + t0
                load_x(1, grp, t0, N)
                cx.dma(BRT[:, :, 0:N], brd[:, :, a0:a0 + N].re("c p n -> p c n"))
                mod_norm(gi, N, 0, 1)
                for ob in range(8):
                    for n in range(4):
                        pb = ptmp()
                        for kc in range(2):
                            cx.mm(pb[:, 0:N], W_BR[:, n * 2 + kc, ob * 128:(ob + 1) * 128], BRT[:, n * 2 + kc, 0:N], start=(kc == 0), stop=(kc == 1))
                        pg = ptmp()
                        for kc in range(8):
                            cx.mm(pg[:, 0:N], W_MG[:, kc, n * 1024 + ob * 128:n * 1024 + (ob + 1) * 128], hT[:, kc, 0:N], start=(kc == 0), stop=(kc == 7))
                        ga = gacc[ob % 2]; gb, gc_ = ggat[n % 2]
                        cx.act(gb[:, 0:N], pg[:, 0:N], AF.Sigmoid)
                        if n == 0:
                            cx.tt("dve", ga[:, 0:N], gb[:, 0:N], pb[:, 0:N], ALU.mult)
                        else:
                            cx.tt("dve", gc_[:, 0:N], gb[:, 0:N], pb[:, 0:N], ALU.mult)
                            cx.tt("pool", ga[:, 0:N], ga[:, 0:N], gc_[:, 0:N], ALU.add)
                    cx.cp("act", mT[:, ob, 0:N], gacc[ob % 2][:, 0:N])
                for ob in range(8):
                    p = ptmp()
                    for kc in range(8):
                        cx.mm(p[:, 0:N], W_OUT[:, kc, ob * 128:(ob + 1) * 128], mT[:, kc, 0:N], start=(kc == 0), stop=(kc == 7))
                    evac(fT[:, ob, 0:N], p[:, 0:N])
                resid_add(gi, N, fT, 2)
                store_x(grp, t0, N, final=False)


    def phase1a(l):
        cx.ar_reset()
        W1 = wview(cx.ar_alloc("W1", [128, 8 * O_MG], BF16), 8, O_MG)
        triu = cx.ar_alloc("triu", [128, 256]); cx.dma(triu[:], c_triu[:])
        tril = cx.ar_alloc("tril", [128, 128]); cx.dma(tril[:], c_tril[:])
        Umat = cx.ar_alloc("Umat", [128, 128]); cx.dma(Umat[:], c_U[:])
        lastmg = [cx.ar_alloc("lastm%d" % g, [128, 128]) for g in range(2)]
        for g in range(2):
            cx.dma(lastmg[g][:], c_last[g])
        wm4 = cx.ar_alloc("wm4", [128, 128]); cx.dma(wm4[:], c_wm4[:])
        iota = cx.ar_alloc("iota", [128, 1]); cx.dma(iota[:], c_iota[:])
        retDg = [cx.ar_alloc("retD0", [128, 4, 128]), cx.ar_alloc("retD1", [8, 4, 8])]
        cx.dma(retDg[0][:], c_retD[0].re("h s c -> s h c")); cx.dma(retDg[1][:], c_retD[1, :, 0:8, 0:8].re("h s c -> s h c"))
        retxig = [cx.ar_alloc("retxi0", [64, 4, 128]), cx.ar_alloc("retxi1", [64, 4, 8])]
        cx.dma(retxig[0][:], c_retxi[0].re("h d c -> d h c")); cx.dma(retxig[1][:], c_retxi[1, :, :, 0:8].re("h d c -> d h c"))
        retzg = [cx.ar_alloc("retz0", [128, 4]), cx.ar_alloc("retz1", [128, 4])]
        cx.dma(retzg[0][:], c_retz[0]); cx.dma(retzg[1][:], c_retz[1])
        BT = cx.ar_alloc("BT", [128, 4, 256]); cx.dma(BT[:].re("p h w -> p (h w)"), bt_d[:])
        LC = cx.ar_alloc("LC", [128, 4, WLC]); cx.dma(LC[:].re("p h w -> p (h w)"), lc_d[:])
        keepadd = cx.ar_alloc("keepadd", [128, 2 * WLC]); cx.dma(keepadd[:], c_keep[:])
        MT1 = 128
        convw = cx.ar_alloc("convw", [128, 2, 31]); cvp = cx.ar_alloc("cvp", [128, 6]); gdw = cx.ar_alloc("gdw", [128, 6, 4])
        gnb = cx.ar_alloc("gnb", [128, 256]); gnorm = cx.ar_alloc("gnorm", [128, 64]); negA = cx.ar_alloc("negA", [128, 4]); dtb = cx.ar_alloc("dtb", [128, 4])
        PEK = cx.ar_alloc("PEK", [128, 128]); PEV = cx.ar_alloc("PEV", [128, 128])
        plf = cx.ar_alloc("plf", [128, 258]); POOLK2 = cx.ar_alloc("POOLK2", [128, 2], BF16); BIGV = cx.ar_alloc("BIGV", [128, 256], BF16)

        def load_layer_params(l):
            for c in range(2):
                load_T(convw[:, c, :], conv_dw[l, :, c * 128:(c + 1) * 128], 31)
            for j, src in enumerate((conv_dw_b, conv_ln_g, conv_ln_b)):
                cx.dma(stg[2 * j:2 * j + 2, :], src[l].re("(c p) -> c p", p=128))
            transpose_to(cvp[:, 0:6], stg[0:6, :], 6, 128)
            for c in range(6):
                load_T(gdw[:, c, :], gdn_conv_w[l, :, c * 128:(c + 1) * 128], 4)
            cx.dma(gnb[:], ret_gn[l].pbc(128)); cx.dma(gnorm[:], gdn_norm[l].pbc(128))
            cx.dma(negA[:], gdn_A_log[l].pbc(128)); cx.dma(dtb[:], gdn_dt_bias[l].pbc(128))
            cx.act(negA[:], negA[:], AF.Exp)
            cx.ts("dve", negA[:], negA[:], -1.0, ALU.mult)
            for half in range(2):
                for kv in range(2):
                    cx.dma(PEK[half * 64:(half + 1) * 64, kv * 64:(kv + 1) * 64], cmp_pe[l, 0])
                    cx.dma(PEV[half * 64:(half + 1) * 64, kv * 64:(kv + 1) * 64], cmp_pe[l, 1])
            cx.memset("dve", plf[:], 0.0)
            pk = cmp_pool[l, 0].re("(p o) -> p o", o=1); pv = cmp_pool[l, 1].re("(p o) -> p o", o=1)
            cx.dma(plf[0:64, 0:1], pk); cx.dma(plf[64:128, 1:2], pk)
            cx.dma(plf[0:64, 128:129], pv); cx.dma(plf[64:128, 129:130], pv)
            cx.cp("dve", POOLK2[:], plf[:, 0:2]); cx.cp("dve", BIGV[:], plf[:, 2:258])

        UB = cx.ar_alloc("UB", [128, 2, 30 + MT1]); GB = cx.ar_alloc("GB", [128, 6, 3 + MT1])
        CY = cx.ar_alloc("CY", [128, 2, MT1]); CY2 = cx.ar_alloc("CY2", [128, 2, MT1]); GY = cx.ar_alloc("GY", [128, 6, MT1])
        SR = cx.ar_alloc("SR", [64, 4, 64]); SGs = cx.ar_alloc("SGs", [64, 4, 64])
        KSEL = cx.ar_alloc("KSEL", [128, SEQ], BF16); VSEL = cx.ar_alloc("VSEL", [128, NT, 2, 65], BF16)
        KWIN = cx.ar_alloc("KWIN", [128, 5, 128], BF16); VWIN = cx.ar_alloc("VWIN", [128, 5, 2, 65], BF16)
        NBK = max(NBP, NBS); NVT = (NBK + 127) // 128
        KC = cx.ar_alloc("KC", [128, NBK], BF16); VC = cx.ar_alloc("VC", [128, NVT, 2, 65], BF16); VCf = cx.ar_alloc("VCf", [128, 128])
        KNEW = cx.ar_alloc("KNEW", [128, 8], BF16); VNEW = cx.ar_alloc("VNEW", [8, 2, 65], BF16)
        for t_ in (VSEL, VWIN, VC, VNEW):
            cx.memset("pool", t_[:], 1.0)
        QA = cx.ar_alloc("QA", [128, MT1], BF16); QB = cx.ar_alloc("QB", [128, MT1], BF16)
        BRT = cx.ar_alloc("BRT", [128, 4, 2, MT1], BF16)
        KVT = cx.ar_alloc("KVT", [128, 780]); RETT = cx.ar_alloc("RETT", [128, 1024]); ZAB = cx.ar_alloc("ZAB", [128, 264])
        RK = cx.ar_alloc("RK", [128, 128], BF16); RV = cx.ar_alloc("RV", [128, 128], BF16)
        Ebuf = [cx.ar_alloc("Eb%d" % k, [128, 512]) for k in range(2)]
        PTb = [cx.ar_alloc("PTb%d" % k, [128, 4, 128], BF16) for k in range(2)]
        OACC = cx.ar_alloc("OACC", [128, 3, 4, 65])
        WS = max(NBK + 1, 16)
        PCN = cx.ar_alloc("PCN", [128, 4, NBK]); SC = cx.ar_alloc("SC", [128, WS]); SC2 = cx.ar_alloc("SC2", [128, WS]); SEL = cx.ar_alloc("SEL", [128, 2, WS])
        M8 = cx.ar_alloc("M8", [128, 16]); rs4 = cx.ar_alloc("rs4", [128, 8]); GT = cx.ar_alloc("GT", [128, 12]); FF = cx.ar_alloc("FF", [128, 12])
        ONSA = cx.ar_alloc("ONSA", [128, 256]); OTMP = cx.ar_alloc("OTMP", [128, 256])
        ROT = cx.ar_alloc("ROT", [128, 64]); QR = cx.ar_alloc("QR", [128, 256]); KR = cx.ar_alloc("KR", [128, 256]); RT1 = cx.ar_alloc("RT1", [128, 128]); RT2 = cx.ar_alloc("RT2", [128, 128])
        QT = cx.ar_alloc("QT", [64, 128]); QXT = cx.ar_alloc("QXT", [64, 128]); KT = cx.ar_alloc("KT", [64, 128]); KZ = cx.ar_alloc("KZ", [128, 64]); ATT = cx.ar_alloc("ATT", [128, 128])
        ST8 = cx.ar_alloc("ST8", [128, 8]); SIL = cx.ar_alloc("SIL", [128, 256])
        GQ = cx.ar_alloc("GQ", [128, 256]); GK = cx.ar_alloc("GK", [128, 256]); GV = cx.ar_alloc("GV", [128, 256])
        GG = cx.ar_alloc("GG", [128, 4]); GBETA = cx.ar_alloc("GBETA", [128, 4]); GC = cx.ar_alloc("GC", [128, 4]); GLB = cx.ar_alloc("GLB", [128, 4])
        EGC = cx.ar_alloc("EGC", [128, 4]); EKD = cx.ar_alloc("EKD", [128, 4]); EGL = cx.ar_alloc("EGL", [128, 4]); G4 = cx.ar_alloc("G4", [128, 4]); G4b = cx.ar_alloc("G4b", [128, 4])
        KBt = cx.ar_alloc("KBt", [128, 64]); RU = cx.ar_alloc("RU", [128, 64]); KBE = cx.ar_alloc("KBE", [128, 64]); QD = cx.ar_alloc("QD", [128, 64]); KD = cx.ar_alloc("KD", [128, 64])
        gKT = cx.ar_alloc("gKT", [64, 128]); gKBT = cx.ar_alloc("gKBT", [64, 128]); gQT = cx.ar_alloc("gQT", [64, 128]); gQDT = cx.ar_alloc("gQDT", [64, 128])
        GREP = cx.ar_alloc("GREP", [128, 128]); LMU = cx.ar_alloc("LMU", [128, 128]); LML = cx.ar_alloc("LML", [128, 128]); LMT = cx.ar_alloc("LMT", [128, 128])
        GATT = cx.ar_alloc("GATT", [128, 128]); YM = cx.ar_alloc("YM", [128, 128])
        PQ = [[cx.ar_alloc("PQ%d%d" % (a, b), [128, 128]) for b in range(2)] for a in range(2)]
        USB = cx.ar_alloc("USB", [128, 64]); WT = cx.ar_alloc("WT", [64, 128]); VN = cx.ar_alloc("VN", [128, 64]); OG = cx.ar_alloc("OG", [128, 256])
        PTI0 = cx.ar_alloc("PTI0", [128, NPG], I32); PTI1 = cx.ar_alloc("PTI1", [128, NPG], I32); PTF = cx.ar_alloc("PTF", [128, NPG])
        PGS = [cx.ar_alloc("PGS%d" % k, [128, 256]) for k in range(8)]
        RKs = [RK, cx.ar_alloc("RK1", [128, 128], BF16)]; RVs = [RV, cx.ar_alloc("RV1", [128, 128], BF16)]
        KPGs = [cx.ar_alloc("KPG%d" % k, [128, 4, 128], BF16) for k in range(2)]
        VPGs = [cx.ar_alloc("VPG%d" % k, [128, 4, 2, 65], BF16) for k in range(2)]
        for t_ in VPGs:
            cx.memset("pool", t_[:], 1.0)
        LCS2 = cx.ar_alloc("LCS2", [8, 4, 128]); keeps = cx.ar_alloc("keeps", [8, 2 * (NBS + 1)]); cx.dma(keeps[:], c_keeps[0:8, :])
        WINT = cx.ar_alloc("WINT", [128, 256])
        for h in range(4):
            cx.ts("dve", LCS2[0:8, h, :], ones_f[0:8, 0:128], c31[0:8, h:h + 1], ALU.mult)
            cx.cp("dve", LCS2[0:8, h, 126:128], LC[0:8, h, J0 - 2:J0])
        eb = {"i": 0}

        def proj_fm(col0, ncols, N):
            p = ptmp()
            for kc in range(8):
                cx.mm(p[0:ncols, 0:N], W1[:, kc, col0:col0 + ncols], hT[:, kc, 0:N], start=(kc == 0), stop=(kc == 7))
            return p[0:ncols, 0:N]

        def proj_tm(col0, ncols, c0, T):
            p = ptmp()
            for kc in range(8):
                cx.mm(p[0:T, 0:ncols], hT[:, kc, c0:c0 + T], W1[:, kc, col0:col0 + ncols], start=(kc == 0), stop=(kc == 7))
            return p[0:T, 0:ncols]

        def hprt(h):
            return (0, 64) if h < 2 else (64, 128)

        def attend(T, h, br, qv, tiles, near_bt=None):
            ntot = sum(t["nk"] for t in tiles)
            p = ptmp(); off = 0
            for t in tiles:
                cx.mm(p[0:T, off:off + t["nk"]], qv, t["kT"]); off += t["nk"]
            eb["i"] ^= 1
            E = Ebuf[eb["i"]]; PT = PTb[eb["i"]]
            if near_bt is None:
                cx.act(E[0:T, 0:ntot], p[0:T, 0:ntot], AF.Exp, bias=c31[0:T, h:h + 1])
            else:
                cx.tt("dve", E[0:T, 0:ntot], p[0:T, 0:ntot], near_bt, ALU.add)
                cx.act(E[0:T, 0:ntot], E[0:T, 0:ntot], AF.Exp)
            off = 0
            for t in tiles:
                if t.get("mask") is not None:
                    ev_ = E[0:T, off:off + t["nk"]]
                    if t.get("mre"):
                        ev_ = ev_.re("p (b k) -> p b k", k=t["mre"])
                    cx.tt("dve" if T <= 8 else "pool", ev_, ev_, t["mask"], ALU.mult)
                off += t["nk"]
            p2 = ptmp(); off = 0
            for k, t in enumerate(tiles):
                cx.tr(p2[0:t["nk"], k * T:(k + 1) * T], E[0:T, off:off + t["nk"]], ident[0:T, 0:T]); off += t["nk"]
            nt = len(tiles)
            evac(PT[:, 0:nt, 0:T], p2[:, 0:nt * T].re("p (k t) -> p k t", t=T))
            p3 = ptmp()
            for k, t in enumerate(tiles):
                cx.mm(p3[0:T, 0:65], PT[0:t["nk"], k, 0:T], t["v"], start=(k == 0), stop=(k == nt - 1))
            cx.tt("dve", OACC[0:T, br, h, :], OACC[0:T, br, h, :], p3[0:T, 0:65], ALU.add)
            return E

        def nsa_select(T, nbv, W, keepv, addv):
            for kv in range(2):
                cx.memset("pool", SC[0:T, 0:W], 0.0)
                cx.tt("dve", SC[0:T, 0:nbv], PCN[0:T, 2 * kv, 0:nbv], PCN[0:T, 2 * kv + 1, 0:nbv], ALU.add)
                cx.tt("dve", SC[0:T, 0:W], SC[0:T, 0:W], keepv, ALU.mult)
                cx.tt("dve", SC[0:T, 0:W], SC[0:T, 0:W], addv, ALU.add)
                cx.memset("dve", SC[0:T, 0:1], 2.0)
                cx.max8(M8[0:T, 0:8], SC[0:T, 0:W])
                cx.match_replace(SC2[0:T, 0:W], M8[0:T, 0:8], SC[0:T, 0:W], -5.0)
                cx.max8(M8[0:T, 8:16], SC2[0:T, 0:W])
                cx.ts("dve", SEL[0:T, kv, 0:W], SC[0:T, 0:W], M8[0:T, 15:16], ALU.is_ge)
                cx.stt(SEL[0:T, kv, 0:W], SC[0:T, 0:W], -0.5, SEL[0:T, kv, 0:W], ALU.is_gt, ALU.mult)

        def nsa_compressed(T, c0, nbv, lcv):
            for h in range(4):
                a, b = hprt(h); kv = h // 2
                qv = (QA if h % 2 == 0 else QB)[a:b, c0:c0 + T]
                for t0 in range(0, nbv, 128):
                    nk = min(128, nbv - t0)
                    tl = [dict(kT=KC[a:b, t0:t0 + nk], v=VC[0:nk, t0 // 128, kv, :], nk=nk)]
                    E = attend(T, h, 0, qv, tl, near_bt=lcv(h, t0, nk))
                    cx.cp("pool", PCN[0:T, h, t0:t0 + nk], E[0:T, 0:nk])
                cx.red(rs4[0:T, h:h + 1], PCN[0:T, h, 0:nbv])
                cx.ts("dve", rs4[0:T, h:h + 1], rs4[0:T, h:h + 1], 1e-30, ALU.max)
                cx.op("dve", [rs4[0:T, h:h + 1]], [rs4[0:T, 4 + h:5 + h]],
                      lambda g, hh=h: g.reciprocal(out=rs4[0:T, 4 + hh:5 + hh].ap, in_=rs4[0:T, hh:hh + 1].ap))
                cx.ts("dve", PCN[0:T, h, 0:nbv], PCN[0:T, h, 0:nbv], rs4[0:T, 4 + h:5 + h], ALU.mult)

        def nsa_combine(T, c0):
            cx.act(GT[0:T, :], KVT[0:T, 768:780], AF.Sigmoid)
            den = OACC[0:T, :, :, 64]
            cx.ts("dve", FF[0:T, :].re("p (b h) -> p b h", h=4), den, 1e-30, ALU.max)
            cx.op("dve", [FF[0:T, :]], [FF[0:T, :]], lambda g: g.reciprocal(out=FF[0:T, :].ap, in_=FF[0:T, :].ap))
            cx.tt("dve", FF[0:T, :], FF[0:T, :], GT[0:T, :], ALU.mult)
            o3 = ONSA[0:T, :].re("p (h e) -> p h e", e=64); t3 = OTMP[0:T, :].re("p (h e) -> p h e", e=64)
            for br in range(3):
                f = FF[0:T, br * 4:(br + 1) * 4].unsq(2).bc([T, 4, 64])
                if br == 0:
                    cx.tt("dve", o3, OACC[0:T, br, :, 0:64], f, ALU.mult)
                else:
                    cx.tt("pool", t3, OACC[0:T, br, :, 0:64], f, ALU.mult)
                    cx.tt("pool", o3, o3, t3, ALU.add)
            for c in range(2):
                transpose_to(BRT[:, 0, c, c0:c0 + T], ONSA[0:T, c * 128:(c + 1) * 128], T, 128)

        def nsa_prompt_tile(l, i, c0):
            T = 128; p0 = i * 128
            cx.cp("pool", VSEL[0:T, i, :, 0:64], KVT[0:T, 384:512].re("p (k d) -> p k d", d=64))
            cx.cp("pool", VWIN[0:T, i % 5, :, 0:64], KVT[0:T, 640:768].re("p (k d) -> p k d", d=64))
            cx.tt("pool", RK[0:T, :], KVT[0:T, 0:128], PEK[0:T, :], ALU.add)
            cx.tt("pool", RV[0:T, :], KVT[0:T, 128:256], PEV[0:T, :], ALU.add)
            p = ptmp(); cx.mm(p[:, 0:2], RK[:, :], POOLK2[:, :]); cx.cp("dve", KC[:, 2 * i:2 * i + 2], p[:, 0:2])
            p = ptmp(); cx.mm(p[:, 0:128], BIGV[:, 126 - 2 * i:254 - 2 * i], RV[:, :])
            cx.tt("dve", VCf[:], VCf[:], p[:, 0:128], ALU.add)
            cx.cp("pool", VC[:, 0, :, 0:64], VCf[:].re("p (k d) -> p k d", d=64))
            cx.memset("pool", OACC[0:T], 0.0)
            nbv = 2 * i + 2; W = max(nbv, 16); jc = J0 - 2 * i
            yield
            nsa_compressed(T, c0, nbv, lambda h, t0, nk: LC[0:T, h, jc + t0:jc + t0 + nk])
            yield
            nsa_select(T, nbv, W, keepadd[0:T, jc:jc + W], keepadd[0:T, WLC + jc:WLC + jc + W])
            yield
            for h in range(4):
                a, b = hprt(h); kv = h // 2
                qv = (QA if h % 2 == 0 else QB)[a:b, c0:c0 + T]
                far = [dict(kT=KSEL[a:b, kt * 128:(kt + 1) * 128], v=VSEL[:, kt, kv, :], nk=128,
                            mask=SEL[0:T, kv, 2 * kt:2 * kt + 2].unsq(2).bc([T, 2, 64]), mre=64) for kt in range(0, i - 1)]
                for g0 in range(0, len(far), 4):
                    attend(T, h, 1, qv, far[g0:g0 + 4])
                    yield
                kts = [kt for kt in (i - 1, i) if kt >= 0]
                near = [dict(kT=KSEL[a:b, kt * 128:(kt + 1) * 128], v=VSEL[:, kt, kv, :], nk=128,
                             mask=SEL[0:T, kv, 2 * kt:2 * kt + 2].unsq(2).bc([T, 2, 64]), mre=64) for kt in kts]
                bt = BT[0:T, h, 256 - 128 * len(kts):256]
                attend(T, h, 1, qv, near, near_bt=bt)
                yield
                farw = [dict(kT=KWIN[a:b, kt % 5, :], v=VWIN[:, kt % 5, kv, :], nk=128,
                             mask=(wm4[0:T, :] if kt == i - 4 else None)) for kt in (i - 4, i - 3, i - 2) if kt >= 0]
                if farw:
                    attend(T, h, 2, qv, farw)
                    yield
                nearw = [dict(kT=KWIN[a:b, kt % 5, :], v=VWIN[:, kt % 5, kv, :], nk=128) for kt in kts]
                attend(T, h, 2, qv, nearw, near_bt=bt)
                yield
            nsa_combine(T, c0)

        def sample_prep(l, s):
            cx.dma(PTI0[:], ptab[s].pbc(128))
            cx.cp("dve", PTF[:], PTI0[:])
            cx.ts("dve", PTF[:], PTF[:], float(DEPTH * 128), ALU.mult, float(l * 128), ALU.add)
            cx.ts("dve", PTF[:], PTF[:], iota[:, 0:1], ALU.add)
            cx.ts("dve", PTF[:], PTF[:], 2.0, ALU.mult)
            cx.cp("dve", PTI0[:], PTF[:])
            cx.ts("dve", PTF[:], PTF[:], 1.0, ALU.add)
            cx.cp("dve", PTI1[:], PTF[:])
            PF = 6

            def issue1(pg):
                cx.dma(PGS[pg % 8][:], pool_d[:, :], q="pool", indirect=PTI0[:, pg:pg + 1])
            for pg in range(min(PF, NPG)):
                issue1(pg)
            pv = None
            for pg in range(NPG):
                if pg + PF < NPG:
                    issue1(pg + PF)
                buf = PGS[pg % 8]; rk = RKs[pg % 2]; rv = RVs[pg % 2]
                cx.tt("dve", rk[:, :], buf[:, 0:128], PEK[:, :], ALU.add)
                cx.tt("dve", rv[:, :], buf[:, 128:256], PEV[:, :], ALU.add)
                p = ptmp(); cx.mm(p[:, 0:2], rk[:, :], POOLK2[:, :]); cx.cp("act", KC[:, 2 * pg:2 * pg + 2], p[:, 0:2])
                r = pg % 64
                if r == 0:
                    pv = pacc()
                cx.mm(pv[:, 0:128], BIGV[:, 126 - 2 * r:254 - 2 * r], rv[:, :], start=(r == 0), stop=(r == 63 or pg == NPG - 1))
                if r == 63 or pg == NPG - 1:
                    cx.cp("act", VC[:, pg // 64, :, 0:64], pv[:, 0:128].re("p (k d) -> p k d", d=64))
            for kt in range(4):
                cx.dma(WINT[:], win_in[l, s, kt * 128:(kt + 1) * 128, :])
                transpose_to(KWIN[:, kt, :], WINT[:, 0:128], 128, 128)
                cx.cp("pool", VWIN[:, kt, :, 0:64], WINT[:, 128:256].re("p (k d) -> p k d", d=64))
            cx.dma(wins[l, s, 0:512 - TS, :], win_in[l, s, TS:512, :])

        def nsa_sample_tile(l, s, c0):
            T = TS
            cx.cp("pool", VNEW[0:T, :, 0:64], KVT[0:T, 384:512].re("p (k d) -> p k d", d=64))
            cx.cp("pool", VWIN[0:T, 4, :, 0:64], KVT[0:T, 640:768].re("p (k d) -> p k d", d=64))
            cx.memset("pool", OACC[0:T], 0.0)
            nbv = NBS; W = NBS + 1
            nsa_compressed(T, c0, nbv, lambda h, t0, nk: (LCS2[0:T, h, 128 - nk:128] if t0 + nk == nbv else None))
            nsa_select(T, nbv, W, keeps[0:T, 0:W], keeps[0:T, W:2 * W])
            qv = lambda h: (QA if h % 2 == 0 else QB)[hprt(h)[0]:hprt(h)[1], c0:c0 + T]
            def gath(g0):
                for k in range(min(4, NPG - g0)):
                    cx.dma(PGS[(g0 + k) % 8][:], pool_d[:, :], q="pool", indirect=PTI1[:, g0 + k:g0 + k + 1])

            def prep(g0):
                bsel = (g0 // 4) % 2
                npg_ = min(4, NPG - g0)
                for k in range(npg_):
                    buf = PGS[(g0 + k) % 8]
                    transpose_to(KPGs[bsel][:, k, :], buf[:, 0:128], 128, 128)
                    cx.cp("pool", VPGs[bsel][:, k, :, 0:64], buf[:, 128:256].re("p (k d) -> p k d", d=64))
            gath(0)
            prep(0)
            yield
            for g0 in range(0, NPG, 4):
                npg = min(4, NPG - g0)
                KPG = KPGs[(g0 // 4) % 2]; VPG = VPGs[(g0 // 4) % 2]
                if g0 + 4 < NPG:
                    gath(g0 + 4)
                for h in range(4):
                    a, b = hprt(h); kv = h // 2
                    tiles = [dict(kT=KPG[a:b, k, :], v=VPG[:, k, kv, :], nk=128,
                                  mask=SEL[0:T, kv, 2 * (g0 + k):2 * (g0 + k) + 2].unsq(2).bc([T, 2, 64]), mre=64) for k in range(npg)]
                    if g0 + npg == NPG:
                        last = tiles[-1]; tiles = tiles[:-1]
                        newt = dict(kT=KNEW[a:b, 0:T], v=VNEW[0:T, kv, :], nk=T, mask=SEL[0:T, kv, NBS:NBS + 1].bc([T, T]))
                        if tiles:
                            attend(T, h, 1, qv(h), tiles)
                        attend(T, h, 1, qv(h), [last, newt], near_bt=BT[0:T, h, 0:128 + T])
                    else:
                        attend(T, h, 1, qv(h), tiles)
                    yield
                if g0 + 4 < NPG:
                    prep(g0 + 4)
            for h in range(4):
                a, b = hprt(h); kv = h // 2
                farw = [dict(kT=KWIN[a:b, kt, :], v=VWIN[:, kt, kv, :], nk=128, mask=(wm4[0:T, :] if kt == 0 else None)) for kt in range(3)]
                attend(T, h, 2, qv(h), farw)
                nearw = [dict(kT=KWIN[a:b, 3, :], v=VWIN[:, 3, kv, :], nk=128), dict(kT=KWIN[a:b, 4, 0:T], v=VWIN[0:T, 4, kv, :], nk=T)]
                attend(T, h, 2, qv(h), nearw, near_bt=BT[0:T, h, 0:128 + T])
                yield
            nsa_combine(T, c0)

        def conformer_mt(N):
            for c in range(2):
                y = CY[:, c, 0:N]
                cx.ts("dve", y, UB[:, c, 0:N], convw[:, c, 0:1], ALU.mult)
                for k in range(1, 31):
                    cx.stt(y, UB[:, c, k:k + N], convw[:, c, k:k + 1], y, ALU.mult, ALU.add)
                    if k % 6 == 0:
                        yield
                cx.ts("dve", y, y, cvp[:, c:c + 1], ALU.add)
                cx.tt("pool", CY2[:, c, 0:N], y, y, ALU.mult)
            pm = ptmp()
            for c in range(2):
                cx.mm(pm[:, 0:N], ones_f[:, :], CY[:, c, 0:N], start=(c == 0), stop=(c == 1))
            pq = ptmp()
            for c in range(2):
                cx.mm(pq[:, 0:N], ones_f[:, :], CY2[:, c, 0:N], start=(c == 0), stop=(c == 1))
            cx.ts("dve", g1[:, 0:N], pm[:, 0:N], 1.0 / 256, ALU.mult)
            cx.tt("dve", g2[:, 0:N], g1[:, 0:N], g1[:, 0:N], ALU.mult)
            cx.stt(g2[:, 0:N], pq[:, 0:N], 1.0 / 256, g2[:, 0:N], ALU.mult, ALU.subtract)
            cx.ts("dve", g2[:, 0:N], g2[:, 0:N], 0.0, ALU.max, EPS, ALU.add)
            cx.rsqrt(g2[:, 0:N], g2[:, 0:N])
            for c in range(2):
                cx.tt("pool", g3[:, 0:N], CY[:, c, 0:N], g1[:, 0:N], ALU.subtract)
                cx.tt("pool", g3[:, 0:N], g3[:, 0:N], g2[:, 0:N], ALU.mult)
                cx.ts("dve", g3[:, 0:N], g3[:, 0:N], cvp[:, 2 + c:3 + c], ALU.mult, cvp[:, 4 + c:5 + c], ALU.add)
                cx.act(BRT[:, 1, c, 0:N], g3[:, 0:N], AF.Silu)

        def gdn_conv_mt(N):
            for c in range(6):
                y = GY[:, c, 0:N]
                cx.ts("dve", y, GB[:, c, 0:N], gdw[:, c, 0:1], ALU.mult)
                for k in range(1, 4):
                    cx.stt(y, GB[:, c, k:k + N], gdw[:, c, k:k + 1], y, ALU.mult, ALU.add)
                cx.act(y, y, AF.Silu)
                yield

        def bc4(v, T):
            return v.unsq(2).bc([T, 4, 64])

        def h3(t, T):
            return t[0:T, :].re("p (h e) -> p h e", e=64)

        def retention_tile(grp, pos, c0):
            T = grp["T"]; g = grp["gi"]
            src = c_rot_p[pos:pos + T, :] if grp["kind"] == "p" else c_rot_s[0:T, :]
            cx.dma(ROT[0:T, :], src)
            cosb = ROT[0:T, 0:32].unsq(1).bc([T, 4, 32]); sinb = ROT[0:T, 32:64].unsq(1).bc([T, 4, 32])
            ta = RT1[0:T, :].re("p (h f) -> p h f", f=32); tb = RT2[0:T, :].re("p (h f) -> p h f", f=32)
            for (off, dst) in ((0, QR), (256, KR)):
                s4 = RETT[0:T, off:off + 256].re("p (h w f) -> p h w f", h=4, w=2)
                d4 = dst[0:T, :].re("p (h w f) -> p h w f", h=4, w=2)
                x1 = s4[:, :, 0, :]; x2 = s4[:, :, 1, :]
                cx.tt("pool", d4[:, :, 0, :], x1, cosb, ALU.mult); cx.tt("pool", ta, x2, sinb, ALU.mult)
                cx.tt("pool", d4[:, :, 0, :], d4[:, :, 0, :], ta, ALU.subtract)
                cx.tt("dve", d4[:, :, 1, :], x1, sinb, ALU.mult); cx.tt("dve", tb, x2, cosb, ALU.mult)
                cx.tt("dve", d4[:, :, 1, :], d4[:, :, 1, :], tb, ALU.add)
            if T == 128:
                chk(7.41)
            po = pacc()
            for h in range(4):
                qh = QR[0:T, h * 64:(h + 1) * 64]; kh = KR[0:T, h * 64:(h + 1) * 64]; vh = RETT[0:T, 512 + h * 64:512 + (h + 1) * 64]
                p = ptmp(); cx.tr(p[0:64, 0:T], qh, ident[0:T, 0:T])
                cx.cp("act", QT[0:64, 0:T], p[0:64, 0:T]); cx.tt("dve", QXT[0:64, 0:T], QT[0:64, 0:T], retxig[g][0:64, h, 0:T], ALU.mult)
                p = ptmp(); cx.tr(p[0:64, 0:T], kh, ident[0:T, 0:T]); cx.act(KT[0:64, 0:T], p[0:64, 0:T], AF.Identity, scale=0.125)
                cx.ts("pool", KZ[0:T, :], kh, retzg[g][0:T, h:h + 1], ALU.mult)
                if T == 128:
                    chk(7.42)
                p = ptmp(); cx.mm(p[0:T, 0:T], KT[0:64, 0:T], QT[0:64, 0:T])
                cx.tt("dve", ATT[0:T, 0:T], p[0:T, 0:T], retDg[g][0:T, h, 0:T], ALU.mult)
                if T == 128:
                    chk(7.43)
                cx.mm(po[0:T, h * 64:(h + 1) * 64], ATT[0:T, 0:T], vh, start=True, stop=False)
                cx.mm(po[0:T, h * 64:(h + 1) * 64], QXT[0:64, 0:T], SR[0:64, h, :], start=False, stop=True)
                if T == 128:
                    chk(7.44)
                pS = ptmp(); cx.mm(pS[0:64, 0:64], KZ[0:T, 0:64], vh)
                cx.stt(SR[0:64, h, :], SR[0:64, h, :], float(GAM[h] ** T), pS[0:64, 0:64], ALU.mult, ALU.add)
                yield
            cx.cp("act", OG[0:T, :], po[0:T, 0:256])
            cx.red(ST8[0:T, 0:4], h3(OG, T))
            cx.tt("pool", OTMP[0:T, :], OG[0:T, :], OG[0:T, :], ALU.mult)
            cx.red(ST8[0:T, 4:8], h3(OTMP, T))
            cx.ts("dve", ST8[0:T, 0:8], ST8[0:T, 0:8], 1.0 / 64, ALU.mult)
            cx.tt("dve", G4[0:T, :], ST8[0:T, 0:4], ST8[0:T, 0:4], ALU.mult)
            cx.tt("dve", G4[0:T, :], ST8[0:T, 4:8], G4[0:T, :], ALU.subtract)
            cx.ts("dve", G4[0:T, :], G4[0:T, :], 0.0, ALU.max, EPS, ALU.add)
            cx.rsqrt(G4[0:T, :], G4[0:T, :])
            cx.tt("dve", h3(OG, T), h3(OG, T), bc4(ST8[0:T, 0:4], T), ALU.subtract)
            cx.tt("dve", h3(OG, T), h3(OG, T), bc4(G4[0:T, :], T), ALU.mult)
            cx.tt("pool", OG[0:T, :], OG[0:T, :], gnb[0:T, :], ALU.mult)
            cx.act(SIL[0:T, :], RETT[0:T, 768:1024], AF.Silu)
            cx.tt("pool", OG[0:T, :], OG[0:T, :], SIL[0:T, :], ALU.mult)
            for c in range(2):
                transpose_to(BRT[:, 2, c, c0:c0 + T], OG[0:T, c * 128:(c + 1) * 128], T, 128)

        def gdn_tile(grp, c0):
            T = grp["T"]; g = grp["gi"]
            for part, dst in enumerate((GQ, GK, GV)):
                for h in range(4):
                    blk = part * 2 + h // 2; a = (h % 2) * 64
                    p = ptmp(); cx.tr(p[0:T, 0:64], GY[a:a + 64, blk, c0:c0 + T], ident[a:a + 64, a:a + 64])
                    evac(dst[0:T, h * 64:(h + 1) * 64], p[0:T, 0:64])
            for X, sc in ((GQ, 0.125), (GK, 1.0)):
                cx.tt("pool", OTMP[0:T, :], X[0:T, :], X[0:T, :], ALU.mult)
                cx.red(G4[0:T, :], h3(OTMP, T))
                cx.ts("dve", G4[0:T, :], G4[0:T, :], 0.0, ALU.max, EPS, ALU.add)
                cx.rsqrt(G4[0:T, :], G4[0:T, :])
                if sc != 1.0:
                    cx.ts("dve", G4[0:T, :], G4[0:T, :], sc, ALU.mult)
                cx.tt("dve", h3(X, T), h3(X, T), bc4(G4[0:T, :], T), ALU.mult)
            cx.tt("dve", G4[0:T, :], ZAB[0:T, 256:260], dtb[0:T, :], ALU.add)
            cx.ts("dve", G4b[0:T, :], G4[0:T, :], -1.0, ALU.mult)
            cx.tt("dve", G4b[0:T, :], G4b[0:T, :], G4[0:T, :], ALU.max)
            cx.act(G4b[0:T, :], G4b[0:T, :], AF.Exp, scale=-1.0)
            cx.act(G4b[0:T, :], G4b[0:T, :], AF.Ln, bias=ones_f[0:T, 0:1])
            cx.ts("dve", G4[0:T, :], G4[0:T, :], 0.0, ALU.max)
            cx.tt("dve", G4[0:T, :], G4[0:T, :], G4b[0:T, :], ALU.add)
            cx.tt("dve", GG[0:T, :], G4[0:T, :], negA[0:T, :], ALU.mult)
            cx.act(GBETA[0:T, :], ZAB[0:T, 260:264], AF.Sigmoid)
            p = ptmp(); cx.mm(p[0:T, 0:4], Umat[0:T, 0:T], GG[0:T, 0:4]); cx.cp("dve", GC[0:T, :], p[0:T, 0:4])
            p = ptmp(); cx.mm(p[0:128, 0:4], lastmg[g][0:T, :], GC[0:T, 0:4]); cx.cp("dve", GLB[:, :], p[0:128, 0:4])
            cx.act(EGC[0:T, :], GC[0:T, :], AF.Exp)
            cx.tt("dve", EKD[0:T, :], GLB[0:T, :], GC[0:T, :], ALU.subtract); cx.act(EKD[0:T, :], EKD[0:T, :], AF.Exp)
            cx.act(EGL[:, :], GLB[:, :], AF.Exp)
            po = pacc()
            nlev = int(round(math.log2(T)))
            for h in range(4):
                hs = slice(h * 64, (h + 1) * 64)
                cx.ts("dve", KBt[0:T, :], GK[0:T, hs], GBETA[0:T, h:h + 1], ALU.mult)
                cx.ts("pool", RU[0:T, :], GV[0:T, hs], GBETA[0:T, h:h + 1], ALU.mult)
                cx.ts("dve", KBE[0:T, :], KBt[0:T, :], EGC[0:T, h:h + 1], ALU.mult)
                cx.ts("pool", QD[0:T, :], GQ[0:T, hs], EGC[0:T, h:h + 1], ALU.mult)
                cx.ts("pool", KD[0:T, :], GK[0:T, hs], EKD[0:T, h:h + 1], ALU.mult)
                for srcv, dstv in ((GK[0:T, hs], gKT), (KBt[0:T, :], gKBT), (GQ[0:T, hs], gQT), (QD[0:T, :], gQDT)):
                    transpose_to(dstv[0:64, 0:T], srcv, T, 64)
                cx.ts("pool", GREP[0:T, 0:T], ones_f[0:T, 0:T], GG[0:T, h:h + 1], ALU.mult)
                pR = ptmp(); cx.mm(pR[0:T, 0:T], GREP[0:T, 0:T], Umat[0:T, 0:T])
                cx.ts("dve", LMU[0:T, 0:T], pR[0:T, 0:T], GC[0:T, h:h + 1], ALU.subtract, 0.0, ALU.min)
                cx.act(LMU[0:T, 0:T], LMU[0:T, 0:T], AF.Exp)
                cx.ts("dve", LML[0:T, 0:T], pR[0:T, 0:T], GC[0:T, h:h + 1], ALU.subtract, 0.0, ALU.max)
                cx.act(LML[0:T, 0:T], LML[0:T, 0:T], AF.Exp, scale=-1.0)
                cx.tt("pool", LMT[0:T, 0:T], LMU[0:T, 0:T], triu[0:T, 0:T], ALU.mult)
                cx.tt("pool", LMU[0:T, 0:T], LMU[0:T, 0:T], triu[0:T, 128:128 + T], ALU.mult)
                cx.tt("pool", LML[0:T, 0:T], LML[0:T, 0:T], tril[0:T, 0:T], ALU.mult)
                p = ptmp(); cx.mm(p[0:T, 0:T], gKT[0:64, 0:T], gQT[0:64, 0:T]); cx.tt("dve", GATT[0:T, 0:T], p[0:T, 0:T], LMT[0:T, 0:T], ALU.mult)
                P_, Q_ = PQ[0]
                p = ptmp(); cx.mm(p[0:T, 0:T], gKT[0:64, 0:T], gKBT[0:64, 0:T]); cx.tt("dve", P_[0:T, 0:T], p[0:T, 0:T], LMU[0:T, 0:T], ALU.mult)
                p = ptmp(); cx.mm(p[0:T, 0:T], gKBT[0:64, 0:T], gKT[0:64, 0:T]); cx.tt("dve", Q_[0:T, 0:T], p[0:T, 0:T], LML[0:T, 0:T], ALU.mult)
                cx.tt("pool", YM[0:T, 0:T], ident[0:T, 0:T], P_[0:T, 0:T], ALU.subtract)
                yield
                cur = 0
                for k in range(1, nlev):
                    P2, Q2 = PQ[1 - cur]
                    pP = ptmp(); cx.mm(pP[0:T, 0:T], Q_[0:T, 0:T], P_[0:T, 0:T]); cx.cp("dve", P2[0:T, 0:T], pP[0:T, 0:T])
                    pQ = ptmp(); cx.mm(pQ[0:T, 0:T], P_[0:T, 0:T], Q_[0:T, 0:T]); cx.cp("act", Q2[0:T, 0:T], pQ[0:T, 0:T])
                    pY = ptmp(); cx.mm(pY[0:T, 0:T], Q2[0:T, 0:T], YM[0:T, 0:T]); cx.tt("dve", YM[0:T, 0:T], YM[0:T, 0:T], pY[0:T, 0:T], ALU.add)
                    P_, Q_ = P2, Q2; cur = 1 - cur
                    yield
                pu = ptmp(); cx.mm(pu[0:T, 0:64], YM[0:T, 0:T], RU[0:T, 0:64]); cx.cp("act", USB[0:T, :], pu[0:T, 0:64])
                pw = ptmp(); cx.mm(pw[0:64, 0:T], KBE[0:T, 0:64], YM[0:T, 0:T]); cx.cp("dve", WT[0:64, 0:T], pw[0:64, 0:T])
                pv = ptmp(); cx.mm(pv[0:T, 0:64], WT[0:64, 0:T], SGs[0:64, h, :]); cx.tt("dve", VN[0:T, :], USB[0:T, :], pv[0:T, 0:64], ALU.subtract)
                cx.mm(po[0:T, hs], gQDT[0:64, 0:T], SGs[0:64, h, :], start=True, stop=False)
                cx.mm(po[0:T, hs], GATT[0:T, 0:T], VN[0:T, 0:64], start=False, stop=True)
                pS = ptmp(); cx.mm(pS[0:64, 0:64], KD[0:T, 0:64], VN[0:T, 0:64])
                cx.stt(SGs[0:64, h, :], SGs[0:64, h, :], EGL[0:64, h:h + 1], pS[0:64, 0:64], ALU.mult, ALU.add)
                yield
            cx.cp("act", OG[0:T, :], po[0:T, 0:256])
            cx.tt("pool", OTMP[0:T, :], OG[0:T, :], OG[0:T, :], ALU.mult)
            cx.red(G4[0:T, :], h3(OTMP, T))
            cx.ts("dve", G4[0:T, :], G4[0:T, :], 1.0 / 64, ALU.mult, EPS, ALU.add)
            cx.rsqrt(G4[0:T, :], G4[0:T, :])
            cx.tt("dve", h3(OG, T), h3(OG, T), bc4(G4[0:T, :], T), ALU.mult)
            cx.tt("pool", h3(OG, T), h3(OG, T), gnorm[0:T, :].unsq(1).bc([T, 4, 64]), ALU.mult)
            cx.act(SIL[0:T, :], ZAB[0:T, 0:256], AF.Silu)
            cx.tt("pool", OG[0:T, :], OG[0:T, :], SIL[0:T, :], ALU.mult)
            for c in range(2):
                transpose_to(BRT[:, 3, c, c0:c0 + T], OG[0:T, c * 128:(c + 1) * 128], T, 128)

        wv = w_in[l].re("(k p) n -> p k n", p=128)
        for j, hh in enumerate((0, 2, 1, 3)):
            cx.dma(W1[:, :, j * 64:(j + 1) * 64], wv[:, :, hh * 64:(hh + 1) * 64], q="pool")
        for c0_ in range(256, O_MG, 2048):
            c1_ = min(O_MG, c0_ + 2048)
            cx.dma(W1[:, :, c0_:c1_], wv[:, :, c0_:c1_], q="pool")
        load_layer_params(l)
        chk(3)
        for gi, grp in enumerate(groups):
            T = grp["T"]; s = grp["s"]; isp = grp["kind"] == "p"
            if isp:
                chk(6)
                cx.memset("pool", UB[:, :, 0:30], 0.0); cx.memset("pool", GB[:, :, 0:3], 0.0)
                cx.memset("pool", SR[:], 0.0); cx.memset("pool", SGs[:], 0.0); cx.memset("pool", VCf[:], 0.0)
            else:
                for c in range(2):
                    load_T(UB[:, c, 0:30], st_conv[l, s, :, c * 128:(c + 1) * 128], 30)
                for c in range(6):
                    load_T(GB[:, c, 0:3], st_gconv[l, s, :, c * 128:(c + 1) * 128], 3)
                cx.dma(SR[:], st_ret[l, s].re("h d e -> d h e")); cx.dma(SGs[:], st_gdn[l, s].re("h d e -> d h e"))
                sample_prep(l, s)
                chk(4)
            mlist = mts(grp, MT1 if isp else T)
            for (t0, N) in mlist:
                a0 = grp["tok0"] + t0
                load_x(l, grp, t0, N)
                if l == 0:
                    store_x(grp, t0, N, final=False)
                mod_norm(gi, N, 0, 1)
                for blk, dst in ((0, QA), (1, QB)):
                    cx.act(dst[:, 0:N], proj_fm(blk * 128, 128, N), AF.Identity, scale=0.125)
                pk_ = proj_fm(O_KV + 256, 128, N)
                if isp:
                    evac(KSEL[:, t0:t0 + N], pk_)
                else:
                    evac(KNEW[:, 0:N], pk_)
                pk_ = proj_fm(O_KV + 512, 128, N)
                slot = (t0 // 128) % 5 if isp else 4
                evac(KWIN[:, slot, 0:T], pk_)
                for c in range(2):
                    pg_ = proj_fm(O_CV + 256 + c * 128, 128, N)
                    cx.act(g1[:, 0:N], pg_, AF.Sigmoid)
                    pa_ = proj_fm(O_CV + c * 128, 128, N)
                    cx.tt("dve", UB[:, c, 30:30 + N], pa_, g1[:, 0:N], ALU.mult)
                for c in range(6):
                    evac(GB[:, c, 3:3 + N], proj_fm(O_GDN + c * 128, 128, N))
                if isp:
                    chk(7.1)
                pos = t0; j = 0
                evac(KVT[0:T, 0:512], proj_tm(O_KV, 512, j, T))
                evac(KVT[0:T, 512:780], proj_tm(O_KV + 512, 268, j, T))
                evac(RETT[0:T, 0:512], proj_tm(O_RET, 512, j, T))
                evac(RETT[0:T, 512:1024], proj_tm(O_RET + 512, 512, j, T))
                evac(ZAB[0:T, :], proj_tm(O_GDN + 768, 264, j, T))
                def gdn_chain():
                    yield from gdn_conv_mt(N)
                    yield from gdn_tile(grp, j)
                if isp:
                    cx.dma(kvp[l, pos:pos + T, :], KVT[0:T, 0:512])
                    if pos >= SEQ - 512:
                        cx.dma(winp[l, pos - (SEQ - 512):pos - (SEQ - 512) + T, :], KVT[0:T, 512:768])
                    run_gens([nsa_prompt_tile(l, pos // 128, j), gdn_chain(), retention_tile(grp, pos, j), conformer_mt(N)])
                else:
                    cx.dma(kvs[s, l, :, :], KVT[0:T, 0:512])
                    cx.dma(wins[l, s, 512 - TS:512, :], KVT[0:T, 512:768])
                    run_gens([nsa_sample_tile(l, s, j), gdn_chain(), retention_tile(grp, pos, j), conformer_mt(N)])
                cx.dma(brd[:, :, a0:a0 + N].re("c p n -> p c n"), BRT[:].re("p n c t -> p (n c) t")[:, :, 0:N])
                last = (t0 + N >= T * grp["ntile"])
                if last:
                    for c in range(2):
                        transpose_to(stg[0:30, :], UB[:, c, N:N + 30], 128, 30)
                        dst = convp[l] if isp else convs[l, s]
                        cx.dma(dst[:, c * 128:(c + 1) * 128], stg[0:30, :])
                    for c in range(6):
                        transpose_to(stg[0:3, :], GB[:, c, N:N + 3], 128, 3)
                        dst = gcp[l] if isp else gcs[l, s]
                        cx.dma(dst[:, c * 128:(c + 1) * 128], stg[0:3, :])
                    cx.dma((retp[l] if isp else rets[l, s]).re("h d e -> d h e"), SR[:])
                    cx.dma((gdnp[l] if isp else gdns[l, s]).re("h d e -> d h e"), SGs[:])
                else:
                    cx.cp("pool", UB[:, :, 0:30], UB[:, :, N:N + 30])
                    cx.cp("pool", GB[:, :, 0:3], GB[:, :, N:N + 3])
                chk(5 if not isp else 7)

    try:
        chk(0)
        build_tables()
        chk(1)
        cx.ar_reset()
        load_cT()
        for l in range(DEPTH):
            compute_mod(l)
            chk(2)
            phase1a(l)
            chk(8)
            phase1b(l)
            chk(9)
            phase2(l)
            chk(10)
    except StopBuild:
        pass
    cx.finish()
    return nc


def make_consts(cfg):
    SEQ, TS, PAST, NBP = cfg.SEQ, cfg.TS, cfg.PAST, cfg.NBP
    NBS = PAST // 64
    J0 = NBP - 2
    WLC = NBP + 16
    f = np.float32
    r = np.arange(128)
    c = {}
    c["c_ident"] = np.eye(128, dtype=f)
    c["c_triu"] = np.concatenate([(r[None, :] >= r[:, None]).astype(f), (r[None, :] > r[:, None]).astype(f)], axis=1)
    c["c_tril"] = (r[:, None] > r[None, :]).astype(f)
    c["c_U"] = (r[:, None] <= r[None, :]).astype(f)
    last = np.zeros((2, 128, 128), f); last[0, 127, :] = 1.0; last[1, TS - 1, :] = 1.0
    c["c_last"] = last
    half = 32
    inv = (np.float32(10000.0) ** (-np.arange(half, dtype=f) / f(half))).astype(f)

    def rot(pos):
        ang = pos.astype(f)[:, None] * inv[None, :]
        return np.concatenate([np.cos(ang), np.sin(ang)], axis=1).astype(f)
    c["c_rot_p"] = rot(np.arange(SEQ))
    c["c_rot_s"] = rot(PAST + np.arange(TS))
    lg = np.log1p(-np.exp2(-5.0 - np.arange(4, dtype=np.float64)))
    diff = (r[None, :] - r[:, None]).astype(np.float64)
    Dm = np.where(diff >= 0, np.exp(np.maximum(diff, 0.0)[None] * lg[:, None, None]), 0.0)
    c["c_retD"] = np.stack([Dm, Dm]).astype(f)
    xi = np.exp((r[None, :] + 1.0) * lg[:, None])
    c["c_retxi"] = np.broadcast_to(xi[None, :, None, :], (2, 4, 64, 128)).astype(f).copy()
    z = np.zeros((2, 128, 4), np.float64)
    for g, C in enumerate((128, TS)):
        t = np.arange(C)
        z[g, :C, :] = np.exp((C - 1.0 - t)[:, None] * lg[None, :]) / 8.0
    c["c_retz"] = z.astype(f)
    cc = np.arange(256)
    c["c_dbt"] = (r[:, None] + 128 - cc[None, :]).astype(f)
    j = np.arange(WLC)
    c["c_dlc"] = (r[:, None] - 63 + 64 * (J0 - j[None, :])).astype(f)
    mp = (j - J0)[None, :]
    ct = (r // 64)[:, None]
    forced = (mp == ct) | (mp == ct - 1)
    after = mp > ct
    keep = (~forced & ~after).astype(f)
    add = 2.0 * forced.astype(f) - after.astype(f)
    c["c_keep"] = np.concatenate([keep, add], axis=1).astype(f)
    W = NBS + 1
    ks = np.ones((8, W), f); ad = np.zeros((8, W), f)
    ks[:, NBS - 1:] = 0.0; ad[:, NBS - 1:] = 2.0
    c["c_keeps"] = np.concatenate([ks, ad], axis=1)
    c["c_wm4"] = (r[None, :] > r[:, None]).astype(f)
    c["c_iota"] = r.astype(f).reshape(128, 1)
    return c


_NC_CACHE = {}


def run_cfg(cfg, inputs, n_cores, n_prompt):
    key = (cfg.SEQ, cfg.NS, cfg.TS, cfg.PAST, cfg.NPOOL)
    if key not in _NC_CACHE:
        _NC_CACHE[key] = build(cfg)
    nc = _NC_CACHE[key]
    consts = make_consts(cfg)
    NS, TS, SEQ = cfg.NS, cfg.TS, cfg.SEQ
    A = lambda k: np.ascontiguousarray(np.asarray(inputs[k]))
    pool = A("cache_nsa_kv").reshape(-1, 256)
    wnames = ["w_ada", "b_ada", "norms", "w_in", "cmp_pool", "cmp_pe", "rel_bias", "conv_dw", "conv_dw_b", "conv_ln_g",
              "conv_ln_b", "ret_gn", "gdn_conv_w", "gdn_A_log", "gdn_dt_bias", "gdn_norm", "w_branch", "w_out", "ffn_up",
              "ffn_dw", "ffn_down"]
    shared = {k: A(k) for k in wnames}
    shared.update(consts)
    shared["pool"] = pool
    x_prompt, x_sample = A("x_prompt"), A("x_sample")
    c_prompt, c_sample = A("c_prompt"), A("c_sample")
    win = A("cache_nsa_win"); page_table = A("page_table")
    sc, sr, sgc, sg, sf = A("state_conv"), A("state_ret"), A("state_gdn_conv"), A("state_gdn"), A("state_ffn_conv")
    DB = x_sample.shape[0]
    in_maps = []
    for c in range(n_cores):
        bp = c % n_prompt
        ss = [(c * NS + i) % DB for i in range(NS)]
        m = dict(shared)
        m["xp"] = x_prompt[bp]; m["cp"] = c_prompt[bp:bp + 1]
        m["xs"] = x_sample[ss].reshape(NS * TS, D); m["cs"] = c_sample[ss]
        m["ptab"] = page_table[ss].astype(np.int32)
        m["win_in"] = np.ascontiguousarray(win[:, ss]).reshape(DEPTH, NS, 512, 256)
        m["st_conv"] = np.ascontiguousarray(sc[:, ss]); m["st_ret"] = np.ascontiguousarray(sr[:, ss])
        m["st_gconv"] = np.ascontiguousarray(sgc[:, ss]); m["st_gdn"] = np.ascontiguousarray(sg[:, ss])
        m["st_ffn"] = np.ascontiguousarray(sf[:, ss])
        in_maps.append(m)
    res = run_bass_kernel_spmd(nc, in_maps, core_ids=list(range(n_cores))).results
    B = n_prompt
    pc = [res[b] for b in range(B)]
    f = np.float32
    y_p = np.stack([pc[b]["yp"] for b in range(B)]).astype(f)
    kv_p = np.stack([pc[b]["kvp"] for b in range(B)]).reshape(B, DEPTH, SEQ, 4, 2, 64)
    win_p = np.stack([pc[b]["winp"] for b in range(B)], axis=1).reshape(DEPTH, B, 512, 2, 2, 64)
    conv_p = np.stack([pc[b]["convp"] for b in range(B)], axis=1)
    ret_p = np.stack([pc[b]["retp"] for b in range(B)], axis=1)
    gc_p = np.stack([pc[b]["gcp"] for b in range(B)], axis=1)
    gdn_p = np.stack([pc[b]["gdnp"] for b in range(B)], axis=1)
    ffn_p = np.stack([pc[b]["ffnp"] for b in range(B)], axis=1)
    ncs = DB // NS
    sc_ = [res[c] for c in range(ncs)]
    y_s = np.concatenate([r["ys"].reshape(NS, TS, D) for r in sc_], axis=0)
    kv_s = np.concatenate([r["kvs"] for r in sc_], axis=0).reshape(DB, DEPTH, TS, 4, 2, 64)
    cat1 = lambda k: np.concatenate([r[k] for r in sc_], axis=1)
    win_s = cat1("wins").reshape(DEPTH, DB, 512, 2, 2, 64)
    outs = (y_p, y_s, kv_p, kv_s, win_p, win_s, conv_p, cat1("convs"), ret_p, cat1("rets"), gc_p, cat1("gcs"),
            gdn_p, cat1("gdns"), ffn_p, cat1("ffns"))
    return tuple(np.ascontiguousarray(o, dtype=np.float32) for o in outs)


def kernel(**inputs):
    cfg = Cfg(NS=4)
    return run_cfg(cfg, inputs, n_cores=8, n_prompt=4)
```

```python
import math
import numpy as np
import concourse.bass as bass
import concourse.mybir as mybir
from concourse.bass_utils import run_bass_kernel_spmd

F32 = mybir.dt.float32
BF16 = mybir.dt.bfloat16
I32 = mybir.dt.int32
AF = mybir.ActivationFunctionType
ALU = mybir.AluOpType
AX = mybir.AxisListType

D = 1024
HD = 64
DEPTH = 2
NKV = 768
CONV_CH = 256
CONV_W = 31
D_FF = 2816
IN_COLS = 7700
EPS = 1e-6
NEG = -30000.0
O_Q, O_KV, O_GT, O_CV, O_RET, O_GDN, O_MG = 0, 256, 1024, 1036, 1548, 2572, 3604
GAM = [1.0 - 2.0 ** (-5.0 - h) for h in range(4)]


class Tile:
    def __init__(self, t, name):
        self.t = t
        self.name = name
        self.w = None
        self.r = {}
        self.psum = False

    def __getitem__(self, idx):
        return View(self, self.t[idx])

    def re(self, pat, **kw):
        return self[:].re(pat, **kw)

    def pbc(self, n):
        return self[:].pbc(n)


class View:
    def __init__(self, tile, ap):
        self.tile = tile
        self.ap = ap

    def __getitem__(self, idx):
        return View(self.tile, self.ap[idx])

    def re(self, pat, **kw):
        return View(self.tile, self.ap.rearrange(pat, **kw))

    def bc(self, shape):
        return View(self.tile, self.ap.to_broadcast(shape))

    def unsq(self, ax):
        return View(self.tile, self.ap.unsqueeze(ax))

    def pbc(self, n):
        return View(self.tile, self.ap.partition_broadcast(n))


def _ap(x):
    return x.ap if isinstance(x, View) else x


class Cx:
    def __init__(self, nc):
        self.nc = nc
        self.eng = dict(pe=nc.tensor, dve=nc.vector, act=nc.scalar, pool=nc.gpsimd, sp=nc.sync)
        self.sem = {k: nc.alloc_semaphore("cs_" + k) for k in ("pe", "dve", "act", "pool")}
        self.cnt = {k: 0 for k in self.sem}
        self.dsem = {}
        self.seen = {k: {} for k in self.eng}
        self.nops = 0

    def sb(self, name, shape, dt=F32):
        return Tile(self.nc.alloc_sbuf_tensor(name, list(shape), dt), name)

    def ps(self, name, shape, dt=F32):
        t = Tile(self.nc.alloc_psum_tensor(name, list(shape), dt), name)
        t.psum = True
        return t

    def dram(self, name, shape, dt=F32, kind="Internal"):
        return Tile(self.nc.dram_tensor(name, list(shape), dt, kind=kind), name)

    def _semobj(self, key):
        return self.sem[key[1]] if key[0] == "c" else self.dsem[key[1]]["sems"][key[2]]

    def _sync(self, e, reads, writes):
        deps = {}

        def add(kv):
            k, v = kv
            if deps.get(k, 0) < v:
                deps[k] = v

        for t in reads:
            if t.w is not None:
                add(t.w)
        for t in writes:
            if t.w is not None:
                add(t.w)
            for kv in t.r.items():
                add(kv)
        seen = self.seen[e]
        rt = set(id(t) for t in reads)
        for k, v in deps.items():
            if k == ("c", e) and e == "pe":
                continue
            if seen.get(k, 0) >= v:
                continue
            self.eng[e].wait_ge(self._semobj(k), v)
            seen[k] = v

    def op(self, e, reads, writes, fn):
        reads = [x.tile for x in reads if isinstance(x, View)]
        writes = [x.tile for x in writes if isinstance(x, View)]
        self._sync(e, reads, writes)
        self.cnt[e] += 1
        n = self.cnt[e]
        fn(self.eng[e]).then_inc(self.sem[e], 1)
        key = ("c", e)
        for t in reads:
            if t.psum:
                t.w = (key, n)
                t.r = {}
            elif t.r.get(key, 0) < n:
                t.r[key] = n
        for t in writes:
            t.w = (key, n)
            t.r = {}
        self.nops += 1

    RING = 36

    def dma(self, out, in_, q="sp", ch=None, indirect=None):
        ch = q
        if ch not in self.dsem:
            self.dsem[ch] = {"sems": [self.nc.alloc_semaphore("ds_%s%d" % (ch, i)) for i in range(self.RING)],
                             "cnt": [0] * self.RING, "n": 0}
        ring = self.dsem[ch]
        r = ring["n"] % self.RING
        ring["n"] += 1
        reads = [in_.tile]
        if indirect is not None:
            reads.append(indirect.tile)
        writes = [out.tile]
        self._sync(q, reads, writes)
        key = ("d", ch, r)
        if ring["cnt"][r] > 0 and self.seen[q].get(key, 0) < 16 * ring["cnt"][r]:
            self.eng[q].wait_ge(ring["sems"][r], 16 * ring["cnt"][r])
            self.seen[q][key] = 16 * ring["cnt"][r]
        ring["cnt"][r] += 1
        val = 16 * ring["cnt"][r]
        sem = ring["sems"][r]
        if indirect is None:
            self.eng[q].dma_start(out=out.ap, in_=in_.ap).then_inc(sem, 16)
        else:
            self.eng[q].indirect_dma_start(
                out=out.ap, out_offset=None, in_=in_.ap,
                in_offset=bass.IndirectOffsetOnAxis(ap=indirect.ap, axis=0)).then_inc(sem, 16)
        for t in reads:
            if t.r.get(key, 0) < val:
                t.r[key] = val
        for t in writes:
            t.w = (key, val)
            t.r = {}
        self.nops += 1

    def _all_sems(self):
        items = [(("c", k), self.sem[k], self.cnt[k]) for k in self.sem if self.cnt[k]]
        for ch, ring in self.dsem.items():
            for r in range(self.RING):
                if ring["cnt"][r]:
                    items.append((("d", ch, r), ring["sems"][r], 16 * ring["cnt"][r]))
        return items

    def barrier(self):
        items = self._all_sems()
        for e in self.eng:
            for key, s, v in items:
                if self.seen[e].get(key, 0) >= v:
                    continue
                self.eng[e].wait_ge(s, v)
                self.seen[e][key] = v

    def arena(self, nbytes):
        self.ar = self.nc.alloc_sbuf_tensor("arena", [128, nbytes // 4], F32)
        self.ar_bytes = nbytes
        self.ar_off = 0

    def ar_reset(self):
        self.barrier()
        self.ar_off = 0

    def ar_alloc(self, name, shape, dt=F32):
        esz = 4 if dt in (F32, I32) else 2
        n = 1
        for d_ in shape[1:]:
            n *= d_
        nb = (n * esz + 31) // 32 * 32
        assert self.ar_off + nb <= self.ar_bytes, ("arena overflow", name, self.ar_off, nb, self.ar_bytes)
        ap = self.ar[0:shape[0], self.ar_off // 4:(self.ar_off + nb) // 4]
        if dt != F32:
            ap = ap.bitcast(dt)
        ap = ap[:, 0:n]
        if len(shape) > 2:
            names = " ".join("d%d" % i for i in range(1, len(shape)))
            kw = {"d%d" % i: shape[i] for i in range(1, len(shape))}
            ap = ap.rearrange("p (%s) -> p %s" % (names, names), **kw)
        self.ar_off += nb
        return Tile(ap, name)

    def finish(self):
        for key, sem, v in self._all_sems():
            self.nc.sync.wait_ge(sem, v)

    def mm(self, out, lhsT, rhs, start=True, stop=True):
        self.op("pe", [lhsT, rhs], [out],
                lambda e: e.matmul(out.ap, lhsT=lhsT.ap, rhs=rhs.ap, start=start, stop=stop))

    def tr(self, out, in_, ident):
        self.op("pe", [in_, ident], [out], lambda e: e.transpose(out.ap, in_.ap, ident.ap))

    def act(self, out, in_, func, bias=None, scale=1.0, accum=None):
        rd = [in_] + ([bias] if isinstance(bias, View) else [])
        wr = [out] + ([accum] if accum is not None else [])
        kw = {}
        if bias is not None:
            kw["bias"] = _ap(bias)
        if accum is not None:
            kw["accum_out"] = accum.ap
        self.op("act", rd, wr, lambda e: e.activation(out=out.ap, in_=in_.ap, func=func, scale=scale, **kw))

    def tt(self, e, out, in0, in1, op):
        self.op(e, [in0, in1], [out], lambda g: g.tensor_tensor(out=out.ap, in0=in0.ap, in1=in1.ap, op=op))

    def ts(self, e, out, in0, s1, op0, s2=None, op1=None):
        rd = [in0] + [s for s in (s1, s2) if isinstance(s, View)]
        if op1 is None:
            self.op(e, rd, [out], lambda g: g.tensor_scalar(out=out.ap, in0=in0.ap, scalar1=_ap(s1), scalar2=None, op0=op0))
        else:
            self.op(e, rd, [out], lambda g: g.tensor_scalar(out=out.ap, in0=in0.ap, scalar1=_ap(s1), scalar2=_ap(s2), op0=op0, op1=op1))

    def stt(self, out, in0, scalar, in1, op0, op1):
        rd = [in0, in1] + ([scalar] if isinstance(scalar, View) else [])
        self.op("dve", rd, [out], lambda g: g.scalar_tensor_tensor(out=out.ap, in0=in0.ap, scalar=_ap(scalar), in1=in1.ap, op0=op0, op1=op1))

    def rsqrt(self, out, in_):
        self.act(out, in_, AF.Ln)
        self.act(out, out, AF.Exp, scale=-0.5)

    def cp(self, e, out, in_):
        if e == "act":
            self.act(out, in_, AF.Copy)
        else:
            self.op(e, [in_], [out], lambda g: g.tensor_copy(out=out.ap, in_=in_.ap))

    def memset(self, e, out, val):
        self.op(e, [], [out], lambda g: g.memset(out.ap, val))

    def red(self, out, in_, op=ALU.add):
        self.op("dve", [in_], [out], lambda g: g.tensor_reduce(out=out.ap, in_=in_.ap, axis=AX.X, op=op))

    def max8(self, out, in_):
        self.op("dve", [in_], [out], lambda g: g.max(out=out.ap, in_=in_.ap))

    def match_replace(self, out, rep, vals, imm):
        self.op("dve", [rep, vals], [out], lambda g: g.match_replace(out=out.ap, in_to_replace=rep.ap, in_values=vals.ap, imm_value=imm))


def t5_thresholds():
    n = np.arange(0, 400)
    exact = 16
    large = exact + (np.log(np.maximum(n, 1).astype(np.float32) / np.float32(exact)) / np.float32(math.log(128 / exact))
                     * np.float32(32 - exact)).astype(np.int32)
    b = np.where(n < exact, n, np.minimum(large, 31))
    thr = []
    for k in range(1, 32):
        thr.append(int(np.min(n[b >= k])))
    return thr


class StopBuild(Exception):
    pass


class Cfg:
    stop = None

    def __init__(self, SEQ=4096, NS=4, TS=8, PAST=16384, NPOOL=5120):
        self.SEQ, self.NS, self.TS, self.PAST, self.NPOOL = SEQ, NS, TS, PAST, NPOOL
        self.NPG = PAST // 128
        self.NT = SEQ // 128
        self.NBP = SEQ // 64
        self.WIN = 512


def build(cfg):
    nc = bass.Bass("TRN2", target_bir_lowering=False)
    cx = Cx(nc)
    SEQ, NS, TS, PAST, NPG, NT, NBP = cfg.SEQ, cfg.NS, cfg.TS, cfg.PAST, cfg.NPG, cfg.NT, cfg.NBP
    NSTOK = NS * TS
    J0 = NBP - 2
    NBS = PAST // 64
    WLC = NBP + 16
    NTOK = SEQ + NSTOK
    NR2 = cfg.NPOOL * DEPTH * 128 * 2

    def chk(n):
        if cfg.stop is not None and cfg.stop == n:
            raise StopBuild()

    def din(name, shape, dt=F32):
        return cx.dram(name, shape, dt, kind="ExternalInput")

    def dout(name, shape, dt=F32):
        return cx.dram(name, shape, dt, kind="ExternalOutput")

    xp = din("xp", [SEQ, D]); cpd = din("cp", [1, D]); xs = din("xs", [NSTOK, D]); csd = din("cs", [NS, D])
    pool_d = din("pool", [NR2, 256])
    ptab = din("ptab", [NS, NPG], I32)
    win_in = din("win_in", [DEPTH, NS, 512, 256])
    st_conv = din("st_conv", [DEPTH, NS, 30, 256]); st_ret = din("st_ret", [DEPTH, NS, 4, 64, 64])
    st_gconv = din("st_gconv", [DEPTH, NS, 3, 768]); st_gdn = din("st_gdn", [DEPTH, NS, 4, 64, 64])
    st_ffn = din("st_ffn", [DEPTH, NS, 2, D_FF])
    w_ada = din("w_ada", [DEPTH, D, 6 * D]); b_ada = din("b_ada", [DEPTH, 6 * D]); norms = din("norms", [DEPTH, 4, D])
    w_in = din("w_in", [DEPTH, D, IN_COLS]); cmp_pool = din("cmp_pool", [DEPTH, 2, 64]); cmp_pe = din("cmp_pe", [DEPTH, 2, 64, 64])
    rel_bias = din("rel_bias", [32, 4]); conv_dw = din("conv_dw", [DEPTH, 31, 256]); conv_dw_b = din("conv_dw_b", [DEPTH, 256])
    conv_ln_g = din("conv_ln_g", [DEPTH, 256]); conv_ln_b = din("conv_ln_b", [DEPTH, 256]); ret_gn = din("ret_gn", [DEPTH, 256])
    gdn_conv_w = din("gdn_conv_w", [DEPTH, 4, 768]); gdn_A_log = din("gdn_A_log", [DEPTH, 4]); gdn_dt_bias = din("gdn_dt_bias", [DEPTH, 4])
    gdn_norm = din("gdn_norm", [DEPTH, 64]); w_branch = din("w_branch", [DEPTH, 4, 256, D]); w_out = din("w_out", [DEPTH, D, D])
    ffn_up = din("ffn_up", [DEPTH, D, 2 * D_FF]); ffn_dw = din("ffn_dw", [DEPTH, 3, D_FF]); ffn_down = din("ffn_down", [DEPTH, D_FF, D])
    c_ident = din("c_ident", [128, 128]); c_triu = din("c_triu", [128, 256]); c_tril = din("c_tril", [128, 128])
    c_U = din("c_U", [128, 128]); c_last = din("c_last", [2, 128, 128])
    c_rot_p = din("c_rot_p", [SEQ, 64]); c_rot_s = din("c_rot_s", [TS, 64])
    c_retD = din("c_retD", [2, 4, 128, 128]); c_retxi = din("c_retxi", [2, 4, 64, 128]); c_retz = din("c_retz", [2, 128, 4])
    c_dbt = din("c_dbt", [128, 256]); c_dlc = din("c_dlc", [128, WLC]); c_keep = din("c_keep", [128, 2 * WLC])
    c_keeps = din("c_keeps", [8, 2 * (NBS + 1)])
    c_wm4 = din("c_wm4", [128, 128]); c_iota = din("c_iota", [128, 1])

    yp = dout("yp", [SEQ, D]); ys = dout("ys", [NSTOK, D])
    kvp = dout("kvp", [DEPTH, SEQ, 512]); kvs = dout("kvs", [NS, DEPTH, TS, 512])
    winp = dout("winp", [DEPTH, 512, 256]); wins = dout("wins", [DEPTH, NS, 512, 256])
    convp = dout("convp", [DEPTH, 30, 256]); convs = dout("convs", [DEPTH, NS, 30, 256])
    retp = dout("retp", [DEPTH, 4, 64, 64]); rets = dout("rets", [DEPTH, NS, 4, 64, 64])
    gcp = dout("gcp", [DEPTH, 3, 768]); gcs = dout("gcs", [DEPTH, NS, 3, 768])
    gdnp = dout("gdnp", [DEPTH, 4, 64, 64]); gdns = dout("gdns", [DEPTH, NS, 4, 64, 64])
    ffnp = dout("ffnp", [DEPTH, 2, D_FF]); ffns = dout("ffns", [DEPTH, NS, 2, D_FF])

    xres = cx.dram("xres", [8, 128, NTOK])
    brd = cx.dram("brd", [8, 128, NTOK], BF16)
    bt_d = cx.dram("bt_d", [128, 4 * 256]); lc_d = cx.dram("lc_d", [128, 4 * WLC])

    MTN = 256
    ident = cx.sb("ident", [128, 128]); cx.dma(ident[:], c_ident[:])
    ones_b = cx.sb("ones_b", [128, 128], BF16); cx.memset("dve", ones_b[:], 1.0)
    ones_f = cx.sb("ones_f", [128, 128]); cx.memset("dve", ones_f[:], 1.0)
    rb = cx.sb("rb", [128, 128]); cx.dma(rb[:], rel_bias.re("b h -> (b h)").pbc(128))
    c31 = rb[:, 124:128]
    xT = cx.sb("xT", [128, 8, MTN]); hT = cx.sb("hT", [128, 8, MTN], BF16)
    sqb = cx.sb("sqb", [128, MTN], BF16); rstd = cx.sb("rstd", [128, MTN]); tmpN = cx.sb("tmpN", [128, MTN])
    tmpN2 = cx.sb("tmpN2", [128, MTN]); tmpNs = [tmpN, tmpN2]
    xtok = cx.sb("xtok", [128, D]); stg = cx.sb("stg", [128, 128]); sttok = cx.sb("sttok", [32, 128])
    g1 = cx.sb("g1", [128, MTN]); g2 = cx.sb("g2", [128, MTN]); g3 = cx.sb("g3", [128, MTN])
    NG = NS + 1
    modT = cx.sb("modT", [128, 48, NG]); cT = cx.sb("cT", [128, 8, NG], BF16); nrm = cx.sb("nrm", [128, 4, 8])
    badaT = cx.sb("badaT", [128, 48]); mv = cx.sb("mv", [128, NG, 6, 8])
    cx.arena(178 * 1024)

    ps_tmp = [cx.ps("pt%d" % i, [128, 512]) for i in range(5)]
    ps_acc = [cx.ps("pa%d" % i, [128, 512]) for i in range(3)]
    rr = {"tmp": 0, "acc": 0, "ev": 0, "eb": 0}

    def ptmp():
        rr["tmp"] = (rr["tmp"] + 1) % len(ps_tmp)
        return ps_tmp[rr["tmp"]]

    def pacc():
        rr["acc"] = (rr["acc"] + 1) % len(ps_acc)
        return ps_acc[rr["acc"]]

    def evac(out, in_):
        rr["ev"] ^= 1
        cx.cp("dve" if rr["ev"] else "act", out, in_)

    def transpose_to(out_sb, in_sb, rows, cols):
        p = ptmp()
        cx.tr(p[0:cols, 0:rows], in_sb, ident[0:rows, 0:rows])
        evac(out_sb, p[0:cols, 0:rows])

    def load_T(dst, src2d, r):
        cx.dma(stg[0:r, :], src2d)
        transpose_to(dst, stg[0:r, :], r, 128)

    def wview(tile, kc, n):
        return tile[:].re("p (k n) -> p k n", k=kc)

    def load_w(dst, src2d, ncols, chunk=2048):
        v = src2d.re("(k p) n -> p k n", p=128)
        for c0 in range(0, ncols, chunk):
            c1 = min(ncols, c0 + chunk)
            cx.dma(dst[:, :, c0:c1], v[:, :, c0:c1], q="pool")

    groups = [dict(kind="s", s=s, T=TS, ntile=1, tok0=SEQ + s * TS, gi=1) for s in range(NS)]
    groups.append(dict(kind="p", s=0, T=128, ntile=NT, tok0=0, gi=0))

    thr = t5_thresholds()

    def build_tables():
        cx.ar_reset()
        rbd = cx.ar_alloc("rbd", [128, 128]); dbt = cx.ar_alloc("dbt", [128, 256]); dlc = cx.ar_alloc("dlc", [128, WLC])
        BTt = cx.ar_alloc("BTt", [128, 4, 256]); LCt = cx.ar_alloc("LCt", [128, 4, WLC]); btmp = cx.ar_alloc("btmp", [128, 256])
        cx.tt("dve", rbd[:, 4:128], rb[:, 4:128], rb[:, 0:124], ALU.subtract)
        cx.dma(dbt[:], c_dbt[:]); cx.dma(dlc[:], c_dlc[:])
        chk(0.1)
        for (tab, dtab, W) in ((BTt, dbt, 256), (LCt, dlc, WLC)):
            for h in range(4):
                dst = tab[:, h, :]
                cx.ts("dve", dst, dtab[:, 0:W], 0.0, ALU.is_lt, NEG, ALU.mult)
                cx.ts("dve", dst, dst, rb[:, h:h + 1], ALU.add)
                for b in range(1, 32):
                    cx.ts("dve", btmp[:, 0:W], dtab[:, 0:W], float(thr[b - 1]), ALU.is_ge, rbd[:, 4 * b + h:4 * b + h + 1], ALU.mult)
                    cx.tt("dve", dst, dst, btmp[:, 0:W], ALU.add)
                chk(0.2)
        chk(0.3)
        cx.dma(bt_d[:], BTt[:].re("p h w -> p (h w)")); cx.dma(lc_d[:], LCt[:].re("p h w -> p (h w)"))

    def load_cT():
        ctok = cx.ar_alloc("ctok", [16, D])
        cx.dma(ctok[0:NS, :], csd[:]); cx.dma(ctok[NS:NS + 1, :], cpd[:])
        cx.act(ctok[0:NG, :], ctok[0:NG, :], AF.Silu)
        for kc in range(8):
            p = ptmp()
            cx.tr(p[:, 0:NG], ctok[0:NG, kc * 128:(kc + 1) * 128], ident[0:NG, 0:NG])
            evac(cT[:, kc, :], p[:, 0:NG])

    def compute_mod(l):
        cx.ar_reset()
        adaw = [wview(cx.ar_alloc("adaw%d" % i, [128, 8 * 512], BF16), 8, 512) for i in range(2)]
        cx.dma(stg[0:48, :], b_ada[l].re("(c p) -> c p", p=128))
        transpose_to(badaT[:, 0:48], stg[0:48, :], 48, 128)
        cx.dma(stg[0:32, :], norms[l].re("j (c p) -> (j c) p", p=128))
        transpose_to(nrm[:].re("p j c -> p (j c)"), stg[0:32, :], 32, 128)
        for g in range(12):
            wt = adaw[g % 2]
            cx.dma(wt, w_ada[l].re("(kc p) n -> p kc n", p=128)[:, :, g * 512:(g + 1) * 512], q="pool")
            for b in range(4):
                p = ptmp()
                for kc in range(8):
                    cx.mm(p[:, 0:NG], wt[:, kc, b * 128:(b + 1) * 128], cT[:, kc, :], start=(kc == 0), stop=(kc == 7))
                blk = g * 4 + b
                cx.ts("dve", modT[:, blk, :], p[:, 0:NG], badaT[:, blk:blk + 1], ALU.add)
        for g in range(NG):
            m = lambda j: modT[:, j * 8:(j + 1) * 8, g]
            cx.stt(mv[:, g, 0, :], m(1), 1.0, nrm[:, 0, :], ALU.add, ALU.mult)
            cx.cp("dve", mv[:, g, 1, :], m(0))
            cx.tt("dve", mv[:, g, 2, :], m(2), nrm[:, 1, :], ALU.mult)
            cx.stt(mv[:, g, 3, :], m(4), 1.0, nrm[:, 2, :], ALU.add, ALU.mult)
            cx.cp("dve", mv[:, g, 4, :], m(3))
            cx.tt("dve", mv[:, g, 5, :], m(5), nrm[:, 3, :], ALU.mult)

    def rms_rstd(src, N, nchunk=8):
        p = ptmp()
        for kc in range(nchunk):
            cx.act(sqb[:, 0:N], src[:, kc, 0:N], AF.Square)
            cx.mm(p[:, 0:N], ones_b[:, :], sqb[:, 0:N], start=(kc == 0), stop=(kc == nchunk - 1))
        cx.ts("dve", rstd[:, 0:N], p[:, 0:N], 1.0 / D, ALU.mult, EPS, ALU.add)
        cx.rsqrt(rstd[:, 0:N], rstd[:, 0:N])

    def mod_norm(g, N, jg, js):
        rms_rstd(xT, N)
        for kc in range(8):
            tn = tmpNs[kc % 2]
            cx.stt(tn[:, 0:N], xT[:, kc, 0:N], mv[:, g, jg, kc:kc + 1], rstd[:, 0:N], ALU.mult, ALU.mult)
            cx.act(hT[:, kc, 0:N], tn[:, 0:N], AF.Identity, bias=mv[:, g, js, kc:kc + 1])

    def resid_add(g, N, src, jg):
        rms_rstd(src, N)
        for kc in range(8):
            tn = tmpNs[kc % 2]
            cx.stt(tn[:, 0:N], src[:, kc, 0:N], mv[:, g, jg, kc:kc + 1], rstd[:, 0:N], ALU.mult, ALU.mult)
            cx.tt("pool", xT[:, kc, 0:N], xT[:, kc, 0:N], tn[:, 0:N], ALU.add)

    def load_x(l, grp, t0, N):
        if l == 0:
            src = xp if grp["kind"] == "p" else xs
            base = t0 if grp["kind"] == "p" else grp["s"] * TS + t0
            for j in range(0, N, 128):
                n = min(128, N - j)
                cx.dma(xtok[0:n, :], src[base + j:base + j + n, :])
                for kc in range(8):
                    p = ptmp()
                    cx.tr(p[:, 0:n], xtok[0:n, kc * 128:(kc + 1) * 128], ident[0:n, 0:n])
                    evac(xT[:, kc, j:j + n], p[:, 0:n])
        else:
            a = grp["tok0"] + t0
            cx.dma(xT[:, :, 0:N], xres[:, :, a:a + N].re("c p n -> p c n"))

    def store_x(grp, t0, N, final):
        a = grp["tok0"] + t0
        if not final:
            cx.dma(xres[:, :, a:a + N].re("c p n -> p c n"), xT[:, :, 0:N])
        else:
            dst = yp if grp["kind"] == "p" else ys
            base = t0 if grp["kind"] == "p" else grp["s"] * TS + t0
            for j in range(0, N, 128):
                n = min(128, N - j)
                for kc in range(8):
                    p = ptmp()
                    cx.tr(p[0:n, 0:128], xT[:, kc, j:j + n], ident[:, :])
                    evac(xtok[0:n, kc * 128:(kc + 1) * 128], p[0:n, 0:128])
                cx.dma(dst[base + j:base + j + n, :], xtok[0:n, :])

    def run_gens(gens):
        live = list(gens)
        while live:
            for g in list(live):
                try:
                    next(g)
                except StopIteration:
                    live.remove(g)

    def mts(grp, mtn):
        ntok = grp["T"] * grp["ntile"]
        return [(t0, min(mtn, ntok - t0)) for t0 in range(0, ntok, mtn)]

    def phase2(l):
        cx.ar_reset()
        W_UP = wview(cx.ar_alloc("W_UP", [128, 8 * 5632], BF16), 8, 5632)
        W_DN = wview(cx.ar_alloc("W_DN", [128, 22 * 1024], BF16), 22, 1024)
        ffw = cx.ar_alloc("ffw", [128, 22, 3]); ffhalo = cx.ar_alloc("ffhalo", [128, 22, 2])
        ffts = [cx.ar_alloc("fft%d" % k, [128, 2 + MTN]) for k in range(2)]
        gsets = [(g1, g2, g3), tuple(cx.ar_alloc("gx%d" % k, [128, MTN]) for k in range(3))]
        actT = cx.ar_alloc("actT", [128, 22, MTN], BF16); fT = cx.ar_alloc("fT", [128, 8, MTN])
        load_w(W_UP, ffn_up[l], 5632); load_w(W_DN, ffn_down[l], 1024)
        for c in range(22):
            load_T(ffw[:, c, :], ffn_dw[l, :, c * 128:(c + 1) * 128], 3)
        for gi, grp in enumerate(groups):
            if grp["kind"] == "p":
                cx.memset("pool", ffhalo[:], 0.0)
            else:
                for c in range(22):
                    cx.dma(sttok[0:2, :], st_ffn[l, grp["s"], :, c * 128:(c + 1) * 128])
                    transpose_to(ffhalo[:, c, :], sttok[0:2, :], 2, 128)
            for (t0, N) in mts(grp, MTN):
                load_x(1, grp, t0, N)
                mod_norm(gi, N, 3, 4)
                for c in range(22):
                    pg = ptmp()
                    for kc in range(8):
                        cx.mm(pg[:, 0:N], W_UP[:, kc, c * 128:(c + 1) * 128], hT[:, kc, 0:N], start=(kc == 0), stop=(kc == 7))
                    pv = ptmp()
                    for kc in range(8):
                        cx.mm(pv[:, 0:N], W_UP[:, kc, D_FF + c * 128:D_FF + (c + 1) * 128], hT[:, kc, 0:N], start=(kc == 0), stop=(kc == 7))
                    fft_ = ffts[c % 2]; ga, gb, gc_ = gsets[c % 2]
                    cx.cp("pool", fft_[:, 0:2], ffhalo[:, c, :])
                    cx.cp("act", fft_[:, 2:2 + N], pg[:, 0:N])
                    cx.cp("pool", ffhalo[:, c, :], fft_[:, N:N + 2])
                    cx.ts("dve", ga[:, 0:N], fft_[:, 0:N], ffw[:, c, 0:1], ALU.mult)
                    cx.stt(ga[:, 0:N], fft_[:, 1:1 + N], ffw[:, c, 1:2], ga[:, 0:N], ALU.mult, ALU.add)
                    cx.stt(ga[:, 0:N], fft_[:, 2:2 + N], ffw[:, c, 2:3], ga[:, 0:N], ALU.mult, ALU.add)
                    cx.tt("pool", gb[:, 0:N], ga[:, 0:N], ga[:, 0:N], ALU.mult)
                    cx.ts("dve", gb[:, 0:N], gb[:, 0:N], 0.044715, ALU.mult, 1.0, ALU.add)
                    cx.tt("pool", gb[:, 0:N], gb[:, 0:N], ga[:, 0:N], ALU.mult)
                    cx.act(gc_[:, 0:N], gb[:, 0:N], AF.Sigmoid, scale=1.5957691216)
                    cx.tt("dve", gc_[:, 0:N], gc_[:, 0:N], ga[:, 0:N], ALU.mult)
                    cx.tt("dve", actT[:, c, 0:N], gc_[:, 0:N], pv[:, 0:N], ALU.mult)
                for ob in range(8):
                    p = ptmp()
                    for c in range(22):
                        cx.mm(p[:, 0:N], W_DN[:, c, ob * 128:(ob + 1) * 128], actT[:, c, 0:N], start=(c == 0), stop=(c == 21))
                    evac(fT[:, ob, 0:N], p[:, 0:N])
                resid_add(gi, N, fT, 5)
                store_x(grp, t0, N, final=(l == DEPTH - 1))
            dst = ffnp[l] if grp["kind"] == "p" else ffns[l, grp["s"]]
            for c in range(22):
                transpose_to(sttok[0:2, :], ffhalo[:, c, :], 128, 2)
                cx.dma(dst[:, c * 128:(c + 1) * 128], sttok[0:2, :])

    def phase1b(l):
        cx.ar_reset()
        W_MG = wview(cx.ar_alloc("W_MG", [128, 8 * 4096], BF16), 8, 4096)
        W_BR = wview(cx.ar_alloc("W_BR", [128, 8 * 1024], BF16), 8, 1024)
        W_OUT = wview(cx.ar_alloc("W_OUT", [128, 8 * 1024], BF16), 8, 1024)
        BRT = cx.ar_alloc("BRTb", [128, 8, MTN], BF16); mT = cx.ar_alloc("mT", [128, 8, MTN], BF16); fT = cx.ar_alloc("fTb", [128, 8, MTN])
        gacc = [g1, cx.ar_alloc("gacc1", [128, MTN])]
        ggat = [(g2, g3), (cx.ar_alloc("ggat2", [128, MTN]), cx.ar_alloc("ggat3", [128, MTN]))]
        wv = w_in[l].re("(k p) n -> p k n", p=128)
        for c0 in range(0, 4096, 2048):
            cx.dma(W_MG[:, :, c0:c0 + 2048], wv[:, :, O_MG + c0:O_MG + c0 + 2048], q="pool")
        load_w(W_BR, w_branch[l].re("n k d -> (n k) d"), 1024)
        load_w(W_OUT, w_out[l], 1024)
        for gi, grp in enumerate(groups):
            for (t0, N) in mts(grp, MTN):
                a0 = grp["tok0"] + t0
                load_x(1, grp, t0, N)
                cx.dma(BRT[:, :, 0:N], brd[:, :, a0:a0 + N].re("c p n -> p c n"))
                mod_norm(gi, N, 0, 1)
                for ob in range(8):
                    for n in range(4):
                        pb = ptmp()
                        for kc in range(2):
                            cx.mm(pb[:, 0:N], W_BR[:, n * 2 + kc, ob * 128:(ob + 1) * 128], BRT[:, n * 2 + kc, 0:N], start=(kc == 0), stop=(kc == 1))
                        pg = ptmp()
                        for kc in range(8):
                            cx.mm(pg[:, 0:N], W_MG[:, kc, n * 1024 + ob * 128:n * 1024 + (ob + 1) * 128], hT[:, kc, 0:N], start=(kc == 0), stop=(kc == 7))
                        ga = gacc[ob % 2]; gb, gc_ = ggat[n % 2]
                        cx.act(gb[:, 0:N], pg[:, 0:N], AF.Sigmoid)
                        if n == 0:
                            cx.tt("dve", ga[:, 0:N], gb[:, 0:N], pb[:, 0:N], ALU.mult)
                        else:
                            cx.tt("dve", gc_[:, 0:N], gb[:, 0:N], pb[:, 0:N], ALU.mult)
                            cx.tt("pool", ga[:, 0:N], ga[:, 0:N], gc_[:, 0:N], ALU.add)
                    cx.cp("act", mT[:, ob, 0:N], gacc[ob % 2][:, 0:N])
                for ob in range(8):
                    p = ptmp()
                    for kc in range(8):
                        cx.mm(p[:, 0:N], W_OUT[:, kc, ob * 128:(ob + 1) * 128], mT[:, kc, 0:N], start=(kc == 0), stop=(kc == 7))
                    evac(fT[:, ob, 0:N], p[:, 0:N])
                resid_add(gi, N, fT, 2)
                store_x(grp, t0, N, final=False)


    def phase1a(l):
        cx.ar_reset()
        W1 = wview(cx.ar_alloc("W1", [128, 8 * O_MG], BF16), 8, O_MG)
        triu = cx.ar_alloc("triu", [128, 256]); cx.dma(triu[:], c_triu[:])
        tril = cx.ar_alloc("tril", [128, 128]); cx.dma(tril[:], c_tril[:])
        Umat = cx.ar_alloc("Umat", [128, 128]); cx.dma(Umat[:], c_U[:])
        lastmg = [cx.ar_alloc("lastm%d" % g, [128, 128]) for g in range(2)]
        for g in range(2):
            cx.dma(lastmg[g][:], c_last[g])
        wm4 = cx.ar_alloc("wm4", [128, 128]); cx.dma(wm4[:], c_wm4[:])
        iota = cx.ar_alloc("iota", [128, 1]); cx.dma(iota[:], c_iota[:])
        retDg = [cx.ar_alloc("retD0", [128, 4, 128]), cx.ar_alloc("retD1", [8, 4, 8])]
        cx.dma(retDg[0][:], c_retD[0].re("h s c -> s h c")); cx.dma(retDg[1][:], c_retD[1, :, 0:8, 0:8].re("h s c -> s h c"))
        retxig = [cx.ar_alloc("retxi0", [64, 4, 128]), cx.ar_alloc("retxi1", [64, 4, 8])]
        cx.dma(retxig[0][:], c_retxi[0].re("h d c -> d h c")); cx.dma(retxig[1][:], c_retxi[1, :, :, 0:8].re("h d c -> d h c"))
        retzg = [cx.ar_alloc("retz0", [128, 4]), cx.ar_alloc("retz1", [128, 4])]
        cx.dma(retzg[0][:], c_retz[0]); cx.dma(retzg[1][:], c_retz[1])
        BT = cx.ar_alloc("BT", [128, 4, 256]); cx.dma(BT[:].re("p h w -> p (h w)"), bt_d[:])
        LC = cx.ar_alloc("LC", [128, 4, WLC]); cx.dma(LC[:].re("p h w -> p (h w)"), lc_d[:])
        keepadd = cx.ar_alloc("keepadd", [128, 2 * WLC]); cx.dma(keepadd[:], c_keep[:])
        MT1 = 128
        convw = cx.ar_alloc("convw", [128, 2, 31]); cvp = cx.ar_alloc("cvp", [128, 6]); gdw = cx.ar_alloc("gdw", [128, 6, 4])
        gnb = cx.ar_alloc("gnb", [128, 256]); gnorm = cx.ar_alloc("gnorm", [128, 64]); negA = cx.ar_alloc("negA", [128, 4]); dtb = cx.ar_alloc("dtb", [128, 4])
        PEK = cx.ar_alloc("PEK", [128, 128]); PEV = cx.ar_alloc("PEV", [128, 128])
        plf = cx.ar_alloc("plf", [128, 258]); POOLK2 = cx.ar_alloc("POOLK2", [128, 2], BF16); BIGV = cx.ar_alloc("BIGV", [128, 256], BF16)

        def load_layer_params(l):
            for c in range(2):
                load_T(convw[:, c, :], conv_dw[l, :, c * 128:(c + 1) * 128], 31)
            for j, src in enumerate((conv_dw_b, conv_ln_g, conv_ln_b)):
                cx.dma(stg[2 * j:2 * j + 2, :], src[l].re("(c p) -> c p", p=128))
            transpose_to(cvp[:, 0:6], stg[0:6, :], 6, 128)
            for c in range(6):
                load_T(gdw[:, c, :], gdn_conv_w[l, :, c * 128:(c + 1) * 128], 4)
            cx.dma(gnb[:], ret_gn[l].pbc(128)); cx.dma(gnorm[:], gdn_norm[l].pbc(128))
            cx.dma(negA[:], gdn_A_log[l].pbc(128)); cx.dma(dtb[:], gdn_dt_bias[l].pbc(128))
            cx.act(negA[:], negA[:], AF.Exp)
            cx.ts("dve", negA[:], negA[:], -1.0, ALU.mult)
            for half in range(2):
                for kv in range(2):
                    cx.dma(PEK[half * 64:(half + 1) * 64, kv * 64:(kv + 1) * 64], cmp_pe[l, 0])
                    cx.dma(PEV[half * 64:(half + 1) * 64, kv * 64:(kv + 1) * 64], cmp_pe[l, 1])
            cx.memset("dve", plf[:], 0.0)
            pk = cmp_pool[l, 0].re("(p o) -> p o", o=1); pv = cmp_pool[l, 1].re("(p o) -> p o", o=1)
            cx.dma(plf[0:64, 0:1], pk); cx.dma(plf[64:128, 1:2], pk)
            cx.dma(plf[0:64, 128:129], pv); cx.dma(plf[64:128, 129:130], pv)
            cx.cp("dve", POOLK2[:], plf[:, 0:2]); cx.cp("dve", BIGV[:], plf[:, 2:258])

        UB = cx.ar_alloc("UB", [128, 2, 30 + MT1]); GB = cx.ar_alloc("GB", [128, 6, 3 + MT1])
        CY = cx.ar_alloc("CY", [128, 2, MT1]); CY2 = cx.ar_alloc("CY2", [128, 2, MT1]); GY = cx.ar_alloc("GY", [128, 6, MT1])
        SR = cx.ar_alloc("SR", [64, 4, 64]); SGs = cx.ar_alloc("SGs", [64, 4, 64])
        KSEL = cx.ar_alloc("KSEL", [128, SEQ], BF16); VSEL = cx.ar_alloc("VSEL", [128, NT, 2, 65], BF16)
        KWIN = cx.ar_alloc("KWIN", [128, 5, 128], BF16); VWIN = cx.ar_alloc("VWIN", [128, 5, 2, 65], BF16)
        NBK = max(NBP, NBS); NVT = (NBK + 127) // 128
        KC = cx.ar_alloc("KC", [128, NBK], BF16); VC = cx.ar_alloc("VC", [128, NVT, 2, 65], BF16); VCf = cx.ar_alloc("VCf", [128, 128])
        KNEW = cx.ar_alloc("KNEW", [128, 8], BF16); VNEW = cx.ar_alloc("VNEW", [8, 2, 65], BF16)
        for t_ in (VSEL, VWIN, VC, VNEW):
            cx.memset("pool", t_[:], 1.0)
        QA = cx.ar_alloc("QA", [128, MT1], BF16); QB = cx.ar_alloc("QB", [128, MT1], BF16)
        BRT = cx.ar_alloc("BRT", [128, 4, 2, MT1], BF16)
        KVT = cx.ar_alloc("KVT", [128, 780]); RETT = cx.ar_alloc("RETT", [128, 1024]); ZAB = cx.ar_alloc("ZAB", [128, 264])
        RK = cx.ar_alloc("RK", [128, 128], BF16); RV = cx.ar_alloc("RV", [128, 128], BF16)
        Ebuf = [cx.ar_alloc("Eb%d" % k, [128, 512]) for k in range(2)]
        PTb = [cx.ar_alloc("PTb%d" % k, [128, 4, 128], BF16) for k in range(2)]
        OACC = cx.ar_alloc("OACC", [128, 3, 4, 65])
        WS = max(NBK + 1, 16)
        PCN = cx.ar_alloc("PCN", [128, 4, NBK]); SC = cx.ar_alloc("SC", [128, WS]); SC2 = cx.ar_alloc("SC2", [128, WS]); SEL = cx.ar_alloc("SEL", [128, 2, WS])
        M8 = cx.ar_alloc("M8", [128, 16]); rs4 = cx.ar_alloc("rs4", [128, 8]); GT = cx.ar_alloc("GT", [128, 12]); FF = cx.ar_alloc("FF", [128, 12])
        ONSA = cx.ar_alloc("ONSA", [128, 256]); OTMP = cx.ar_alloc("OTMP", [128, 256])
        ROT = cx.ar_alloc("ROT", [128, 64]); QR = cx.ar_alloc("QR", [128, 256]); KR = cx.ar_alloc("KR", [128, 256]); RT1 = cx.ar_alloc("RT1", [128, 128]); RT2 = cx.ar_alloc("RT2", [128, 128])
        QT = cx.ar_alloc("QT", [64, 128]); QXT = cx.ar_alloc("QXT", [64, 128]); KT = cx.ar_alloc("KT", [64, 128]); KZ = cx.ar_alloc("KZ", [128, 64]); ATT = cx.ar_alloc("ATT", [128, 128])
        ST8 = cx.ar_alloc("ST8", [128, 8]); SIL = cx.ar_alloc("SIL", [128, 256])
        GQ = cx.ar_alloc("GQ", [128, 256]); GK = cx.ar_alloc("GK", [128, 256]); GV = cx.ar_alloc("GV", [128, 256])
        GG = cx.ar_alloc("GG", [128, 4]); GBETA = cx.ar_alloc("GBETA", [128, 4]); GC = cx.ar_alloc("GC", [128, 4]); GLB = cx.ar_alloc("GLB", [128, 4])
        EGC = cx.ar_alloc("EGC", [128, 4]); EKD = cx.ar_alloc("EKD", [128, 4]); EGL = cx.ar_alloc("EGL", [128, 4]); G4 = cx.ar_alloc("G4", [128, 4]); G4b = cx.ar_alloc("G4b", [128, 4])
        KBt = cx.ar_alloc("KBt", [128, 64]); RU = cx.ar_alloc("RU", [128, 64]); KBE = cx.ar_alloc("KBE", [128, 64]); QD = cx.ar_alloc("QD", [128, 64]); KD = cx.ar_alloc("KD", [128, 64])
        gKT = cx.ar_alloc("gKT", [64, 128]); gKBT = cx.ar_alloc("gKBT", [64, 128]); gQT = cx.ar_alloc("gQT", [64, 128]); gQDT = cx.ar_alloc("gQDT", [64, 128])
        GREP = cx.ar_alloc("GREP", [128, 128]); LMU = cx.ar_alloc("LMU", [128, 128]); LML = cx.ar_alloc("LML", [128, 128]); LMT = cx.ar_alloc("LMT", [128, 128])
        GATT = cx.ar_alloc("GATT", [128, 128]); YM = cx.ar_alloc("YM", [128, 128])
        PQ = [[cx.ar_alloc("PQ%d%d" % (a, b), [128, 128]) for b in range(2)] for a in range(2)]
        USB = cx.ar_alloc("USB", [128, 64]); WT = cx.ar_alloc("WT", [64, 128]); VN = cx.ar_alloc("VN", [128, 64]); OG = cx.ar_alloc("OG", [128, 256])
        PTI0 = cx.ar_alloc("PTI0", [128, NPG], I32); PTI1 = cx.ar_alloc("PTI1", [128, NPG], I32); PTF = cx.ar_alloc("PTF", [128, NPG])
        PGS = [cx.ar_alloc("PGS%d" % k, [128, 256]) for k in range(8)]
        RKs = [RK, cx.ar_alloc("RK1", [128, 128], BF16)]; RVs = [RV, cx.ar_alloc("RV1", [128, 128], BF16)]
        KPGs = [cx.ar_alloc("KPG%d" % k, [128, 4, 128], BF16) for k in range(2)]
        VPGs = [cx.ar_alloc("VPG%d" % k, [128, 4, 2, 65], BF16) for k in range(2)]
        for t_ in VPGs:
            cx.memset("pool", t_[:], 1.0)
        LCS2 = cx.ar_alloc("LCS2", [8, 4, 128]); keeps = cx.ar_alloc("keeps", [8, 2 * (NBS + 1)]); cx.dma(keeps[:], c_keeps[0:8, :])
        WINT = cx.ar_alloc("WINT", [128, 256])
        for h in range(4):
            cx.ts("dve", LCS2[0:8, h, :], ones_f[0:8, 0:128], c31[0:8, h:h + 1], ALU.mult)
            cx.cp("dve", LCS2[0:8, h, 126:128], LC[0:8, h, J0 - 2:J0])
        eb = {"i": 0}

        def proj_fm(col0, ncols, N):
            p = ptmp()
            for kc in range(8):
                cx.mm(p[0:ncols, 0:N], W1[:, kc, col0:col0 + ncols], hT[:, kc, 0:N], start=(kc == 0), stop=(kc == 7))
            return p[0:ncols, 0:N]

        def proj_tm(col0, ncols, c0, T):
            p = ptmp()
            for kc in range(8):
                cx.mm(p[0:T, 0:ncols], hT[:, kc, c0:c0 + T], W1[:, kc, col0:col0 + ncols], start=(kc == 0), stop=(kc == 7))
            return p[0:T, 0:ncols]

        def hprt(h):
            return (0, 64) if h < 2 else (64, 128)

        def attend(T, h, br, qv, tiles, near_bt=None, kT_all=None):
            ntot = sum(t["nk"] for t in tiles)
            p = ptmp(); off = 0
            if kT_all is not None:
                cx.mm(p[0:T, 0:ntot], qv, kT_all)
            else:
                for t in tiles:
                    cx.mm(p[0:T, off:off + t["nk"]], qv, t["kT"]); off += t["nk"]
            eb["i"] ^= 1
            E = Ebuf[eb["i"]]; PT = PTb[eb["i"]]
            if near_bt is None:
                cx.act(E[0:T, 0:ntot], p[0:T, 0:ntot], AF.Exp, bias=c31[0:T, h:h + 1])
            else:
                cx.tt("dve", E[0:T, 0:ntot], p[0:T, 0:ntot], near_bt, ALU.add)
                cx.act(E[0:T, 0:ntot], E[0:T, 0:ntot], AF.Exp)
            off = 0
            for t in tiles:
                if t.get("mask") is not None:
                    ev_ = E[0:T, off:off + t["nk"]]
                    if t.get("mre"):
                        ev_ = ev_.re("p (b k) -> p b k", k=t["mre"])
                    cx.tt("dve" if T <= 8 else "pool", ev_, ev_, t["mask"], ALU.mult)
                off += t["nk"]
            p2 = ptmp(); off = 0
            for k, t in enumerate(tiles):
                cx.tr(p2[0:t["nk"], k * T:(k + 1) * T], E[0:T, off:off + t["nk"]], ident[0:T, 0:T]); off += t["nk"]
            nt = len(tiles)
            evac(PT[:, 0:nt, 0:T], p2[:, 0:nt * T].re("p (k t) -> p k t", t=T))
            p3 = ptmp()
            for k, t in enumerate(tiles):
                cx.mm(p3[0:T, 0:65], PT[0:t["nk"], k, 0:T], t["v"], start=(k == 0), stop=(k == nt - 1))
            cx.tt("dve", OACC[0:T, br, h, :], OACC[0:T, br, h, :], p3[0:T, 0:65], ALU.add)
            return E

        def nsa_select(T, nbv, W, keepv, addv):
            for kv in range(2):
                cx.memset("pool", SC[0:T, 0:W], 0.0)
                cx.tt("dve", SC[0:T, 0:nbv], PCN[0:T, 2 * kv, 0:nbv], PCN[0:T, 2 * kv + 1, 0:nbv], ALU.add)
                cx.tt("dve", SC[0:T, 0:W], SC[0:T, 0:W], keepv, ALU.mult)
                cx.tt("dve", SC[0:T, 0:W], SC[0:T, 0:W], addv, ALU.add)
                cx.memset("dve", SC[0:T, 0:1], 2.0)
                cx.max8(M8[0:T, 0:8], SC[0:T, 0:W])
                cx.match_replace(SC2[0:T, 0:W], M8[0:T, 0:8], SC[0:T, 0:W], -5.0)
                cx.max8(M8[0:T, 8:16], SC2[0:T, 0:W])
                cx.ts("dve", SEL[0:T, kv, 0:W], SC[0:T, 0:W], M8[0:T, 15:16], ALU.is_ge)
                cx.stt(SEL[0:T, kv, 0:W], SC[0:T, 0:W], -0.5, SEL[0:T, kv, 0:W], ALU.is_gt, ALU.mult)

        def nsa_compressed(T, c0, nbv, lcv):
            for h in range(4):
                a, b = hprt(h); kv = h // 2
                qv = (QA if h % 2 == 0 else QB)[a:b, c0:c0 + T]
                for t0 in range(0, nbv, 128):
                    nk = min(128, nbv - t0)
                    tl = [dict(kT=KC[a:b, t0:t0 + nk], v=VC[0:nk, t0 // 128, kv, :], nk=nk)]
                    E = attend(T, h, 0, qv, tl, near_bt=lcv(h, t0, nk))
                    cx.cp("pool", PCN[0:T, h, t0:t0 + nk], E[0:T, 0:nk])
                cx.red(rs4[0:T, h:h + 1], PCN[0:T, h, 0:nbv])
                cx.ts("dve", rs4[0:T, h:h + 1], rs4[0:T, h:h + 1], 1e-30, ALU.max)
                cx.op("dve", [rs4[0:T, h:h + 1]], [rs4[0:T, 4 + h:5 + h]],
                      lambda g, hh=h: g.reciprocal(out=rs4[0:T, 4 + hh:5 + hh].ap, in_=rs4[0:T, hh:hh + 1].ap))
                cx.ts("dve", PCN[0:T, h, 0:nbv], PCN[0:T, h, 0:nbv], rs4[0:T, 4 + h:5 + h], ALU.mult)

        def nsa_combine(T, c0):
            cx.act(GT[0:T, :], KVT[0:T, 768:780], AF.Sigmoid)
            den = OACC[0:T, :, :, 64]
            cx.ts("dve", FF[0:T, :].re("p (b h) -> p b h", h=4), den, 1e-30, ALU.max)
            cx.op("dve", [FF[0:T, :]], [FF[0:T, :]], lambda g: g.reciprocal(out=FF[0:T, :].ap, in_=FF[0:T, :].ap))
            cx.tt("dve", FF[0:T, :], FF[0:T, :], GT[0:T, :], ALU.mult)
            o3 = ONSA[0:T, :].re("p (h e) -> p h e", e=64); t3 = OTMP[0:T, :].re("p (h e) -> p h e", e=64)
            for br in range(3):
                f = FF[0:T, br * 4:(br + 1) * 4].unsq(2).bc([T, 4, 64])
                if br == 0:
                    cx.tt("dve", o3, OACC[0:T, br, :, 0:64], f, ALU.mult)
                else:
                    cx.tt("pool", t3, OACC[0:T, br, :, 0:64], f, ALU.mult)
                    cx.tt("pool", o3, o3, t3, ALU.add)
            for c in range(2):
                transpose_to(BRT[:, 0, c, c0:c0 + T], ONSA[0:T, c * 128:(c + 1) * 128], T, 128)

        def nsa_prompt_tile(l, i, c0):
            T = 128; p0 = i * 128
            cx.cp("pool", VSEL[0:T, i, :, 0:64], KVT[0:T, 384:512].re("p (k d) -> p k d", d=64))
            cx.cp("pool", VWIN[0:T, i % 5, :, 0:64], KVT[0:T, 640:768].re("p (k d) -> p k d", d=64))
            cx.tt("pool", RK[0:T, :], KVT[0:T, 0:128], PEK[0:T, :], ALU.add)
            cx.tt("pool", RV[0:T, :], KVT[0:T, 128:256], PEV[0:T, :], ALU.add)
            p = ptmp(); cx.mm(p[:, 0:2], RK[:, :], POOLK2[:, :]); cx.cp("dve", KC[:, 2 * i:2 * i + 2], p[:, 0:2])
            p = ptmp(); cx.mm(p[:, 0:128], BIGV[:, 126 - 2 * i:254 - 2 * i], RV[:, :])
            cx.tt("dve", VCf[:], VCf[:], p[:, 0:128], ALU.add)
            cx.cp("pool", VC[:, 0, :, 0:64], VCf[:].re("p (k d) -> p k d", d=64))
            cx.memset("pool", OACC[0:T], 0.0)
            nbv = 2 * i + 2; W = max(nbv, 16); jc = J0 - 2 * i
            yield
            nsa_compressed(T, c0, nbv, lambda h, t0, nk: LC[0:T, h, jc + t0:jc + t0 + nk])
            yield
            nsa_select(T, nbv, W, keepadd[0:T, jc:jc + W], keepadd[0:T, WLC + jc:WLC + jc + W])
            yield
            for h in range(4):
                a, b = hprt(h); kv = h // 2
                qv = (QA if h % 2 == 0 else QB)[a:b, c0:c0 + T]
                far = [dict(kT=KSEL[a:b, kt * 128:(kt + 1) * 128], v=VSEL[:, kt, kv, :], nk=128,
                            mask=SEL[0:T, kv, 2 * kt:2 * kt + 2].unsq(2).bc([T, 2, 64]), mre=64) for kt in range(0, i - 1)]
                for g0 in range(0, len(far), 4):
                    ng = len(far[g0:g0 + 4])
                    attend(T, h, 1, qv, far[g0:g0 + 4], kT_all=KSEL[a:b, g0 * 128:(g0 + ng) * 128])
                    yield
                kts = [kt for kt in (i - 1, i) if kt >= 0]
                near = [dict(kT=KSEL[a:b, kt * 128:(kt + 1) * 128], v=VSEL[:, kt, kv, :], nk=128,
                             mask=SEL[0:T, kv, 2 * kt:2 * kt + 2].unsq(2).bc([T, 2, 64]), mre=64) for kt in kts]
                bt = BT[0:T, h, 256 - 128 * len(kts):256]
                attend(T, h, 1, qv, near, near_bt=bt)
                yield
                farw = [dict(kT=KWIN[a:b, kt % 5, :], v=VWIN[:, kt % 5, kv, :], nk=128,
                             mask=(wm4[0:T, :] if kt == i - 4 else None)) for kt in (i - 4, i - 3, i - 2) if kt >= 0]
                if farw:
                    attend(T, h, 2, qv, farw)
                    yield
                nearw = [dict(kT=KWIN[a:b, kt % 5, :], v=VWIN[:, kt % 5, kv, :], nk=128) for kt in kts]
                attend(T, h, 2, qv, nearw, near_bt=bt)
                yield
            nsa_combine(T, c0)

        def sample_prep(l, s):
            cx.dma(PTI0[:], ptab[s].pbc(128))
            cx.cp("dve", PTF[:], PTI0[:])
            cx.ts("dve", PTF[:], PTF[:], float(DEPTH * 128), ALU.mult, float(l * 128), ALU.add)
            cx.ts("dve", PTF[:], PTF[:], iota[:, 0:1], ALU.add)
            cx.ts("dve", PTF[:], PTF[:], 2.0, ALU.mult)
            cx.cp("dve", PTI0[:], PTF[:])
            cx.ts("dve", PTF[:], PTF[:], 1.0, ALU.add)
            cx.cp("dve", PTI1[:], PTF[:])
            PF = 6

            def issue1(pg):
                cx.dma(PGS[pg % 8][:], pool_d[:, :], q="pool", indirect=PTI0[:, pg:pg + 1])
            for pg in range(min(PF, NPG)):
                issue1(pg)
            pv = None
            for pg in range(NPG):
                if pg + PF < NPG:
                    issue1(pg + PF)
                buf = PGS[pg % 8]; rk = RKs[pg % 2]; rv = RVs[pg % 2]
                cx.tt("dve", rk[:, :], buf[:, 0:128], PEK[:, :], ALU.add)
                cx.tt("dve", rv[:, :], buf[:, 128:256], PEV[:, :], ALU.add)
                p = ptmp(); cx.mm(p[:, 0:2], rk[:, :], POOLK2[:, :]); cx.cp("act", KC[:, 2 * pg:2 * pg + 2], p[:, 0:2])
                r = pg % 64
                if r == 0:
                    pv = pacc()
                cx.mm(pv[:, 0:128], BIGV[:, 126 - 2 * r:254 - 2 * r], rv[:, :], start=(r == 0), stop=(r == 63 or pg == NPG - 1))
                if r == 63 or pg == NPG - 1:
                    cx.cp("act", VC[:, pg // 64, :, 0:64], pv[:, 0:128].re("p (k d) -> p k d", d=64))
            for kt in range(4):
                cx.dma(WINT[:], win_in[l, s, kt * 128:(kt + 1) * 128, :])
                transpose_to(KWIN[:, kt, :], WINT[:, 0:128], 128, 128)
                cx.cp("pool", VWIN[:, kt, :, 0:64], WINT[:, 128:256].re("p (k d) -> p k d", d=64))
            cx.dma(wins[l, s, 0:512 - TS, :], win_in[l, s, TS:512, :])

        def nsa_sample_tile(l, s, c0):
            T = TS
            cx.cp("pool", VNEW[0:T, :, 0:64], KVT[0:T, 384:512].re("p (k d) -> p k d", d=64))
            cx.cp("pool", VWIN[0:T, 4, :, 0:64], KVT[0:T, 640:768].re("p (k d) -> p k d", d=64))
            cx.memset("pool", OACC[0:T], 0.0)
            nbv = NBS; W = NBS + 1
            nsa_compressed(T, c0, nbv, lambda h, t0, nk: (LCS2[0:T, h, 128 - nk:128] if t0 + nk == nbv else None))
            nsa_select(T, nbv, W, keeps[0:T, 0:W], keeps[0:T, W:2 * W])
            qv = lambda h: (QA if h % 2 == 0 else QB)[hprt(h)[0]:hprt(h)[1], c0:c0 + T]
            def gath(g0):
                for k in range(min(4, NPG - g0)):
                    cx.dma(PGS[(g0 + k) % 8][:], pool_d[:, :], q="pool", indirect=PTI1[:, g0 + k:g0 + k + 1])

            def prep(g0):
                bsel = (g0 // 4) % 2
                npg_ = min(4, NPG - g0)
                for k in range(npg_):
                    buf = PGS[(g0 + k) % 8]
                    transpose_to(KPGs[bsel][:, k, :], buf[:, 0:128], 128, 128)
                    cx.cp("pool", VPGs[bsel][:, k, :, 0:64], buf[:, 128:256].re("p (k d) -> p k d", d=64))
            gath(0)
            prep(0)
            yield
            for g0 in range(0, NPG, 4):
                npg = min(4, NPG - g0)
                KPG = KPGs[(g0 // 4) % 2]; VPG = VPGs[(g0 // 4) % 2]
                if g0 + 4 < NPG:
                    gath(g0 + 4)
                for h in range(4):
                    a, b = hprt(h); kv = h // 2
                    tiles = [dict(kT=KPG[a:b, k, :], v=VPG[:, k, kv, :], nk=128,
                                  mask=SEL[0:T, kv, 2 * (g0 + k):2 * (g0 + k) + 2].unsq(2).bc([T, 2, 64]), mre=64) for k in range(npg)]
                    if g0 + npg == NPG:
                        last = tiles[-1]; tiles = tiles[:-1]
                        newt = dict(kT=KNEW[a:b, 0:T], v=VNEW[0:T, kv, :], nk=T, mask=SEL[0:T, kv, NBS:NBS + 1].bc([T, T]))
                        if tiles:
                            attend(T, h, 1, qv(h), tiles)
                        attend(T, h, 1, qv(h), [last, newt], near_bt=BT[0:T, h, 0:128 + T])
                    else:
                        attend(T, h, 1, qv(h), tiles)
                    yield
                if g0 + 4 < NPG:
                    prep(g0 + 4)
            for h in range(4):
                a, b = hprt(h); kv = h // 2
                farw = [dict(kT=KWIN[a:b, kt, :], v=VWIN[:, kt, kv, :], nk=128, mask=(wm4[0:T, :] if kt == 0 else None)) for kt in range(3)]
                attend(T, h, 2, qv(h), farw)
                nearw = [dict(kT=KWIN[a:b, 3, :], v=VWIN[:, 3, kv, :], nk=128), dict(kT=KWIN[a:b, 4, 0:T], v=VWIN[0:T, 4, kv, :], nk=T)]
                attend(T, h, 2, qv(h), nearw, near_bt=BT[0:T, h, 0:128 + T])
                yield
            nsa_combine(T, c0)

        def conformer_mt(N):
            for c in range(2):
                y = CY[:, c, 0:N]
                cx.ts("dve", y, UB[:, c, 0:N], convw[:, c, 0:1], ALU.mult)
                for k in range(1, 31):
                    cx.stt(y, UB[:, c, k:k + N], convw[:, c, k:k + 1], y, ALU.mult, ALU.add)
                    if k % 6 == 0:
                        yield
                cx.ts("dve", y, y, cvp[:, c:c + 1], ALU.add)
                cx.tt("pool", CY2[:, c, 0:N], y, y, ALU.mult)
            pm = ptmp()
            for c in range(2):
                cx.mm(pm[:, 0:N], ones_f[:, :], CY[:, c, 0:N], start=(c == 0), stop=(c == 1))
            pq = ptmp()
            for c in range(2):
                cx.mm(pq[:, 0:N], ones_f[:, :], CY2[:, c, 0:N], start=(c == 0), stop=(c == 1))
            cx.ts("dve", g1[:, 0:N], pm[:, 0:N], 1.0 / 256, ALU.mult)
            cx.tt("dve", g2[:, 0:N], g1[:, 0:N], g1[:, 0:N], ALU.mult)
            cx.stt(g2[:, 0:N], pq[:, 0:N], 1.0 / 256, g2[:, 0:N], ALU.mult, ALU.subtract)
            cx.ts("dve", g2[:, 0:N], g2[:, 0:N], 0.0, ALU.max, EPS, ALU.add)
            cx.rsqrt(g2[:, 0:N], g2[:, 0:N])
            for c in range(2):
                cx.tt("pool", g3[:, 0:N], CY[:, c, 0:N], g1[:, 0:N], ALU.subtract)
                cx.tt("pool", g3[:, 0:N], g3[:, 0:N], g2[:, 0:N], ALU.mult)
                cx.ts("dve", g3[:, 0:N], g3[:, 0:N], cvp[:, 2 + c:3 + c], ALU.mult, cvp[:, 4 + c:5 + c], ALU.add)
                cx.act(BRT[:, 1, c, 0:N], g3[:, 0:N], AF.Silu)

        def gdn_conv_mt(N):
            for c in range(6):
                y = GY[:, c, 0:N]
                cx.ts("dve", y, GB[:, c, 0:N], gdw[:, c, 0:1], ALU.mult)
                for k in range(1, 4):
                    cx.stt(y, GB[:, c, k:k + N], gdw[:, c, k:k + 1], y, ALU.mult, ALU.add)
                cx.act(y, y, AF.Silu)
                yield

        def bc4(v, T):
            return v.unsq(2).bc([T, 4, 64])

        def h3(t, T):
            return t[0:T, :].re("p (h e) -> p h e", e=64)

        def retention_tile(grp, pos, c0):
            T = grp["T"]; g = grp["gi"]
            src = c_rot_p[pos:pos + T, :] if grp["kind"] == "p" else c_rot_s[0:T, :]
            cx.dma(ROT[0:T, :], src)
            cosb = ROT[0:T, 0:32].unsq(1).bc([T, 4, 32]); sinb = ROT[0:T, 32:64].unsq(1).bc([T, 4, 32])
            ta = RT1[0:T, :].re("p (h f) -> p h f", f=32); tb = RT2[0:T, :].re("p (h f) -> p h f", f=32)
            for (off, dst) in ((0, QR), (256, KR)):
                s4 = RETT[0:T, off:off + 256].re("p (h w f) -> p h w f", h=4, w=2)
                d4 = dst[0:T, :].re("p (h w f) -> p h w f", h=4, w=2)
                x1 = s4[:, :, 0, :]; x2 = s4[:, :, 1, :]
                cx.tt("pool", d4[:, :, 0, :], x1, cosb, ALU.mult); cx.tt("pool", ta, x2, sinb, ALU.mult)
                cx.tt("pool", d4[:, :, 0, :], d4[:, :, 0, :], ta, ALU.subtract)
                cx.tt("dve", d4[:, :, 1, :], x1, sinb, ALU.mult); cx.tt("dve", tb, x2, cosb, ALU.mult)
                cx.tt("dve", d4[:, :, 1, :], d4[:, :, 1, :], tb, ALU.add)
            if T == 128:
                chk(7.41)
            po = pacc()
            for h in range(4):
                qh = QR[0:T, h * 64:(h + 1) * 64]; kh = KR[0:T, h * 64:(h + 1) * 64]; vh = RETT[0:T, 512 + h * 64:512 + (h + 1) * 64]
                p = ptmp(); cx.tr(p[0:64, 0:T], qh, ident[0:T, 0:T])
                cx.cp("act", QT[0:64, 0:T], p[0:64, 0:T]); cx.tt("dve", QXT[0:64, 0:T], QT[0:64, 0:T], retxig[g][0:64, h, 0:T], ALU.mult)
                p = ptmp(); cx.tr(p[0:64, 0:T], kh, ident[0:T, 0:T]); cx.act(KT[0:64, 0:T], p[0:64, 0:T], AF.Identity, scale=0.125)
                cx.ts("pool", KZ[0:T, :], kh, retzg[g][0:T, h:h + 1], ALU.mult)
                if T == 128:
                    chk(7.42)
                p = ptmp(); cx.mm(p[0:T, 0:T], KT[0:64, 0:T], QT[0:64, 0:T])
                cx.tt("dve", ATT[0:T, 0:T], p[0:T, 0:T], retDg[g][0:T, h, 0:T], ALU.mult)
                if T == 128:
                    chk(7.43)
                cx.mm(po[0:T, h * 64:(h + 1) * 64], ATT[0:T, 0:T], vh, start=True, stop=False)
                cx.mm(po[0:T, h * 64:(h + 1) * 64], QXT[0:64, 0:T], SR[0:64, h, :], start=False, stop=True)
                if T == 128:
                    chk(7.44)
                pS = ptmp(); cx.mm(pS[0:64, 0:64], KZ[0:T, 0:64], vh)
                cx.stt(SR[0:64, h, :], SR[0:64, h, :], float(GAM[h] ** T), pS[0:64, 0:64], ALU.mult, ALU.add)
                yield
            cx.cp("act", OG[0:T, :], po[0:T, 0:256])
            cx.red(ST8[0:T, 0:4], h3(OG, T))
            cx.tt("pool", OTMP[0:T, :], OG[0:T, :], OG[0:T, :], ALU.mult)
            cx.red(ST8[0:T, 4:8], h3(OTMP, T))
            cx.ts("dve", ST8[0:T, 0:8], ST8[0:T, 0:8], 1.0 / 64, ALU.mult)
            cx.tt("dve", G4[0:T, :], ST8[0:T, 0:4], ST8[0:T, 0:4], ALU.mult)
            cx.tt("dve", G4[0:T, :], ST8[0:T, 4:8], G4[0:T, :], ALU.subtract)
            cx.ts("dve", G4[0:T, :], G4[0:T, :], 0.0, ALU.max, EPS, ALU.add)
            cx.rsqrt(G4[0:T, :], G4[0:T, :])
            cx.tt("dve", h3(OG, T), h3(OG, T), bc4(ST8[0:T, 0:4], T), ALU.subtract)
            cx.tt("dve", h3(OG, T), h3(OG, T), bc4(G4[0:T, :], T), ALU.mult)
            cx.tt("pool", OG[0:T, :], OG[0:T, :], gnb[0:T, :], ALU.mult)
            cx.act(SIL[0:T, :], RETT[0:T, 768:1024], AF.Silu)
            cx.tt("pool", OG[0:T, :], OG[0:T, :], SIL[0:T, :], ALU.mult)
            for c in range(2):
                transpose_to(BRT[:, 2, c, c0:c0 + T], OG[0:T, c * 128:(c + 1) * 128], T, 128)

        def gdn_tile(grp, c0):
            T = grp["T"]; g = grp["gi"]
            for part, dst in enumerate((GQ, GK, GV)):
                for h in range(4):
                    blk = part * 2 + h // 2; a = (h % 2) * 64
                    p = ptmp(); cx.tr(p[0:T, 0:64], GY[a:a + 64, blk, c0:c0 + T], ident[a:a + 64, a:a + 64])
                    evac(dst[0:T, h * 64:(h + 1) * 64], p[0:T, 0:64])
            for X, sc in ((GQ, 0.125), (GK, 1.0)):
                cx.tt("pool", OTMP[0:T, :], X[0:T, :], X[0:T, :], ALU.mult)
                cx.red(G4[0:T, :], h3(OTMP, T))
                cx.ts("dve", G4[0:T, :], G4[0:T, :], 0.0, ALU.max, EPS, ALU.add)
                cx.rsqrt(G4[0:T, :], G4[0:T, :])
                if sc != 1.0:
                    cx.ts("dve", G4[0:T, :], G4[0:T, :], sc, ALU.mult)
                cx.tt("dve", h3(X, T), h3(X, T), bc4(G4[0:T, :], T), ALU.mult)
            cx.tt("dve", G4[0:T, :], ZAB[0:T, 256:260], dtb[0:T, :], ALU.add)
            cx.ts("dve", G4b[0:T, :], G4[0:T, :], -1.0, ALU.mult)
            cx.tt("dve", G4b[0:T, :], G4b[0:T, :], G4[0:T, :], ALU.max)
            cx.act(G4b[0:T, :], G4b[0:T, :], AF.Exp, scale=-1.0)
            cx.act(G4b[0:T, :], G4b[0:T, :], AF.Ln, bias=ones_f[0:T, 0:1])
            cx.ts("dve", G4[0:T, :], G4[0:T, :], 0.0, ALU.max)
            cx.tt("dve", G4[0:T, :], G4[0:T, :], G4b[0:T, :], ALU.add)
            cx.tt("dve", GG[0:T, :], G4[0:T, :], negA[0:T, :], ALU.mult)
            cx.act(GBETA[0:T, :], ZAB[0:T, 260:264], AF.Sigmoid)
            p = ptmp(); cx.mm(p[0:T, 0:4], Umat[0:T, 0:T], GG[0:T, 0:4]); cx.cp("dve", GC[0:T, :], p[0:T, 0:4])
            p = ptmp(); cx.mm(p[0:128, 0:4], lastmg[g][0:T, :], GC[0:T, 0:4]); cx.cp("dve", GLB[:, :], p[0:128, 0:4])
            cx.act(EGC[0:T, :], GC[0:T, :], AF.Exp)
            cx.tt("dve", EKD[0:T, :], GLB[0:T, :], GC[0:T, :], ALU.subtract); cx.act(EKD[0:T, :], EKD[0:T, :], AF.Exp)
            cx.act(EGL[:, :], GLB[:, :], AF.Exp)
            po = pacc()
            nlev = int(round(math.log2(T)))
            for h in range(4):
                hs = slice(h * 64, (h + 1) * 64)
                cx.ts("dve", KBt[0:T, :], GK[0:T, hs], GBETA[0:T, h:h + 1], ALU.mult)
                cx.ts("pool", RU[0:T, :], GV[0:T, hs], GBETA[0:T, h:h + 1], ALU.mult)
                cx.ts("dve", KBE[0:T, :], KBt[0:T, :], EGC[0:T, h:h + 1], ALU.mult)
                cx.ts("pool", QD[0:T, :], GQ[0:T, hs], EGC[0:T, h:h + 1], ALU.mult)
                cx.ts("pool", KD[0:T, :], GK[0:T, hs], EKD[0:T, h:h + 1], ALU.mult)
                for srcv, dstv in ((GK[0:T, hs], gKT), (KBt[0:T, :], gKBT), (GQ[0:T, hs], gQT), (QD[0:T, :], gQDT)):
                    transpose_to(dstv[0:64, 0:T], srcv, T, 64)
                cx.ts("pool", GREP[0:T, 0:T], ones_f[0:T, 0:T], GG[0:T, h:h + 1], ALU.mult)
                pR = ptmp(); cx.mm(pR[0:T, 0:T], GREP[0:T, 0:T], Umat[0:T, 0:T])
                cx.ts("dve", LMU[0:T, 0:T], pR[0:T, 0:T], GC[0:T, h:h + 1], ALU.subtract, 0.0, ALU.min)
                cx.act(LMU[0:T, 0:T], LMU[0:T, 0:T], AF.Exp)
                cx.ts("dve", LML[0:T, 0:T], pR[0:T, 0:T], GC[0:T, h:h + 1], ALU.subtract, 0.0, ALU.max)
                cx.act(LML[0:T, 0:T], LML[0:T, 0:T], AF.Exp, scale=-1.0)
                cx.tt("pool", LMT[0:T, 0:T], LMU[0:T, 0:T], triu[0:T, 0:T], ALU.mult)
                cx.tt("pool", LMU[0:T, 0:T], LMU[0:T, 0:T], triu[0:T, 128:128 + T], ALU.mult)
                cx.tt("pool", LML[0:T, 0:T], LML[0:T, 0:T], tril[0:T, 0:T], ALU.mult)
                p = ptmp(); cx.mm(p[0:T, 0:T], gKT[0:64, 0:T], gQT[0:64, 0:T]); cx.tt("dve", GATT[0:T, 0:T], p[0:T, 0:T], LMT[0:T, 0:T], ALU.mult)
                P_, Q_ = PQ[0]
                p = ptmp(); cx.mm(p[0:T, 0:T], gKT[0:64, 0:T], gKBT[0:64, 0:T]); cx.tt("dve", P_[0:T, 0:T], p[0:T, 0:T], LMU[0:T, 0:T], ALU.mult)
                p = ptmp(); cx.mm(p[0:T, 0:T], gKBT[0:64, 0:T], gKT[0:64, 0:T]); cx.tt("dve", Q_[0:T, 0:T], p[0:T, 0:T], LML[0:T, 0:T], ALU.mult)
                cx.tt("pool", YM[0:T, 0:T], ident[0:T, 0:T], P_[0:T, 0:T], ALU.subtract)
                yield
                cur = 0
                for k in range(1, nlev):
                    P2, Q2 = PQ[1 - cur]
                    if k < nlev - 1:
                        pP = ptmp(); cx.mm(pP[0:T, 0:T], Q_[0:T, 0:T], P_[0:T, 0:T]); cx.cp("dve", P2[0:T, 0:T], pP[0:T, 0:T])
                    pQ = ptmp(); cx.mm(pQ[0:T, 0:T], P_[0:T, 0:T], Q_[0:T, 0:T]); cx.cp("act", Q2[0:T, 0:T], pQ[0:T, 0:T])
                    pY = ptmp(); cx.mm(pY[0:T, 0:T], Q2[0:T, 0:T], YM[0:T, 0:T]); cx.tt("dve", YM[0:T, 0:T], YM[0:T, 0:T], pY[0:T, 0:T], ALU.add)
                    P_, Q_ = P2, Q2; cur = 1 - cur
                    yield
                pu = ptmp(); cx.mm(pu[0:T, 0:64], YM[0:T, 0:T], RU[0:T, 0:64]); cx.cp("act", USB[0:T, :], pu[0:T, 0:64])
                pw = ptmp(); cx.mm(pw[0:64, 0:T], KBE[0:T, 0:64], YM[0:T, 0:T]); cx.cp("dve", WT[0:64, 0:T], pw[0:64, 0:T])
                pv = ptmp(); cx.mm(pv[0:T, 0:64], WT[0:64, 0:T], SGs[0:64, h, :]); cx.tt("dve", VN[0:T, :], USB[0:T, :], pv[0:T, 0:64], ALU.subtract)
                cx.mm(po[0:T, hs], gQDT[0:64, 0:T], SGs[0:64, h, :], start=True, stop=False)
                cx.mm(po[0:T, hs], GATT[0:T, 0:T], VN[0:T, 0:64], start=False, stop=True)
                pS = ptmp(); cx.mm(pS[0:64, 0:64], KD[0:T, 0:64], VN[0:T, 0:64])
                cx.stt(SGs[0:64, h, :], SGs[0:64, h, :], EGL[0:64, h:h + 1], pS[0:64, 0:64], ALU.mult, ALU.add)
                yield
            cx.cp("act", OG[0:T, :], po[0:T, 0:256])
            cx.tt("pool", OTMP[0:T, :], OG[0:T, :], OG[0:T, :], ALU.mult)
            cx.red(G4[0:T, :], h3(OTMP, T))
            cx.ts("dve", G4[0:T, :], G4[0:T, :], 1.0 / 64, ALU.mult, EPS, ALU.add)
            cx.rsqrt(G4[0:T, :], G4[0:T, :])
            cx.tt("dve", h3(OG, T), h3(OG, T), bc4(G4[0:T, :], T), ALU.mult)
            cx.tt("pool", h3(OG, T), h3(OG, T), gnorm[0:T, :].unsq(1).bc([T, 4, 64]), ALU.mult)
            cx.act(SIL[0:T, :], ZAB[0:T, 0:256], AF.Silu)
            cx.tt("pool", OG[0:T, :], OG[0:T, :], SIL[0:T, :], ALU.mult)
            for c in range(2):
                transpose_to(BRT[:, 3, c, c0:c0 + T], OG[0:T, c * 128:(c + 1) * 128], T, 128)

        wv = w_in[l].re("(k p) n -> p k n", p=128)
        for j, hh in enumerate((0, 2, 1, 3)):
            cx.dma(W1[:, :, j * 64:(j + 1) * 64], wv[:, :, hh * 64:(hh + 1) * 64], q="pool")
        for c0_ in range(256, O_MG, 2048):
            c1_ = min(O_MG, c0_ + 2048)
            cx.dma(W1[:, :, c0_:c1_], wv[:, :, c0_:c1_], q="pool")
        load_layer_params(l)
        chk(3)
        for gi, grp in enumerate(groups):
            T = grp["T"]; s = grp["s"]; isp = grp["kind"] == "p"
            if isp:
                chk(6)
                cx.memset("pool", UB[:, :, 0:30], 0.0); cx.memset("pool", GB[:, :, 0:3], 0.0)
                cx.memset("pool", SR[:], 0.0); cx.memset("pool", SGs[:], 0.0); cx.memset("pool", VCf[:], 0.0)
            else:
                for c in range(2):
                    load_T(UB[:, c, 0:30], st_conv[l, s, :, c * 128:(c + 1) * 128], 30)
                for c in range(6):
                    load_T(GB[:, c, 0:3], st_gconv[l, s, :, c * 128:(c + 1) * 128], 3)
                cx.dma(SR[:], st_ret[l, s].re("h d e -> d h e")); cx.dma(SGs[:], st_gdn[l, s].re("h d e -> d h e"))
                sample_prep(l, s)
                chk(4)
            mlist = mts(grp, MT1 if isp else T)
            for (t0, N) in mlist:
                a0 = grp["tok0"] + t0
                load_x(l, grp, t0, N)
                if l == 0:
                    store_x(grp, t0, N, final=False)
                mod_norm(gi, N, 0, 1)
                for blk, dst in ((0, QA), (1, QB)):
                    cx.act(dst[:, 0:N], proj_fm(blk * 128, 128, N), AF.Identity, scale=0.125)
                pk_ = proj_fm(O_KV + 256, 128, N)
                if isp:
                    evac(KSEL[:, t0:t0 + N], pk_)
                else:
                    evac(KNEW[:, 0:N], pk_)
                pk_ = proj_fm(O_KV + 512, 128, N)
                slot = (t0 // 128) % 5 if isp else 4
                evac(KWIN[:, slot, 0:T], pk_)
                for c in range(2):
                    pg_ = proj_fm(O_CV + 256 + c * 128, 128, N)
                    cx.act(g1[:, 0:N], pg_, AF.Sigmoid)
                    pa_ = proj_fm(O_CV + c * 128, 128, N)
                    cx.tt("dve", UB[:, c, 30:30 + N], pa_, g1[:, 0:N], ALU.mult)
                for c in range(6):
                    evac(GB[:, c, 3:3 + N], proj_fm(O_GDN + c * 128, 128, N))
                if isp:
                    chk(7.1)
                pos = t0; j = 0
                evac(KVT[0:T, 0:512], proj_tm(O_KV, 512, j, T))
                evac(KVT[0:T, 512:780], proj_tm(O_KV + 512, 268, j, T))
                evac(RETT[0:T, 0:512], proj_tm(O_RET, 512, j, T))
                evac(RETT[0:T, 512:1024], proj_tm(O_RET + 512, 512, j, T))
                evac(ZAB[0:T, :], proj_tm(O_GDN + 768, 264, j, T))
                def gdn_chain():
                    yield from gdn_conv_mt(N)
                    yield from gdn_tile(grp, j)
                if isp:
                    cx.dma(kvp[l, pos:pos + T, :], KVT[0:T, 0:512])
                    if pos >= SEQ - 512:
                        cx.dma(winp[l, pos - (SEQ - 512):pos - (SEQ - 512) + T, :], KVT[0:T, 512:768])
                    run_gens([nsa_prompt_tile(l, pos // 128, j), gdn_chain(), retention_tile(grp, pos, j), conformer_mt(N)])
                else:
                    cx.dma(kvs[s, l, :, :], KVT[0:T, 0:512])
                    cx.dma(wins[l, s, 512 - TS:512, :], KVT[0:T, 512:768])
                    run_gens([nsa_sample_tile(l, s, j), gdn_chain(), retention_tile(grp, pos, j), conformer_mt(N)])
                cx.dma(brd[:, :, a0:a0 + N].re("c p n -> p c n"), BRT[:].re("p n c t -> p (n c) t")[:, :, 0:N])
                last = (t0 + N >= T * grp["ntile"])
                if last:
                    for c in range(2):
                        transpose_to(stg[0:30, :], UB[:, c, N:N + 30], 128, 30)
                        dst = convp[l] if isp else convs[l, s]
                        cx.dma(dst[:, c * 128:(c + 1) * 128], stg[0:30, :])
                    for c in range(6):
                        transpose_to(stg[0:3, :], GB[:, c, N:N + 3], 128, 3)
                        dst = gcp[l] if isp else gcs[l, s]
                        cx.dma(dst[:, c * 128:(c + 1) * 128], stg[0:3, :])
                    cx.dma((retp[l] if isp else rets[l, s]).re("h d e -> d h e"), SR[:])
                    cx.dma((gdnp[l] if isp else gdns[l, s]).re("h d e -> d h e"), SGs[:])
                else:
                    cx.cp("pool", UB[:, :, 0:30], UB[:, :, N:N + 30])
                    cx.cp("pool", GB[:, :, 0:3], GB[:, :, N:N + 3])
                chk(5 if not isp else 7)

    try:
        chk(0)
        build_tables()
        chk(1)
        cx.ar_reset()
        load_cT()
        for l in range(DEPTH):
            compute_mod(l)
            chk(2)
            phase1a(l)
            chk(8)
            phase1b(l)
            chk(9)
            phase2(l)
            chk(10)
    except StopBuild:
        pass
    cx.finish()
    return nc


def make_consts(cfg):
    SEQ, TS, PAST, NBP = cfg.SEQ, cfg.TS, cfg.PAST, cfg.NBP
    NBS = PAST // 64
    J0 = NBP - 2
    WLC = NBP + 16
    f = np.float32
    r = np.arange(128)
    c = {}
    c["c_ident"] = np.eye(128, dtype=f)
    c["c_triu"] = np.concatenate([(r[None, :] >= r[:, None]).astype(f), (r[None, :] > r[:, None]).astype(f)], axis=1)
    c["c_tril"] = (r[:, None] > r[None, :]).astype(f)
    c["c_U"] = (r[:, None] <= r[None, :]).astype(f)
    last = np.zeros((2, 128, 128), f); last[0, 127, :] = 1.0; last[1, TS - 1, :] = 1.0
    c["c_last"] = last
    half = 32
    inv = (np.float32(10000.0) ** (-np.arange(half, dtype=f) / f(half))).astype(f)

    def rot(pos):
        ang = pos.astype(f)[:, None] * inv[None, :]
        return np.concatenate([np.cos(ang), np.sin(ang)], axis=1).astype(f)
    c["c_rot_p"] = rot(np.arange(SEQ))
    c["c_rot_s"] = rot(PAST + np.arange(TS))
    lg = np.log1p(-np.exp2(-5.0 - np.arange(4, dtype=np.float64)))
    diff = (r[None, :] - r[:, None]).astype(np.float64)
    Dm = np.where(diff >= 0, np.exp(np.maximum(diff, 0.0)[None] * lg[:, None, None]), 0.0)
    c["c_retD"] = np.stack([Dm, Dm]).astype(f)
    xi = np.exp((r[None, :] + 1.0) * lg[:, None])
    c["c_retxi"] = np.broadcast_to(xi[None, :, None, :], (2, 4, 64, 128)).astype(f).copy()
    z = np.zeros((2, 128, 4), np.float64)
    for g, C in enumerate((128, TS)):
        t = np.arange(C)
        z[g, :C, :] = np.exp((C - 1.0 - t)[:, None] * lg[None, :]) / 8.0
    c["c_retz"] = z.astype(f)
    cc = np.arange(256)
    c["c_dbt"] = (r[:, None] + 128 - cc[None, :]).astype(f)
    j = np.arange(WLC)
    c["c_dlc"] = (r[:, None] - 63 + 64 * (J0 - j[None, :])).astype(f)
    mp = (j - J0)[None, :]
    ct = (r // 64)[:, None]
    forced = (mp == ct) | (mp == ct - 1)
    after = mp > ct
    keep = (~forced & ~after).astype(f)
    add = 2.0 * forced.astype(f) - after.astype(f)
    c["c_keep"] = np.concatenate([keep, add], axis=1).astype(f)
    W = NBS + 1
    ks = np.ones((8, W), f); ad = np.zeros((8, W), f)
    ks[:, NBS - 1:] = 0.0; ad[:, NBS - 1:] = 2.0
    c["c_keeps"] = np.concatenate([ks, ad], axis=1)
    c["c_wm4"] = (r[None, :] > r[:, None]).astype(f)
    c["c_iota"] = r.astype(f).reshape(128, 1)
    return c


_NC_CACHE = {}


def run_cfg(cfg, inputs, n_cores, n_prompt):
    key = (cfg.SEQ, cfg.NS, cfg.TS, cfg.PAST, cfg.NPOOL)
    if key not in _NC_CACHE:
        _NC_CACHE[key] = build(cfg)
    nc = _NC_CACHE[key]
    consts = make_consts(cfg)
    NS, TS, SEQ = cfg.NS, cfg.TS, cfg.SEQ
    A = lambda k: np.ascontiguousarray(np.asarray(inputs[k]))
    pool = A("cache_nsa_kv").reshape(-1, 256)
    wnames = ["w_ada", "b_ada", "norms", "w_in", "cmp_pool", "cmp_pe", "rel_bias", "conv_dw", "conv_dw_b", "conv_ln_g",
              "conv_ln_b", "ret_gn", "gdn_conv_w", "gdn_A_log", "gdn_dt_bias", "gdn_norm", "w_branch", "w_out", "ffn_up",
              "ffn_dw", "ffn_down"]
    shared = {k: A(k) for k in wnames}
    shared.update(consts)
    shared["pool"] = pool
    x_prompt, x_sample = A("x_prompt"), A("x_sample")
    c_prompt, c_sample = A("c_prompt"), A("c_sample")
    win = A("cache_nsa_win"); page_table = A("page_table")
    sc, sr, sgc, sg, sf = A("state_conv"), A("state_ret"), A("state_gdn_conv"), A("state_gdn"), A("state_ffn_conv")
    DB = x_sample.shape[0]
    in_maps = []
    for c in range(n_cores):
        bp = c % n_prompt
        ss = [(c * NS + i) % DB for i in range(NS)]
        m = dict(shared)
        m["xp"] = x_prompt[bp]; m["cp"] = c_prompt[bp:bp + 1]
        m["xs"] = x_sample[ss].reshape(NS * TS, D); m["cs"] = c_sample[ss]
        m["ptab"] = page_table[ss].astype(np.int32)
        m["win_in"] = np.ascontiguousarray(win[:, ss]).reshape(DEPTH, NS, 512, 256)
        m["st_conv"] = np.ascontiguousarray(sc[:, ss]); m["st_ret"] = np.ascontiguousarray(sr[:, ss])
        m["st_gconv"] = np.ascontiguousarray(sgc[:, ss]); m["st_gdn"] = np.ascontiguousarray(sg[:, ss])
        m["st_ffn"] = np.ascontiguousarray(sf[:, ss])
        in_maps.append(m)
    res = run_bass_kernel_spmd(nc, in_maps, core_ids=list(range(n_cores))).results
    B = n_prompt
    pc = [res[b] for b in range(B)]
    f = np.float32
    y_p = np.stack([pc[b]["yp"] for b in range(B)]).astype(f)
    kv_p = np.stack([pc[b]["kvp"] for b in range(B)]).reshape(B, DEPTH, SEQ, 4, 2, 64)
    win_p = np.stack([pc[b]["winp"] for b in range(B)], axis=1).reshape(DEPTH, B, 512, 2, 2, 64)
    conv_p = np.stack([pc[b]["convp"] for b in range(B)], axis=1)
    ret_p = np.stack([pc[b]["retp"] for b in range(B)], axis=1)
    gc_p = np.stack([pc[b]["gcp"] for b in range(B)], axis=1)
    gdn_p = np.stack([pc[b]["gdnp"] for b in range(B)], axis=1)
    ffn_p = np.stack([pc[b]["ffnp"] for b in range(B)], axis=1)
    ncs = DB // NS
    sc_ = [res[c] for c in range(ncs)]
    y_s = np.concatenate([r["ys"].reshape(NS, TS, D) for r in sc_], axis=0)
    kv_s = np.concatenate([r["kvs"] for r in sc_], axis=0).reshape(DB, DEPTH, TS, 4, 2, 64)
    cat1 = lambda k: np.concatenate([r[k] for r in sc_], axis=1)
    win_s = cat1("wins").reshape(DEPTH, DB, 512, 2, 2, 64)
    outs = (y_p, y_s, kv_p, kv_s, win_p, win_s, conv_p, cat1("convs"), ret_p, cat1("rets"), gc_p, cat1("gcs"),
            gdn_p, cat1("gdns"), ffn_p, cat1("ffns"))
    return tuple(np.ascontiguousarray(o, dtype=np.float32) for o in outs)


def kernel(**inputs):
    cfg = Cfg(NS=4)
    return run_cfg(cfg, inputs, n_cores=8, n_prompt=4)
```
